# Optimizing a Trainium2 kernel written in Bass

```python
import math
import jax, jax.numpy as jnp
from jax import lax
import numpy as np

D_MODEL = 1024
BATCH = 4
SEQ = 4096
DEPTH = 4

N_MEM = 256
HEAD_DIM = 64
MIX_WIDTH = D_MODEL
MEM_HEADS = 4
MEM_WIDTH = MEM_HEADS * HEAD_DIM
MAIN_WIDTH = MIX_WIDTH - MEM_WIDTH
SB_HEADS = MAIN_WIDTH // HEAD_DIM
CONV_WIDTH = 3
D_FF = -(-8 * D_MODEL // (3 * 256)) * 256
N_A_LAYERS = DEPTH // 2
N_B_LAYERS = DEPTH - N_A_LAYERS
BLOCK_Q = 128
EPS = 1e-6

kernel_name = "shortconv_stickbreaking_yoco_hybrid"


def rmsnorm(x, g):
    xf = x.astype(jnp.float32)
    y = xf * lax.rsqrt(jnp.mean(xf * xf, axis=-1, keepdims=True) + EPS)
    return (y * g.astype(jnp.float32)).astype(x.dtype)


def causal_short_conv(u, w):
    c = u.shape[-1]
    return lax.conv_general_dilated(
        u, w[:, None, :].astype(u.dtype), window_strides=(1,),
        padding=[(CONV_WIDTH - 1, 0)],
        dimension_numbers=("NWC", "WIO", "NWC"),
        feature_group_count=c)


def memory_cross_attention(q, mem_k, mem_v):
    scale = 1.0 / math.sqrt(HEAD_DIM)
    s = jnp.einsum("bshd,bmhd->bhsm", q.astype(jnp.float32), mem_k.astype(jnp.float32)) * scale
    p = jax.nn.softmax(s, axis=-1)
    o = jnp.einsum("bhsm,bmhd->bshd", p, mem_v.astype(jnp.float32))
    return o.astype(q.dtype)


def stick_breaking_attention(q, k, v):
    b, s_len, h, d = q.shape
    scale = 1.0 / math.sqrt(d)
    qh = jnp.transpose(q, (0, 2, 1, 3)).astype(jnp.float32)
    kh = jnp.transpose(k, (0, 2, 1, 3)).astype(jnp.float32)
    vh = jnp.transpose(v, (0, 2, 1, 3)).astype(jnp.float32)
    outs = []
    for blk in range(s_len // BLOCK_Q):
        start = blk * BLOCK_Q
        end = start + BLOCK_Q
        qb = qh[:, :, start:end]
        kb = kh[:, :, :end]
        vb = vh[:, :, :end]
        z = jnp.einsum("bhtd,bhsd->bhts", qb, kb) * scale
        t_idx = start + jnp.arange(BLOCK_Q)[:, None]
        s_idx = jnp.arange(end)[None, :]
        causal = s_idx < t_idx
        log_not = jnp.where(causal, jax.nn.log_sigmoid(-z), 0.0)
        tail = lax.cumsum(log_not, axis=3, reverse=True) - log_not
        log_a = jax.nn.log_sigmoid(z) + tail
        a = jnp.where(causal, jnp.exp(log_a), 0.0)
        outs.append(jnp.einsum("bhts,bhsd->bhtd", a, vb))
    o = jnp.concatenate(outs, axis=2)
    return jnp.transpose(o, (0, 2, 1, 3)).astype(q.dtype)


def swiglu(h, w_gate, w_up, w_down):
    return (jax.nn.silu(h @ w_gate) * (h @ w_up)) @ w_down


def setup_inputs(seed: int = 0) -> dict:
    key = jax.random.key(seed)
    ks = jax.random.split(key, 16)
    f32 = jnp.float32

    def nrm(k, shape, fan_in):
        return jax.random.normal(k, shape, f32) * (fan_in ** -0.5)

    def gain(k, shape):
        return jnp.ones(shape, f32) + 0.02 * jax.random.normal(k, shape, f32)

    x = jax.random.normal(ks[0], (BATCH, SEQ, D_MODEL), f32)
    mem = jax.random.normal(ks[1], (BATCH, N_MEM, D_MODEL), f32)
    return {
        "x": x,
        "mem": mem,
        "mix_norm": gain(ks[2], (DEPTH, D_MODEL)),
        "a_in": nrm(ks[3], (N_A_LAYERS, D_MODEL, 3 * MAIN_WIDTH + MEM_WIDTH), D_MODEL),
        "conv_w": nrm(ks[4], (N_A_LAYERS, CONV_WIDTH, MAIN_WIDTH), CONV_WIDTH),
        "b_in": nrm(ks[5], (N_B_LAYERS, D_MODEL, MAIN_WIDTH + MEM_WIDTH), D_MODEL),
        "kv_norm": gain(ks[6], (D_MODEL,)),
        "w_kv_shared": nrm(ks[7], (D_MODEL, 2 * MAIN_WIDTH), D_MODEL),
        "w_mem_kv": nrm(ks[8], (DEPTH, D_MODEL, 2 * MEM_WIDTH), D_MODEL),
        "w_o": nrm(ks[9], (DEPTH, MIX_WIDTH, D_MODEL), MIX_WIDTH),
        "ffn_norm": gain(ks[10], (DEPTH, D_MODEL)),
        "w_gate": nrm(ks[11], (DEPTH, D_MODEL, D_FF), D_MODEL),
        "w_up": nrm(ks[12], (DEPTH, D_MODEL, D_FF), D_MODEL),
        "w_down": nrm(ks[13], (DEPTH, D_FF, D_MODEL), D_FF),
        "mem_norm": gain(ks[14], (D_MODEL,)),
        "final_norm": gain(ks[15], (D_MODEL,)),
    }


def reference(x, mem, mix_norm, a_in, conv_w, b_in, kv_norm, w_kv_shared, w_mem_kv,
              w_o, ffn_norm, w_gate, w_up, w_down, mem_norm, final_norm):
    b, s_len, _ = x.shape
    m_len = mem.shape[1]
    mem_n = rmsnorm(mem, mem_norm)
    k_sh = None
    v_sh = None
    for i in range(DEPTH):
        h = rmsnorm(x, mix_norm[i])
        mkv = (mem_n @ w_mem_kv[i]).reshape(b, m_len, 2, MEM_HEADS, HEAD_DIM)
        mem_k, mem_v = mkv[:, :, 0], mkv[:, :, 1]
        if i < N_A_LAYERS:
            p = h @ a_in[i]
            b_gate = p[..., :MAIN_WIDTH]
            c_gate = p[..., MAIN_WIDTH:2 * MAIN_WIDTH]
            u = p[..., 2 * MAIN_WIDTH:3 * MAIN_WIDTH]
            q_mem = p[..., 3 * MAIN_WIDTH:]
            y_main = b_gate * causal_short_conv(c_gate * u, conv_w[i])
        else:
            j = i - N_A_LAYERS
            p = h @ b_in[j]
            q_sb = p[..., :MAIN_WIDTH].reshape(b, s_len, SB_HEADS, HEAD_DIM)
            q_mem = p[..., MAIN_WIDTH:]
            y_main = stick_breaking_attention(q_sb, k_sh, v_sh).reshape(b, s_len, MAIN_WIDTH)
        y_mem = memory_cross_attention(
            q_mem.reshape(b, s_len, MEM_HEADS, HEAD_DIM), mem_k, mem_v
        ).reshape(b, s_len, MEM_WIDTH)
        x = x + jnp.concatenate([y_main, y_mem], axis=-1) @ w_o[i]
        x = x + swiglu(rmsnorm(x, ffn_norm[i]), w_gate[i], w_up[i], w_down[i])
        if i == N_A_LAYERS - 1:
            kv = (rmsnorm(x, kv_norm) @ w_kv_shared).reshape(b, s_len, 2, SB_HEADS, HEAD_DIM)
            k_sh, v_sh = kv[:, :, 0], kv[:, :, 1]
    return rmsnorm(x, final_norm)
```

```python
import numpy as np
import concourse.bass as bass
import concourse.mybir as mybir
from concourse.bass_utils import run_bass_kernel_spmd

F32 = mybir.dt.float32
BF16 = mybir.dt.bfloat16
AF = mybir.ActivationFunctionType
ALU = mybir.AluOpType

D = 1024
KC = 8
T = 2048
HALO = 32
TT = T + HALO
NG = 4
GN = 512
SEQ = 4096
NMEM = 256
DFF = 2816
FC = 22
MAINW = 768
EPS = 1e-6
N_CORES = 8
SEM_EPOCH = 24000


class Buf:
    __slots__ = ("w", "r", "name")

    def __init__(self, name=""):
        self.w = None
        self.r = {}
        self.name = name


class Op:
    __slots__ = ("eng", "fn", "deps", "sig", "sem", "val", "dma", "idx")

    def __init__(self, eng, fn, dma):
        self.eng = eng
        self.fn = fn
        self.deps = set()
        self.sig = False
        self.sem = None
        self.val = 0
        self.dma = dma
        self.idx = 0


class Sched:
    ENGS = ("pe", "act", "dve", "pool", "sp")

    def __init__(self, nc):
        self.nc = nc
        self.ops = {e: [] for e in self.ENGS}
        self.n = 0

    def emit(self, eng, fn, reads=(), writes=(), dma=None, force_sig=False):
        op = Op(eng, fn, dma)
        op.idx = self.n
        self.n += 1
        deps = op.deps
        for b in reads:
            if b.w is not None:
                deps.add(b.w)
        for b in writes:
            if b.w is not None:
                deps.add(b.w)
            for r in b.r.values():
                if isinstance(r, list):
                    deps.update(r)
                else:
                    deps.add(r)
        deps.discard(op)
        if eng == "pe" and dma is None:
            for d in [d for d in deps if d.eng == "pe" and d.dma is None]:
                deps.discard(d)
        for d in deps:
            d.sig = True
        for b in reads:
            if dma is not None:
                b.r.setdefault(("dma", eng), [])
                b.r[("dma", eng)].append(op)
            else:
                b.r[eng] = op
        for b in writes:
            b.w = op
            b.r = {}
        if dma is not None or force_sig:
            op.sig = True
        self.ops[eng].append(op)
        return op

    def wait_all(self, eng, ops):
        op = Op(eng, None, None)
        op.deps = set(ops)
        for d in ops:
            d.sig = True
        self.ops[eng].append(op)
        return op

    def finalize(self, semalloc):
        for e in self.ENGS:
            cnt = 0
            sem = None
            for op in self.ops[e]:
                if op.fn is None or not op.sig:
                    continue
                if op.dma is not None:
                    op.dma.count += 1
                    op.sem = op.dma.sem
                    op.val = 16 * op.dma.count
                else:
                    if sem is None or cnt >= SEM_EPOCH:
                        sem = semalloc("e_" + e)
                        cnt = 0
                    cnt += 1
                    op.sem = sem
                    op.val = cnt

    def replay(self, eng, handle):
        waited = {}
        for op in self.ops[eng]:
            need = {}
            for d in op.deps:
                k = id(d.sem)
                if k not in need or need[k][1] < d.val:
                    need[k] = (d.sem, d.val)
            for k, (sem, val) in need.items():
                if waited.get(k, 0) < val:
                    handle.wait_ge(sem, val)
                    waited[k] = val
            if op.fn is None:
                continue
            name, kw = op.fn
            pos = kw.pop("_pos", ())
            ins = getattr(handle, name)(*pos, **kw)
            if op.sig:
                ins.then_inc(op.sem, 16 if op.dma is not None else 1)


class DmaChan:
    def __init__(self, sem):
        self.sem = sem
        self.count = 0


def gcols(gi):
    if gi < 0:
        return 0, HALO
    return HALO + GN * gi, GN


class Builder:
    def __init__(self, n_layers=4, debug_out=None, kv_only=False, only_inputs=None):
        self.n_layers = n_layers
        self.debug_out = debug_out
        self.kv_only = kv_only
        self.only_inputs = only_inputs

    def sem(self, name):
        self._semn += 1
        s = self.nc.alloc_semaphore(f"{name}_{self._semn}")
        return s

    def chan(self, name):
        return DmaChan(self.sem("d_" + name))

    def build(self):
        nc = bass.Bass("TRN2", target_bir_lowering=False)
        self.nc = nc
        self._semn = 0
        S = Sched(nc)
        self.S = S
        dt0 = nc.dram_tensor

        class _Dummy:
            def ap(self):
                return None

        def dt(name, shape, dtype, kind="Internal"):
            if kind == "ExternalInput" and self.only_inputs is not None and name not in self.only_inputs:
                return _Dummy()
            return dt0(name, shape, dtype, kind=kind)
        self.xT_in = dt("xT", [D, TT], F32, kind="ExternalInput").ap()
        self.memT_in = dt("memT", [D, NMEM], F32, kind="ExternalInput").ap()
        self.small_in = dt("small", [128, 128], F32, kind="ExternalInput").ap()
        self.consts_in = dt("consts", [128, 512], F32, kind="ExternalInput").ap()
        self.a_in = dt("a_in", [2, D, 2560], F32, kind="ExternalInput").ap()
        self.b_in = dt("b_in", [2, D, 1024], F32, kind="ExternalInput").ap()
        self.w_kv = dt("w_kv_shared", [D, 1536], F32, kind="ExternalInput").ap()
        self.w_mem_kv = dt("w_mem_kv", [4, D, 512], F32, kind="ExternalInput").ap()
        self.w_o = dt("w_o", [4, D, D], F32, kind="ExternalInput").ap()
        self.w_gate = dt("w_gate", [4, D, DFF], F32, kind="ExternalInput").ap()
        self.w_up = dt("w_up", [4, D, DFF], F32, kind="ExternalInput").ap()
        self.w_down = dt("w_down", [4, DFF, D], F32, kind="ExternalInput").ap()
        self.outT = dt("outT", [D, T], F32, kind="ExternalOutput").ap()
        self.k_src = [dt(f"k_src{i}", [256, T], BF16) for i in range(3)]
        self.v_src = [dt(f"v_src{i}", [T, 256], BF16) for i in range(3)]
        self.k_g = [dt(f"k_g{i}", [512, T], BF16) for i in range(3)]
        self.v_g = [dt(f"v_g{i}", [2 * T, 256], BF16) for i in range(3)]

        st = nc.alloc_sbuf_tensor
        self.XT = st("XT", [128, KC, TT], F32)
        self.HT = st("HT", [128, KC, TT], BF16)
        self.YT = st("YT", [128, KC, TT], BF16)
        self.ARENA = st("ARENA", [128, 11776], F32)
        self.WR = [st(f"WR{i}", [128, 3072], BF16) for i in range(2)]
        self.small = st("small_sb", [128, 128], F32)
        self.cst = st("cst", [128, 512], BF16)
        self.memnT = st("memnT", [128, KC, NMEM], BF16)
        self.mKT = st("mKT", [128, 2, NMEM], BF16)
        self.mV = st("mV", [128, 2, 256], BF16)
        self.sq = [st(f"sq{i}", [128, GN], BF16) for i in range(2)]
        self.rstd = [st(f"rstd{i}", [128, GN], F32) for i in range(2)]
        self.PS = [nc.alloc_psum_tensor(f"ps{i}", [128, GN], F32) for i in range(8)]

        self.bPS = [Buf(f"ps{i}") for i in range(8)]
        self.ps_rr = 0
        self.bWR = [[Buf(f"wr{i}_{j}") for j in range(3)] for i in range(2)]
        self.wr_rr = 0
        self.wr_chan = [[self.chan(f"wr{i}_{j}") for j in range(3)] for i in range(2)]
        self.sg = [st(f"sg{i}", [128, GN], F32) for i in range(2)]
        self.bsg = [Buf() for _ in range(2)]
        self.sg_rr = 0
        self.zeros_bf = st("zeros_bf", [128, GN], BF16)
        self.b_zeros = Buf()
        self.bG = {(c, s): Buf() for c in range(FC) for s in range(3)}
        self.ch_v = [self.chan("v0"), self.chan("v1")]
        self.ch_kt = [self.chan("kt0"), self.chan("kt1")]
        self.arena_dmas = []
        self.ps_rrs = {}
        self.bX = {(k, g): Buf() for k in range(KC) for g in range(-1, NG)}
        self.bH = {(k, g): Buf() for k in range(KC) for g in range(-1, NG)}
        self.bY = {(k, g): Buf() for k in range(KC) for g in range(-1, NG)}
        self.bsq = [Buf() for _ in range(2)]
        self.brstd = [Buf() for _ in range(2)]
        self.sq_rr = 0
        self.rstd_rr = 0
        self.b_small = Buf()
        self.b_cst = Buf()
        self.b_memn = Buf()
        self.b_mK = Buf()
        self.b_mV = Buf()
        self.ch_misc = self.chan("misc")
        self.ch_shift = [self.chan(f"sh{i}") for i in range(4)]
        self.shift_rr = 0
        self.ch_out = self.chan("out")
        self.out_ops = []
        return nc

    def ps_next(self, banks=None):
        if banks is None:
            banks = list(range(8))
        key = tuple(banks)
        r = self.ps_rrs.get(key, 0)
        self.ps_rrs[key] = r + 1
        i = banks[r % len(banks)]
        return self.PS[i], self.bPS[i]

    def barrier(self):
        S = self.S
        last = []
        for e in ("pe", "act", "dve"):
            for op in reversed(S.ops[e]):
                if op.fn is not None:
                    last.append(op)
                    break
        dm = list(self.arena_dmas)
        self.arena_dmas = []
        for e in ("pe", "act", "dve", "sp"):
            S.wait_all(e, [o for o in last if o.eng != e] + dm)

    def carve(self, off, shape, dtype):
        n = 1
        for s in shape[1:]:
            n *= s
        words = n if dtype == F32 else (n + 1) // 2
        ap = self.ARENA[:, off:off + words]
        if dtype == BF16:
            ap = ap.bitcast(BF16)
        if len(shape) == 3:
            ap = ap.rearrange("p (a b) -> p a b", a=shape[1])
        return ap, off + words

    def add(self, load, compute):
        self.units.append((load, compute))

    def run_units(self, prefetch=1):
        pend = []
        n = len(self.units)
        for i in range(n + prefetch):
            if i < n:
                ld = self.units[i][0]
                pend.append(ld() if ld is not None else None)
            if i >= prefetch:
                self.units[i - prefetch][1](pend[i - prefetch])

    def wslot(self):
        i = self.wr_rr % 2
        self.wr_rr += 1
        return i

    def wload(self, i, seg0, nseg, shape, src):
        n = shape[1] * shape[2]
        dst = self.WR[i][:, seg0 * 1024: seg0 * 1024 + n].rearrange("p (a b) -> p a b", a=shape[1])
        bufs = [self.bWR[i][s] for s in range(seg0, seg0 + nseg)]
        self.S.emit("pool", ("dma_start", dict(out=dst, in_=src)),
                    writes=bufs, dma=self.wr_chan[i][seg0])
        return dst

    def norm_group(self, xs, bxs, gcol, outs, bouts, n):
        S = self.S
        ps, bps = self.ps_next()
        ones = self.cst[:, 256:384]
        for k in range(KC):
            i = self.sq_rr % 2
            self.sq_rr += 1
            sq, bsq = self.sq[i], self.bsq[i]
            S.emit("dve", ("tensor_tensor", dict(out=sq[:, :n], in0=xs[k], in1=xs[k], op=ALU.mult)),
                   reads=[bxs[k]], writes=[bsq])
            S.emit("pe", ("matmul", dict(out=ps[:, :n], lhsT=ones, rhs=sq[:, :n],
                                                         start=(k == 0), stop=(k == KC - 1))),
                   reads=[bsq, self.b_cst], writes=[bps])
        i = self.rstd_rr % 2
        self.rstd_rr += 1
        rstd, brstd = self.rstd[i], self.brstd[i]
        S.emit("act", ("activation", dict(out=rstd[:, :n], in_=ps[:, :n], func=AF.Ln, bias=EPS, scale=1.0 / D)),
               reads=[bps], writes=[brstd])
        S.emit("act", ("activation", dict(out=rstd[:, :n], in_=rstd[:, :n], func=AF.Exp, scale=-0.5)),
               reads=[brstd], writes=[brstd])
        for k in range(KC):
            g = self.small[:, gcol + k: gcol + k + 1]
            S.emit("dve", ("scalar_tensor_tensor", dict(out=outs[k], in0=xs[k], scalar=g,
                                                                     in1=rstd[:, :n], op0=ALU.mult, op1=ALU.mult)),
                   reads=[bxs[k], brstd, self.b_small], writes=[bouts[k]])

    def norm_tokens(self, groups, gcol):
        for gi in groups:
            c0, n = gcols(gi)
            self.norm_group([self.XT[:, k, c0:c0 + n] for k in range(KC)], [self.bX[(k, gi)] for k in range(KC)],
                            gcol, [self.HT[:, k, c0:c0 + n] for k in range(KC)],
                            [self.bH[(k, gi)] for k in range(KC)], n)

    def proj(self, wview, bw, gi, banks=None):
        c0, n = gcols(gi)
        ps, bps = self.ps_next(banks)
        for k in range(KC):
            self.S.emit("pe", ("matmul", dict(out=ps[:, :n], lhsT=wview[:, k, :], rhs=self.HT[:, k, c0:c0 + n],
                                                       start=(k == 0), stop=(k == KC - 1))),
                        reads=[bw, self.bH[(k, gi)]], writes=[bps])
        return ps, bps

    def mem_kv(self, l):
        S = self.S

        def ld(part):
            def f():
                i = self.wslot()
                src = self.w_mem_kv[l, :, part * 256:(part + 1) * 256].rearrange("(k p) f -> p k f", p=128)
                return i, self.wload(i, 0, 2, [128, KC, 256], src)
            return f

        def ck(slot):
            i, w = slot
            for c in range(2):
                ps, bps = self.ps_next()
                for k in range(KC):
                    S.emit("pe", ("matmul", dict(out=ps[:, :NMEM], lhsT=w[:, k, c * 128:(c + 1) * 128],
                                                                     rhs=self.memnT[:, k, :], start=(k == 0),
                                                                     stop=(k == KC - 1))),
                           reads=[self.bWR[i][0], self.bWR[i][1], self.b_memn], writes=[bps])
                S.emit("act", ("copy", dict(out=self.mKT[:, c, :], in_=ps[:, :NMEM])),
                       reads=[bps], writes=[self.b_mK])

        def cv(slot):
            i, w = slot
            for mc in range(2):
                ps, bps = self.ps_next()
                for k in range(KC):
                    S.emit("pe", ("matmul", dict(out=ps[:, :256],
                                                                       lhsT=self.memnT[:, k, mc * 128:(mc + 1) * 128],
                                                                       rhs=w[:, k, :], start=(k == 0),
                                                                       stop=(k == KC - 1))),
                           reads=[self.bWR[i][0], self.bWR[i][1], self.b_memn], writes=[bps])
                S.emit("act", ("copy", dict(out=self.mV[:, mc, :], in_=ps[:, :256])),
                       reads=[bps], writes=[self.b_mV])

        self.add(ld(0), ck)
        self.add(ld(1), cv)

    def mem_attn(self, groups, qmT, b_qm, PT, bPT, rden, brden, otmp, botmp):
        S = self.S
        ones64 = self.cst[:, 256:320]
        rr = 0
        for gi in groups:
            c0, n = gcols(gi)
            for hm in range(4):
                c, off = hm // 2, (hm % 2) * 64
                psO, bO = self.ps_next()
                psD, bD = self.ps_next()
                for mc in range(2):
                    psS, bS = self.ps_next()
                    S.emit("pe", ("matmul", dict(out=psS[:, :n], lhsT=self.mKT[off:off + 64, c, mc * 128:(mc + 1) * 128],
                        rhs=qmT[off:off + 64, c, c0:c0 + n], start=True, stop=True)),
                        reads=[self.b_mK, b_qm[(c, gi)]], writes=[bS])
                    j = rr % 2
                    rr += 1
                    S.emit("act", ("activation", dict(out=PT[j][:, :n], in_=psS[:, :n], func=AF.Exp,
                                                                       scale=0.125)),
                           reads=[bS], writes=[bPT[j]])
                    S.emit("pe", ("matmul", dict(out=psO[0:64, :n], lhsT=self.mV[:, mc, hm * 64:(hm + 1) * 64], rhs=PT[j][:, :n],
                        start=(mc == 0), stop=(mc == 1))), reads=[self.b_mV, bPT[j]], writes=[bO])
                    S.emit("pe", ("matmul", dict(out=psD[0:64, :n], lhsT=ones64, rhs=PT[j][:, :n], start=(mc == 0), stop=(mc == 1))),
                        reads=[self.b_cst, bPT[j]], writes=[bD])
                j = rr % 2
                S.emit("dve", ("reciprocal", dict(out=rden[j][0:64, :n], in_=psD[0:64, :n])),
                       reads=[bD], writes=[brden[j]])
                if off == 0:
                    S.emit("dve", ("tensor_tensor", dict(
                        out=self.YT[0:64, 6 + c, c0:c0 + n], in0=psO[0:64, :n], in1=rden[j][0:64, :n], op=ALU.mult)),
                        reads=[bO, brden[j]], writes=[self.bY[(6 + c, gi)]])
                else:
                    S.emit("dve", ("tensor_tensor", dict(
                        out=otmp[j][0:64, :n], in0=psO[0:64, :n], in1=rden[j][0:64, :n], op=ALU.mult)),
                        reads=[bO, brden[j]], writes=[botmp[j]])
                    self.shift(otmp[j][0:64, :n], botmp[j], self.YT[64:128, 6 + c, c0:c0 + n], self.bY[(6 + c, gi)])

    def shift(self, src, bsrc, dst, bdst):
        ch = self.ch_shift[self.shift_rr % len(self.ch_shift)]
        self.shift_rr += 1
        op = self.S.emit("sp", ("dma_start", dict(out=dst, in_=src)), reads=[bsrc], writes=[bdst], dma=ch)
        self.arena_dmas.append(op)

    def w_o_units(self, l, groups):
        S = self.S
        for m in range(KC):
            def ld(m=m):
                i = self.wslot()
                src = self.w_o[l, :, m * 128:(m + 1) * 128].rearrange("(k p) f -> p k f", p=128)
                return i, self.wload(i, 0, 1, [128, KC, 128], src)

            def cp(slot, m=m):
                i, w = slot
                for gi in groups:
                    c0, n = gcols(gi)
                    ps, bps = self.ps_next()
                    for c in range(KC):
                        S.emit("pe", ("matmul", dict(out=ps[:, :n], lhsT=w[:, c, :],
                                                                    rhs=self.YT[:, c, c0:c0 + n], start=(c == 0),
                                                                    stop=(c == KC - 1))),
                               reads=[self.bWR[i][0], self.bY[(c, gi)]], writes=[bps])
                    S.emit("dve", ("tensor_tensor", dict(out=self.XT[:, m, c0:c0 + n],
                                                                   in0=ps[:, :n], in1=self.XT[:, m, c0:c0 + n],
                                                                   op=ALU.add)),
                           reads=[bps, self.bX[(m, gi)]], writes=[self.bX[(m, gi)]])
            self.add(ld, cp)

    def ffn(self, l, with_halo):
        S = self.S
        GTW = 2 * GN + HALO
        GT, _ = self.carve(0, [128, FC, GTW], BF16)
        for sb in range(2):
            groups = [2 * sb, 2 * sb + 1]
            if with_halo and sb == 0:
                groups = [-1] + groups

            def lcol(gi, sb=sb):
                return 2 * GN if gi < 0 else (gi - 2 * sb) * GN

            def slot_of(gi, sb=sb):
                return 2 if gi < 0 else gi - 2 * sb

            self.add(None, lambda _, groups=groups: self.norm_tokens(groups, 32 + 8 * l))
            for c in range(FC):
                def ld(c=c):
                    i = self.wslot()
                    sg = self.w_gate[l, :, c * 128:(c + 1) * 128].rearrange("(k p) f -> p k f", p=128)
                    su = self.w_up[l, :, c * 128:(c + 1) * 128].rearrange("(k p) f -> p k f", p=128)
                    return i, self.wload(i, 0, 1, [128, KC, 128], sg), self.wload(i, 1, 1, [128, KC, 128], su)

                def cp(slot, c=c, groups=groups, lcol=lcol, slot_of=slot_of):
                    i, wg, wu = slot
                    for gi in groups:
                        c0, n = gcols(gi)
                        psg, bg = self.proj(wg, self.bWR[i][0], gi)
                        psu, bu = self.proj(wu, self.bWR[i][1], gi)
                        j = self.sg_rr % 2
                        self.sg_rr += 1
                        sgt, bsg = self.sg[j], self.bsg[j]
                        S.emit("act", ("activation", dict(out=sgt[:, :n], in_=psg[:, :n],
                                                                               func=AF.Silu)),
                               reads=[bg], writes=[bsg])
                        lc = lcol(gi)
                        S.emit("dve", ("tensor_tensor", dict(
                            out=GT[:, c, lc:lc + n], in0=psu[:, :n], in1=sgt[:, :n], op=ALU.mult)),
                            reads=[bu, bsg], writes=[self.bG[(c, slot_of(gi))]])
                self.add(ld, cp)
            for m in range(KC):
                def ld(m=m):
                    i = self.wslot()
                    src = self.w_down[l, :, m * 128:(m + 1) * 128].rearrange("(c p) f -> p c f", p=128)
                    return i, self.wload(i, 0, 3, [128, FC, 128], src)

                def cp(slot, m=m, groups=groups, lcol=lcol, slot_of=slot_of):
                    i, w = slot
                    for gi in groups:
                        c0, n = gcols(gi)
                        lc = lcol(gi)
                        ps, bps = self.ps_next()
                        for c in range(FC):
                            S.emit("pe", ("matmul", dict(out=ps[:, :n], lhsT=w[:, c, :],
                                                                               rhs=GT[:, c, lc:lc + n],
                                                                               start=(c == 0), stop=(c == FC - 1))),
                                   reads=[self.bWR[i][0], self.bG[(c, slot_of(gi))]], writes=[bps])
                        S.emit("dve", ("tensor_tensor", dict(out=self.XT[:, m, c0:c0 + n],
                                                                       in0=ps[:, :n], in1=self.XT[:, m, c0:c0 + n],
                                                                       op=ALU.add)),
                               reads=[bps, self.bX[(m, gi)]], writes=[self.bX[(m, gi)]])
                self.add(ld, cp)

    def mixer_a(self, l):
        S = self.S
        halo_full = (l == 0)
        allg = [-1, 0, 1, 2, 3]
        outg = allg if halo_full else [0, 1, 2, 3]
        off = 0
        vbuf, off = self.carve(off, [128, TT + 4], F32)
        qmT, off = self.carve(off, [128, 2, TT], BF16)
        cgs, c1, c2 = [], [], []
        for lst in (cgs, c1, c2):
            for _ in range(2):
                a, off = self.carve(off, [128, GN], F32)
                lst.append(a)
        PT, rden, otmp = [], [], []
        for _ in range(2):
            a, off = self.carve(off, [128, GN], BF16)
            PT.append(a)
        for _ in range(2):
            a, off = self.carve(off, [128, GN], F32)
            rden.append(a)
        for _ in range(2):
            a, off = self.carve(off, [128, GN], BF16)
            otmp.append(a)
        bv = {g: Buf() for g in allg}
        bcgs = [Buf(), Buf()]
        bc1 = [Buf(), Buf()]
        bc2 = [Buf(), Buf()]
        bPT = [Buf(), Buf()]
        brden = [Buf(), Buf()]
        botmp = [Buf(), Buf()]
        b_qm = {(c, g): Buf() for c in range(2) for g in allg}
        flag = self.small[:, 124:125]
        rr = [0]

        def pre(_):
            self.norm_tokens(allg, 8 * l)
            S.emit("dve", ("memset", dict(ap=vbuf[:, 0:2], constant=0.0)), writes=[bv[-1]])
        self.add(None, pre)
        self.mem_kv(l)

        for j in range(6):
            def ld(j=j):
                i = self.wslot()
                views = []
                for s in range(3):
                    src = self.a_in[l, :, s * MAINW + j * 128: s * MAINW + (j + 1) * 128].rearrange(
                        "(k p) f -> p k f", p=128)
                    views.append(self.wload(i, s, 1, [128, KC, 128], src))
                return i, views

            def cp(slot, j=j):
                i, (wb, wc, wu) = slot
                for gi in allg:
                    c0, n = gcols(gi)
                    need_y = gi in outg
                    psc, bpc = self.proj(wc, self.bWR[i][1], gi)
                    psu, bpu = self.proj(wu, self.bWR[i][2], gi)
                    if need_y:
                        psb, bpb = self.proj(wb, self.bWR[i][0], gi)
                    q = rr[0] % 2
                    rr[0] += 1
                    S.emit("act", ("copy", dict(out=cgs[q][:, :n], in_=psc[:, :n])),
                           reads=[bpc], writes=[bcgs[q]])
                    S.emit("dve", ("tensor_tensor", dict(out=vbuf[:, 2 + c0:2 + c0 + n],
                                                                          in0=psu[:, :n], in1=cgs[q][:, :n],
                                                                          op=ALU.mult)),
                           reads=[bpu, bcgs[q]], writes=[bv[gi]])
                    if gi < 0:
                        S.emit("dve", ("tensor_scalar", dict(out=vbuf[:, 2:2 + HALO], in0=vbuf[:, 2:2 + HALO],
                                                                scalar1=flag, scalar2=None, op0=ALU.mult)),
                               reads=[self.b_small], writes=[bv[gi]])
                    if not need_y:
                        continue
                    wcol = 88 + (l * 6 + j) * 3
                    w0 = self.small[:, wcol:wcol + 1]
                    w1 = self.small[:, wcol + 1:wcol + 2]
                    w2 = self.small[:, wcol + 2:wcol + 3]
                    prev = [bv[gi - 1]] if gi >= 0 else []
                    S.emit("act", ("mul", dict(out=c1[q][:, :n], in_=vbuf[:, 2 + c0:2 + c0 + n], mul=w2)),
                           reads=[bv[gi], self.b_small], writes=[bc1[q]])
                    S.emit("dve", ("scalar_tensor_tensor", dict(
                        out=c2[q][:, :n], in0=vbuf[:, 1 + c0:1 + c0 + n], scalar=w1, in1=c1[q][:, :n],
                        op0=ALU.mult, op1=ALU.add)), reads=[bv[gi], bc1[q], self.b_small] + prev, writes=[bc2[q]])
                    S.emit("dve", ("scalar_tensor_tensor", dict(
                        out=c1[q][:, :n], in0=vbuf[:, c0:c0 + n], scalar=w0, in1=c2[q][:, :n],
                        op0=ALU.mult, op1=ALU.add)), reads=[bv[gi], bc2[q], self.b_small] + prev, writes=[bc1[q]])
                    S.emit("dve", ("tensor_tensor", dict(out=self.YT[:, j, c0:c0 + n],
                                                                          in0=psb[:, :n], in1=c1[q][:, :n],
                                                                          op=ALU.mult)),
                           reads=[bpb, bc1[q]], writes=[self.bY[(j, gi)]])
            self.add(ld, cp)

        def ldq():
            i = self.wslot()
            src = self.a_in[l, :, 3 * MAINW:3 * MAINW + 256].rearrange("(k p) f -> p k f", p=128)
            return i, self.wload(i, 0, 2, [128, KC, 256], src)

        def cpq(slot):
            i, w = slot
            for gi in outg:
                c0, n = gcols(gi)
                for c in range(2):
                    ps, bps = self.ps_next()
                    for k in range(KC):
                        S.emit("pe", ("matmul", dict(out=ps[:, :n], lhsT=w[:, k, c * 128:(c + 1) * 128],
                                                                         rhs=self.HT[:, k, c0:c0 + n], start=(k == 0),
                                                                         stop=(k == KC - 1))),
                               reads=[self.bWR[i][0], self.bWR[i][1], self.bH[(k, gi)]], writes=[bps])
                    S.emit("act", ("copy", dict(out=qmT[:, c, c0:c0 + n], in_=ps[:, :n])),
                           reads=[bps], writes=[b_qm[(c, gi)]])
            self.mem_attn(outg, qmT, b_qm, PT, bPT, rden, brden, otmp, botmp)
        self.add(ldq, cpq)
        self.w_o_units(l, outg)
        self.add(None, lambda _: self.barrier())

    def kv_proj(self):
        S = self.S
        off = 0
        kst, vst = [], []
        for _ in range(2):
            a, off = self.carve(off, [128, GN], BF16)
            kst.append(a)
        for _ in range(2):
            a, off = self.carve(off, [128, 256], BF16)
            vst.append(a)
        bk = [Buf(), Buf()]
        bvs = [Buf(), Buf()]
        ch_st = self.chan("st")
        b_gate = Buf()
        self.b_kvready = Buf()
        rr = [0]
        self.add(None, lambda _: self.norm_tokens([0, 1, 2, 3], 64))
        for c in range(6):
            def ld(c=c):
                i = self.wslot()
                src = self.w_kv[:, c * 128:(c + 1) * 128].rearrange("(k p) f -> p k f", p=128)
                return i, self.wload(i, 0, 1, [128, KC, 128], src)

            def cp(slot, c=c):
                i, w = slot
                for g in range(NG):
                    ps, bps = self.proj(w, self.bWR[i][0], g)
                    q = rr[0] % 2
                    rr[0] += 1
                    S.emit("act", ("copy", dict(out=kst[q][:, :], in_=ps[:, :])),
                           reads=[bps], writes=[bk[q]])
                    dst = self.k_src[c // 2][(c % 2) * 128:(c % 2 + 1) * 128, g * GN:(g + 1) * GN]
                    op = S.emit("sp", ("dma_start", dict(out=dst, in_=kst[q][:, :])),
                                reads=[bk[q], b_gate], dma=ch_st)
                    self.arena_dmas.append(op)
            self.add(ld, cp)
        for hv in range(3):
            def ld(hv=hv):
                i = self.wslot()
                src = self.w_kv[:, MAINW + hv * 256: MAINW + (hv + 1) * 256].rearrange("(k p) f -> p k f", p=128)
                return i, self.wload(i, 0, 2, [128, KC, 256], src)

            def cp(slot, hv=hv):
                i, w = slot
                for tt in range(16):
                    g = tt // 4
                    c0 = HALO + tt * 128
                    ps, bps = self.ps_next()
                    for k in range(KC):
                        S.emit("pe", ("matmul", dict(out=ps[:, :256], lhsT=self.HT[:, k, c0:c0 + 128],
                                                                           rhs=w[:, k, :], start=(k == 0),
                                                                           stop=(k == KC - 1))),
                               reads=[self.bWR[i][0], self.bWR[i][1], self.bH[(k, g)]], writes=[bps])
                    q = rr[0] % 2
                    rr[0] += 1
                    S.emit("act", ("copy", dict(out=vst[q][:, :], in_=ps[:, :256])),
                           reads=[bps], writes=[bvs[q]])
                    dst = self.v_src[hv][tt * 128:(tt + 1) * 128, :]
                    op = S.emit("sp", ("dma_start", dict(out=dst, in_=vst[q][:, :])),
                                reads=[bvs[q], b_gate], dma=ch_st)
                    self.arena_dmas.append(op)
            self.add(ld, cp)

        def coll(_):
            rg = [[0, 1], [2, 3], [4, 5], [6, 7]]
            srcs = self.k_src + self.v_src
            dsts = self.k_g + self.v_g
            for ci in range(6):
                S.emit("pool", ("collective_compute", dict(_pos=("AllGather", ALU.bypass), replica_groups=rg,
                                                           ins=[srcs[ci].ap().opt()], outs=[dsts[ci].ap().opt()])),
                       reads=([b_gate] if ci else []),
                       writes=([b_gate] if ci == 0 else []) + ([self.b_kvready] if ci == 5 else []), force_sig=True)
            self.barrier()
        self.add(None, coll)

    def mixer_b(self, l):
        S = self.S
        jb = l - 2
        G4 = [0, 1, 2, 3]
        off = 0
        Qc = []
        for _ in range(2):
            a, off = self.carve(off, [128, T], BF16)
            Qc.append(a)
        qmT, off = self.carve(off, [128, 2, TT], BF16)
        KT, off = self.carve(off, [128, SEQ], BF16)
        Vh, off = self.carve(off, [128, 32, 64], BF16)
        ebuf, off = self.carve(off, [128, GN], F32)
        spb, apb, scl, obf = [], [], [], []
        for _ in range(2):
            a, off = self.carve(off, [128, GN], BF16)
            spb.append(a)
        for _ in range(2):
            a, off = self.carve(off, [128, GN], BF16)
            apb.append(a)
        for _ in range(2):
            a, off = self.carve(off, [128, GN], F32)
            scl.append(a)
        sacc, off = self.carve(off, [128, GN], BF16)
        for _ in range(2):
            a, off = self.carve(off, [128, GN], BF16)
            obf.append(a)
        assert off <= 11776, off
        b_Q = [Buf(), Buf()]
        b_qm = {(c, g): Buf() for c in range(2) for g in G4}
        b_KT = [Buf(), Buf()]
        b_V = [Buf(), Buf()]
        b_e = Buf()
        b_sp = [Buf(), Buf()]
        b_ap = [Buf(), Buf()]
        b_scl = [Buf(), Buf()]
        b_sacc = Buf()
        b_obf = [Buf(), Buf()]
        flag = self.small[:, 124:125]
        mask = self.cst[:, 0:128]
        trin8 = self.cst[:, 128:256]
        ones64 = self.cst[:, 256:320]
        neg8 = self.cst[:, 384:512]
        zeros = self.zeros_bf
        BA, BB, BC = [0, 1], [2, 3], [4, 5]
        rr = {"q": 0, "s": 0, "o": 0}

        self.add(None, lambda _: self.norm_tokens(G4, 8 * l))
        self.mem_kv(l)

        def ldq():
            i = self.wslot()
            src = self.b_in[jb, :, MAINW:MAINW + 256].rearrange("(k p) f -> p k f", p=128)
            return i, self.wload(i, 0, 2, [128, KC, 256], src)

        def cpq(slot):
            i, w = slot
            for gi in G4:
                c0, n = gcols(gi)
                for c in range(2):
                    ps, bps = self.ps_next([7])
                    for k in range(KC):
                        S.emit("pe", ("matmul", dict(out=ps[:, :n], lhsT=w[:, k, c * 128:(c + 1) * 128],
                                                                         rhs=self.HT[:, k, c0:c0 + n], start=(k == 0),
                                                                         stop=(k == KC - 1))),
                               reads=[self.bWR[i][0], self.bWR[i][1], self.bH[(k, gi)]], writes=[bps])
                    S.emit("act", ("copy", dict(out=qmT[:, c, c0:c0 + n], in_=ps[:, :n])),
                           reads=[bps], writes=[b_qm[(c, gi)]])
            self.mem_attn(G4, qmT, b_qm, apb, b_ap, scl, b_scl, obf, b_obf)
        self.add(ldq, cpq)

        def attn_chunk(c, qi):
            steps = []
            for eh in range(2):
                for G in range(NG):
                    blocks = [(16 + ob, (ob - 4 * G) * 128 if ob >= 4 * G else 0, ob >= 4 * G)
                              for ob in range(4 * G + 3, -1, -1)]
                    blocks += [(pb, 0, False) for pb in range(15, -1, -1)]
                    for bi, (kb, c0, diag) in enumerate(blocks):
                        steps.append(dict(eh=eh, G=G, kb=kb, c0=c0, diag=diag, first=(bi == 0),
                                          last=(bi == len(blocks) - 1)))
            ns = len(steps)

            def views(st):
                po = 64 * st["eh"]
                c0 = st["c0"]
                n = GN - c0
                q0 = st["G"] * GN + c0
                ktb = KT[po:po + 64, st["kb"] * 128:(st["kb"] + 1) * 128]
                qv = Qc[qi][po:po + 64, q0:q0 + n]
                bkt = b_KT[0] if st["kb"] < 16 else b_KT[1]
                bvv = b_V[0] if st["kb"] < 16 else b_V[1]
                return c0, ktb, qv, bkt, bvv

            def stageA(k):
                st = steps[k]
                c0, ktb, qv, bkt, bvv = views(st)
                psA, bA = self.PS[BA[k % 2]], self.bPS[BA[k % 2]]
                S.emit("pe", ("matmul", dict(out=psA[:, c0:], lhsT=ktb, rhs=qv, start=True, stop=True)),
                       reads=[bkt, b_Q[qi]], writes=[bA])

            def stage1(k):
                st = steps[k]
                c0 = st["c0"]
                s = k % 2
                psA, bA = self.PS[BA[k % 2]], self.bPS[BA[k % 2]]
                S.emit("act", ("activation", dict(out=ebuf[:, c0:], in_=psA[:, c0:], func=AF.Exp, scale=0.125)),
                       reads=[bA], writes=[b_e])
                S.emit("act", ("activation", dict(out=spb[s][:, c0:], in_=ebuf[:, c0:], func=AF.Ln, bias=1.0,
                                                  scale=1.0)), reads=[b_e], writes=[b_sp[s]])
                if st["diag"]:
                    S.emit("dve", ("tensor_tensor", dict(out=spb[s][:, c0:c0 + 128], in0=spb[s][:, c0:c0 + 128],
                                                         in1=mask, op=ALU.mult)),
                           reads=[self.b_cst], writes=[b_sp[s]])

            def load_v(eh):
                h = 2 * c + eh
                srcp = self.v_g[h // 4][0:T, (h % 4) * 64:(h % 4 + 1) * 64].rearrange("(b p) d -> p b d", p=128)
                srco = self.v_src[h // 4][:, (h % 4) * 64:(h % 4 + 1) * 64].rearrange("(b p) d -> p b d", p=128)
                op = S.emit("sp", ("dma_start", dict(out=Vh[:, 0:16, :], in_=srcp)), reads=[self.b_kvready],
                            writes=[b_V[0]], dma=self.ch_v[0])
                self.arena_dmas.append(op)
                op = S.emit("sp", ("dma_start", dict(out=Vh[:, 16:32, :], in_=srco)), reads=[self.b_kvready],
                            writes=[b_V[1]], dma=self.ch_v[1])
                self.arena_dmas.append(op)
                S.emit("dve", ("tensor_scalar", dict(out=Vh[:, 0:16, :], in0=Vh[:, 0:16, :], scalar1=flag,
                                                     scalar2=None, op0=ALU.mult)),
                       reads=[self.b_small], writes=[b_V[0]])

            stageA(0)
            if ns > 1:
                stageA(1)
            stage1(0)
            seg = -1
            for k in range(ns):
                st = steps[k]
                c0, ktb, qv, bkt, bvv = views(st)
                s = k % 2
                psB, bB = self.PS[BB[k % 2]], self.bPS[BB[k % 2]]
                if st["first"]:
                    seg += 1
                psC, bC = self.PS[BC[seg % 2]], self.bPS[BC[seg % 2]]
                if st["first"] and st["G"] == 0:
                    load_v(st["eh"])
                S.emit("pe", ("matmul", dict(out=psB[:, c0:], lhsT=ktb, rhs=qv, start=True, stop=False)),
                       reads=[bkt, b_Q[qi]], writes=[bB])
                S.emit("pe", ("matmul", dict(out=psB[:, c0:], lhsT=trin8, rhs=spb[s][:, c0:], start=False,
                                             stop=st["first"])),
                       reads=[self.b_cst, b_sp[s]], writes=[bB])
                if not st["first"]:
                    S.emit("pe", ("matmul", dict(out=psB[:, c0:], lhsT=neg8, rhs=sacc[:, c0:], start=False, stop=True)),
                           reads=[self.b_cst, b_sacc], writes=[bB])
                if k + 2 < ns:
                    stageA(k + 2)
                if k + 1 < ns:
                    stage1(k + 1)
                if st["first"]:
                    S.emit("dve", ("memset", dict(ap=sacc[:, :], constant=0.0)), writes=[b_sacc])
                    S.emit("pe", ("matmul", dict(out=psC[0:64, :], lhsT=ones64, rhs=zeros[:, :], start=True,
                                                 stop=False)),
                           reads=[self.b_cst, self.b_zeros], writes=[bC])
                if not st["last"]:
                    S.emit("dve", ("tensor_tensor", dict(out=sacc[:, c0:], in0=sacc[:, c0:], in1=spb[s][:, c0:],
                                                         op=ALU.add)),
                           reads=[b_sp[s]], writes=[b_sacc])
                S.emit("act", ("activation", dict(out=apb[s][:, c0:], in_=psB[:, c0:], func=AF.Exp, scale=0.125)),
                       reads=[bB], writes=[b_ap[s]])
                if st["diag"]:
                    S.emit("dve", ("tensor_tensor", dict(out=apb[s][:, c0:c0 + 128], in0=apb[s][:, c0:c0 + 128],
                                                         in1=mask, op=ALU.mult)),
                           reads=[self.b_cst], writes=[b_ap[s]])
                S.emit("pe", ("matmul", dict(out=psC[0:64, c0:], lhsT=Vh[:, st["kb"], :], rhs=apb[s][:, c0:],
                                             start=False, stop=st["last"])),
                       reads=[bvv, b_ap[s]], writes=[bC])
                if st["last"]:
                    G = st["G"]
                    cg0, _ = gcols(G)
                    if st["eh"] == 0:
                        S.emit("act", ("copy", dict(out=self.YT[0:64, c, cg0:cg0 + GN], in_=psC[0:64, :])),
                               reads=[bC], writes=[self.bY[(c, G)]])
                    else:
                        o = rr["o"] % 2
                        rr["o"] += 1
                        S.emit("act", ("copy", dict(out=obf[o][0:64, :], in_=psC[0:64, :])),
                               reads=[bC], writes=[b_obf[o]])
                        self.shift(obf[o][0:64, :], b_obf[o], self.YT[64:128, c, cg0:cg0 + GN], self.bY[(c, G)])

        for c in range(6):
            def ld(c=c):
                i = self.wslot()
                src = self.b_in[jb, :, c * 128:(c + 1) * 128].rearrange("(k p) f -> p k f", p=128)
                return i, self.wload(i, 0, 1, [128, KC, 128], src)

            def cp(slot, c=c):
                i, w = slot
                qi = rr["q"] % 2
                rr["q"] += 1
                for g in G4:
                    ps, bps = self.proj(w, self.bWR[i][0], g, banks=[7])
                    S.emit("act", ("copy", dict(out=Qc[qi][:, g * GN:(g + 1) * GN], in_=ps[:, :])),
                           reads=[bps], writes=[b_Q[qi]])
                op = S.emit("sp", ("dma_start", dict(out=KT[:, 0:T], in_=self.k_g[c // 2][(c % 2) * 128:(c % 2 + 1) * 128, :])),
                            reads=[self.b_kvready], writes=[b_KT[0]], dma=self.ch_kt[0])
                self.arena_dmas.append(op)
                op = S.emit("sp", ("dma_start", dict(out=KT[:, T:SEQ], in_=self.k_src[c // 2][(c % 2) * 128:(c % 2 + 1) * 128, :])),
                            reads=[self.b_kvready], writes=[b_KT[1]], dma=self.ch_kt[1])
                self.arena_dmas.append(op)
                attn_chunk(c, qi)
            self.add(ld, cp)
        self.w_o_units(l, G4)
        self.add(None, lambda _: self.barrier())

    def final(self):
        S = self.S
        ost = []
        off = 0
        for _ in range(2):
            a, off = self.carve(off, [128, KC, GN], F32)
            ost.append(a)
        bo = [[Buf() for _ in range(KC)] for _ in range(2)]
        outv = self.outT.rearrange("(k p) t -> p k t", p=128)

        def fn(_):
            for g in range(NG):
                c0, n = gcols(g)
                q = g % 2
                if self.debug_out == "x":
                    src = self.XT[:, :, c0:c0 + n]
                    rd = [self.bX[(k, g)] for k in range(KC)]
                else:
                    self.norm_group([self.XT[:, k, c0:c0 + n] for k in range(KC)],
                                    [self.bX[(k, g)] for k in range(KC)], 80,
                                    [ost[q][:, k, :] for k in range(KC)], bo[q], n)
                    src = ost[q][:, :, :]
                    rd = bo[q]
                op = S.emit("sp", ("dma_start", dict(out=outv[:, :, g * GN:(g + 1) * GN], in_=src)),
                            reads=rd, dma=self.ch_out)
                self.out_ops.append(op)
            S.wait_all("sp", list(self.out_ops))
        self.add(None, fn)

    def program(self):
        nc = self.build()
        S = self.S
        self.units = []
        allX = [self.bX[(k, g)] for k in range(KC) for g in range(-1, NG)]
        memx, _ = self.carve(9400, [128, KC, NMEM], F32)
        b_memx = [Buf() for _ in range(KC)]

        def start(_):
            S.emit("sp", ("dma_start", dict(out=self.small[:, :], in_=self.small_in)), writes=[self.b_small],
                   dma=self.chan("l0"))
            S.emit("pool", ("dma_start", dict(out=self.cst[:, :], in_=self.consts_in)), writes=[self.b_cst],
                   dma=self.chan("l1"))
            S.emit("sp", ("dma_start", dict(out=memx, in_=self.memT_in.rearrange("(k p) t -> p k t", p=128))),
                   writes=b_memx, dma=self.chan("l2"))
            S.emit("sp", ("dma_start", dict(out=self.XT[:, :, :],
                                               in_=self.xT_in.rearrange("(k p) t -> p k t", p=128))),
                   writes=allX, dma=self.chan("l3"))
            S.emit("dve", ("memset", dict(ap=self.zeros_bf[:, :], constant=0.0)), writes=[self.b_zeros])
            self.norm_group([memx[:, k, :] for k in range(KC)], b_memx, 72,
                            [self.memnT[:, k, :] for k in range(KC)], [self.b_memn] * KC, NMEM)
            self.barrier()
        self.add(None, start)
        if self.kv_only:
            self.kv_proj()
        for l in range(self.n_layers):
            if l < 2:
                self.mixer_a(l)
                self.ffn(l, with_halo=(l == 0))
                if l == 1:
                    self.add(None, lambda _: self.barrier())
                    self.kv_proj()
            else:
                self.mixer_b(l)
                self.ffn(l, with_halo=False)
            if l != 1:
                self.add(None, lambda _: self.barrier())
        self.final()
        self.run_units(prefetch=1)
        S.finalize(self.sem)
        with nc.Block() as block:
            @block.tensor
            def _(e):
                S.replay("pe", e)

            @block.scalar
            def _(e):
                S.replay("act", e)

            @block.vector
            def _(e):
                S.replay("dve", e)

            @block.gpsimd
            def _(e):
                S.replay("pool", e)

            @block.sync
            def _(e):
                S.replay("sp", e)
        return nc


def _host_inputs(inp):
    x = np.asarray(inp["x"], dtype=np.float32)
    mem = np.asarray(inp["mem"], dtype=np.float32)
    small = np.zeros((128, 128), np.float32)

    def put(col, vec):
        small[:, col:col + 8] = np.asarray(vec, np.float32).reshape(8, 128).T
    for i in range(4):
        put(8 * i, inp["mix_norm"][i])
        put(32 + 8 * i, inp["ffn_norm"][i])
    put(64, inp["kv_norm"])
    put(72, inp["mem_norm"])
    put(80, inp["final_norm"])
    cw = np.asarray(inp["conv_w"], np.float32)
    for l in range(2):
        for j in range(6):
            for k in range(3):
                small[:, 88 + (l * 6 + j) * 3 + k] = cw[l, k, j * 128:(j + 1) * 128]
    consts = np.zeros((128, 512), np.float32)
    s = np.arange(128)
    consts[:, 0:128] = (s[:, None] < s[None, :]).astype(np.float32)
    consts[:, 128:256] = -8.0 * (s[:, None] >= s[None, :]).astype(np.float32)
    consts[:, 256:384] = 1.0
    consts[:, 384:512] = -8.0
    shared = {k: np.ascontiguousarray(np.asarray(inp[k], np.float32)) for k in
              ("a_in", "b_in", "w_kv_shared", "w_mem_kv", "w_o", "w_gate", "w_up", "w_down")}
    maps = []
    for c in range(N_CORES):
        b, half = c // 2, c % 2
        xT = np.zeros((D, TT), np.float32)
        xT[:, HALO:] = x[b, half * T:(half + 1) * T, :].T
        if half == 1:
            xT[:, :HALO] = x[b, T - HALO:T, :].T
        sm = small.copy()
        sm[:, 124] = float(half)
        m = {"xT": xT, "memT": np.ascontiguousarray(mem[b].T), "small": sm, "consts": consts}
        m.update(shared)
        maps.append(m)
    return maps


_NC_CACHE = {}


def kernel(**inputs):
    key = "full"
    if key not in _NC_CACHE:
        _NC_CACHE[key] = Builder().program()
    nc = _NC_CACHE[key]
    maps = _host_inputs(inputs)
    res = run_bass_kernel_spmd(nc, maps, core_ids=list(range(N_CORES)))
    out = np.zeros((4, SEQ, D), np.float32)
    for c in range(N_CORES):
        b, half = c // 2, c % 2
        out[b, half * T:(half + 1) * T, :] = np.asarray(res.results[c]["outT"]).T
    return out
```

```python
import numpy as np
import concourse.bass as bass
import concourse.mybir as mybir
from concourse.bass_utils import run_bass_kernel_spmd

F32 = mybir.dt.float32
BF16 = mybir.dt.bfloat16
AF = mybir.ActivationFunctionType
ALU = mybir.AluOpType

D = 1024
KC = 8
T = 2048
HALO = 32
TT = T + HALO
NG = 4
GN = 512
SEQ = 4096
NMEM = 256
DFF = 2816
FC = 22
MAINW = 768
EPS = 1e-6
N_CORES = 8
SEM_EPOCH = 24000


class Buf:
    __slots__ = ("w", "r", "name")

    def __init__(self, name=""):
        self.w = None
        self.r = {}
        self.name = name


class Op:
    __slots__ = ("eng", "fn", "deps", "sig", "sem", "val", "dma", "idx")

    def __init__(self, eng, fn, dma):
        self.eng = eng
        self.fn = fn
        self.deps = set()
        self.sig = False
        self.sem = None
        self.val = 0
        self.dma = dma
        self.idx = 0


class Sched:
    ENGS = ("pe", "act", "dve", "pool", "sp")

    def __init__(self, nc):
        self.nc = nc
        self.ops = {e: [] for e in self.ENGS}
        self.n = 0

    def emit(self, eng, fn, reads=(), writes=(), dma=None, force_sig=False):
        op = Op(eng, fn, dma)
        op.idx = self.n
        self.n += 1
        deps = op.deps
        for b in reads:
            if b.w is not None:
                deps.add(b.w)
        for b in writes:
            if b.w is not None:
                deps.add(b.w)
            for r in b.r.values():
                if isinstance(r, list):
                    deps.update(r)
                else:
                    deps.add(r)
        deps.discard(op)
        if eng == "pe" and dma is None:
            for d in [d for d in deps if d.eng == "pe" and d.dma is None]:
                deps.discard(d)
        for d in deps:
            d.sig = True
        for b in reads:
            if dma is not None:
                b.r.setdefault(("dma", eng), [])
                b.r[("dma", eng)].append(op)
            else:
                b.r[eng] = op
        for b in writes:
            b.w = op
            b.r = {}
        if dma is not None or force_sig:
            op.sig = True
        self.ops[eng].append(op)
        return op

    def wait_all(self, eng, ops):
        op = Op(eng, None, None)
        op.deps = set(ops)
        for d in ops:
            d.sig = True
        self.ops[eng].append(op)
        return op

    def finalize(self, semalloc):
        for e in self.ENGS:
            cnt = 0
            sem = None
            for op in self.ops[e]:
                if op.fn is None or not op.sig:
                    continue
                if op.dma is not None:
                    op.dma.count += 1
                    op.sem = op.dma.sem
                    op.val = 16 * op.dma.count
                else:
                    if sem is None or cnt >= SEM_EPOCH:
                        sem = semalloc("e_" + e)
                        cnt = 0
                    cnt += 1
                    op.sem = sem
                    op.val = cnt

    def replay(self, eng, handle):
        waited = {}
        for op in self.ops[eng]:
            need = {}
            for d in op.deps:
                k = id(d.sem)
                if k not in need or need[k][1] < d.val:
                    need[k] = (d.sem, d.val)
            for k, (sem, val) in need.items():
                if waited.get(k, 0) < val:
                    handle.wait_ge(sem, val)
                    waited[k] = val
            if op.fn is None:
                continue
            name, kw = op.fn
            pos = kw.pop("_pos", ())
            ins = getattr(handle, name)(*pos, **kw)
            if op.sig:
                ins.then_inc(op.sem, 16 if op.dma is not None else 1)


class DmaChan:
    def __init__(self, sem):
        self.sem = sem
        self.count = 0


def gcols(gi):
    if gi < 0:
        return 0, HALO
    return HALO + GN * gi, GN


class Builder:
    def __init__(self, n_layers=4, debug_out=None, kv_only=False, only_inputs=None):
        self.n_layers = n_layers
        self.debug_out = debug_out
        self.kv_only = kv_only
        self.only_inputs = only_inputs

    def sem(self, name):
        self._semn += 1
        s = self.nc.alloc_semaphore(f"{name}_{self._semn}")
        return s

    def chan(self, name):
        return DmaChan(self.sem("d_" + name))

    def build(self):
        nc = bass.Bass("TRN2", target_bir_lowering=False)
        self.nc = nc
        self._semn = 0
        S = Sched(nc)
        self.S = S
        dt0 = nc.dram_tensor

        class _Dummy:
            def ap(self):
                return None

        def dt(name, shape, dtype, kind="Internal"):
            if kind == "ExternalInput" and self.only_inputs is not None and name not in self.only_inputs:
                return _Dummy()
            return dt0(name, shape, dtype, kind=kind)
        self.xT_in = dt("xT", [D, TT], F32, kind="ExternalInput").ap()
        self.memT_in = dt("memT", [D, NMEM], F32, kind="ExternalInput").ap()
        self.small_in = dt("small", [128, 128], F32, kind="ExternalInput").ap()
        self.consts_in = dt("consts", [128, 512], F32, kind="ExternalInput").ap()
        self.a_in = dt("a_in", [2, D, 2560], F32, kind="ExternalInput").ap()
        self.b_in = dt("b_in", [2, D, 1024], F32, kind="ExternalInput").ap()
        self.w_kv = dt("w_kv_shared", [D, 1536], F32, kind="ExternalInput").ap()
        self.w_mem_kv = dt("w_mem_kv", [4, D, 512], F32, kind="ExternalInput").ap()
        self.w_o = dt("w_o", [4, D, D], F32, kind="ExternalInput").ap()
        self.w_gate = dt("w_gate", [4, D, DFF], F32, kind="ExternalInput").ap()
        self.w_up = dt("w_up", [4, D, DFF], F32, kind="ExternalInput").ap()
        self.w_down = dt("w_down", [4, DFF, D], F32, kind="ExternalInput").ap()
        self.outT = dt("outT", [D, T], F32, kind="ExternalOutput").ap()
        self.k_src = [dt(f"k_src{i}", [256, T], BF16) for i in range(3)]
        self.v_src = [dt(f"v_src{i}", [T, 256], BF16) for i in range(3)]
        self.k_g = [dt(f"k_g{i}", [512, T], BF16) for i in range(3)]
        self.v_g = [dt(f"v_g{i}", [2 * T, 256], BF16) for i in range(3)]

        st = nc.alloc_sbuf_tensor
        self.XT = st("XT", [128, KC, TT], F32)
        self.HT = st("HT", [128, KC, TT], BF16)
        self.YT = st("YT", [128, KC, TT], BF16)
        self.ARENA = st("ARENA", [128, 11776], F32)
        self.WR = [st(f"WR{i}", [128, 3072], BF16) for i in range(2)]
        self.small = st("small_sb", [128, 128], F32)
        self.cst = st("cst", [128, 512], BF16)
        self.memnT = st("memnT", [128, KC, NMEM], BF16)
        self.mKT = st("mKT", [128, 2, NMEM], BF16)
        self.mV = st("mV", [128, 2, 256], BF16)
        self.sq = [st(f"sq{i}", [128, GN], BF16) for i in range(2)]
        self.rstd = [st(f"rstd{i}", [128, GN], F32) for i in range(2)]
        self.PS = [nc.alloc_psum_tensor(f"ps{i}", [128, GN], F32) for i in range(8)]

        self.bPS = [Buf(f"ps{i}") for i in range(8)]
        self.ps_rr = 0
        self.bWR = [[Buf(f"wr{i}_{j}") for j in range(3)] for i in range(2)]
        self.wr_rr = 0
        self.wr_chan = [[self.chan(f"wr{i}_{j}") for j in range(3)] for i in range(2)]
        self.sg = [st(f"sg{i}", [128, GN], F32) for i in range(2)]
        self.bsg = [Buf() for _ in range(2)]
        self.sg_rr = 0
        self.zeros_bf = st("zeros_bf", [128, GN], BF16)
        self.b_zeros = Buf()
        self.bG = {(c, s): Buf() for c in range(FC) for s in range(3)}
        self.ch_v = [self.chan("v0"), self.chan("v1")]
        self.ch_kt = [self.chan("kt0"), self.chan("kt1")]
        self.arena_dmas = []
        self.ps_rrs = {}
        self.bX = {(k, g): Buf() for k in range(KC) for g in range(-1, NG)}
        self.bH = {(k, g): Buf() for k in range(KC) for g in range(-1, NG)}
        self.bY = {(k, g): Buf() for k in range(KC) for g in range(-1, NG)}
        self.bsq = [Buf() for _ in range(2)]
        self.brstd = [Buf() for _ in range(2)]
        self.sq_rr = 0
        self.rstd_rr = 0
        self.b_small = Buf()
        self.b_cst = Buf()
        self.b_memn = Buf()
        self.b_mK = Buf()
        self.b_mV = Buf()
        self.ch_misc = self.chan("misc")
        self.ch_shift = [self.chan(f"sh{i}") for i in range(4)]
        self.shift_rr = 0
        self.ch_out = self.chan("out")
        self.out_ops = []
        return nc

    def ps_next(self, banks=None):
        if banks is None:
            banks = list(range(8))
        key = tuple(banks)
        r = self.ps_rrs.get(key, 0)
        self.ps_rrs[key] = r + 1
        i = banks[r % len(banks)]
        return self.PS[i], self.bPS[i]

    def barrier(self):
        S = self.S
        last = []
        for e in ("pe", "act", "dve"):
            for op in reversed(S.ops[e]):
                if op.fn is not None:
                    last.append(op)
                    break
        dm = list(self.arena_dmas)
        self.arena_dmas = []
        for e in ("pe", "act", "dve", "sp"):
            S.wait_all(e, [o for o in last if o.eng != e] + dm)

    def carve(self, off, shape, dtype):
        n = 1
        for s in shape[1:]:
            n *= s
        words = n if dtype == F32 else (n + 1) // 2
        ap = self.ARENA[:, off:off + words]
        if dtype == BF16:
            ap = ap.bitcast(BF16)
        if len(shape) == 3:
            ap = ap.rearrange("p (a b) -> p a b", a=shape[1])
        return ap, off + words

    def add(self, load, compute):
        self.units.append((load, compute))

    def run_units(self, prefetch=1):
        pend = []
        n = len(self.units)
        for i in range(n + prefetch):
            if i < n:
                ld = self.units[i][0]
                pend.append(ld() if ld is not None else None)
            if i >= prefetch:
                self.units[i - prefetch][1](pend[i - prefetch])

    def wslot(self):
        i = self.wr_rr % 2
        self.wr_rr += 1
        return i

    def wload(self, i, seg0, nseg, shape, src):
        n = shape[1] * shape[2]
        dst = self.WR[i][:, seg0 * 1024: seg0 * 1024 + n].rearrange("p (a b) -> p a b", a=shape[1])
        bufs = [self.bWR[i][s] for s in range(seg0, seg0 + nseg)]
        self.S.emit("pool", ("dma_start", dict(out=dst, in_=src)),
                    writes=bufs, dma=self.wr_chan[i][seg0])
        return dst

    def norm_group(self, xs, bxs, gcol, outs, bouts, n):
        S = self.S
        ps, bps = self.ps_next()
        ones = self.cst[:, 256:384]
        for k in range(KC):
            i = self.sq_rr % 2
            self.sq_rr += 1
            sq, bsq = self.sq[i], self.bsq[i]
            S.emit("dve", ("tensor_tensor", dict(out=sq[:, :n], in0=xs[k], in1=xs[k], op=ALU.mult)),
                   reads=[bxs[k]], writes=[bsq])
            S.emit("pe", ("matmul", dict(out=ps[:, :n], lhsT=ones, rhs=sq[:, :n],
                                                         start=(k == 0), stop=(k == KC - 1))),
                   reads=[bsq, self.b_cst], writes=[bps])
        i = self.rstd_rr % 2
        self.rstd_rr += 1
        rstd, brstd = self.rstd[i], self.brstd[i]
        S.emit("act", ("activation", dict(out=rstd[:, :n], in_=ps[:, :n], func=AF.Ln, bias=EPS, scale=1.0 / D)),
               reads=[bps], writes=[brstd])
        S.emit("act", ("activation", dict(out=rstd[:, :n], in_=rstd[:, :n], func=AF.Exp, scale=-0.5)),
               reads=[brstd], writes=[brstd])
        for k in range(KC):
            g = self.small[:, gcol + k: gcol + k + 1]
            S.emit("dve", ("scalar_tensor_tensor", dict(out=outs[k], in0=xs[k], scalar=g,
                                                                     in1=rstd[:, :n], op0=ALU.mult, op1=ALU.mult)),
                   reads=[bxs[k], brstd, self.b_small], writes=[bouts[k]])

    def norm_tokens(self, groups, gcol):
        for gi in groups:
            c0, n = gcols(gi)
            self.norm_group([self.XT[:, k, c0:c0 + n] for k in range(KC)], [self.bX[(k, gi)] for k in range(KC)],
                            gcol, [self.HT[:, k, c0:c0 + n] for k in range(KC)],
                            [self.bH[(k, gi)] for k in range(KC)], n)

    def proj(self, wview, bw, gi, banks=None):
        c0, n = gcols(gi)
        ps, bps = self.ps_next(banks)
        for k in range(KC):
            self.S.emit("pe", ("matmul", dict(out=ps[:, :n], lhsT=wview[:, k, :], rhs=self.HT[:, k, c0:c0 + n],
                                                       start=(k == 0), stop=(k == KC - 1))),
                        reads=[bw, self.bH[(k, gi)]], writes=[bps])
        return ps, bps

    def mem_kv(self, l):
        S = self.S

        def ld(part):
            def f():
                i = self.wslot()
                src = self.w_mem_kv[l, :, part * 256:(part + 1) * 256].rearrange("(k p) f -> p k f", p=128)
                return i, self.wload(i, 0, 2, [128, KC, 256], src)
            return f

        def ck(slot):
            i, w = slot
            for c in range(2):
                ps, bps = self.ps_next()
                for k in range(KC):
                    S.emit("pe", ("matmul", dict(out=ps[:, :NMEM], lhsT=w[:, k, c * 128:(c + 1) * 128],
                                                                     rhs=self.memnT[:, k, :], start=(k == 0),
                                                                     stop=(k == KC - 1))),
                           reads=[self.bWR[i][0], self.bWR[i][1], self.b_memn], writes=[bps])
                S.emit("act", ("copy", dict(out=self.mKT[:, c, :], in_=ps[:, :NMEM])),
                       reads=[bps], writes=[self.b_mK])

        def cv(slot):
            i, w = slot
            for mc in range(2):
                ps, bps = self.ps_next()
                for k in range(KC):
                    S.emit("pe", ("matmul", dict(out=ps[:, :256],
                                                                       lhsT=self.memnT[:, k, mc * 128:(mc + 1) * 128],
                                                                       rhs=w[:, k, :], start=(k == 0),
                                                                       stop=(k == KC - 1))),
                           reads=[self.bWR[i][0], self.bWR[i][1], self.b_memn], writes=[bps])
                S.emit("act", ("copy", dict(out=self.mV[:, mc, :], in_=ps[:, :256])),
                       reads=[bps], writes=[self.b_mV])

        self.add(ld(0), ck)
        self.add(ld(1), cv)

    def mem_attn(self, groups, qmT, b_qm, PT, bPT, rden, brden, otmp, botmp):
        S = self.S
        ones64 = self.cst[:, 256:320]
        rr = 0
        for gi in groups:
            c0, n = gcols(gi)
            for hm in range(4):
                c, off = hm // 2, (hm % 2) * 64
                psO, bO = self.ps_next()
                psD, bD = self.ps_next()
                for mc in range(2):
                    psS, bS = self.ps_next()
                    S.emit("pe", ("matmul", dict(out=psS[:, :n], lhsT=self.mKT[off:off + 64, c, mc * 128:(mc + 1) * 128],
                        rhs=qmT[off:off + 64, c, c0:c0 + n], start=True, stop=True)),
                        reads=[self.b_mK, b_qm[(c, gi)]], writes=[bS])
                    j = rr % 2
                    rr += 1
                    S.emit("act", ("activation", dict(out=PT[j][:, :n], in_=psS[:, :n], func=AF.Exp,
                                                                       scale=0.125)),
                           reads=[bS], writes=[bPT[j]])
                    S.emit("pe", ("matmul", dict(out=psO[0:64, :n], lhsT=self.mV[:, mc, hm * 64:(hm + 1) * 64], rhs=PT[j][:, :n],
                        start=(mc == 0), stop=(mc == 1))), reads=[self.b_mV, bPT[j]], writes=[bO])
                    S.emit("pe", ("matmul", dict(out=psD[0:64, :n], lhsT=ones64, rhs=PT[j][:, :n], start=(mc == 0), stop=(mc == 1))),
                        reads=[self.b_cst, bPT[j]], writes=[bD])
                j = rr % 2
                S.emit("dve", ("reciprocal", dict(out=rden[j][0:64, :n], in_=psD[0:64, :n])),
                       reads=[bD], writes=[brden[j]])
                if off == 0:
                    S.emit("dve", ("tensor_tensor", dict(
                        out=self.YT[0:64, 6 + c, c0:c0 + n], in0=psO[0:64, :n], in1=rden[j][0:64, :n], op=ALU.mult)),
                        reads=[bO, brden[j]], writes=[self.bY[(6 + c, gi)]])
                else:
                    S.emit("dve", ("tensor_tensor", dict(
                        out=otmp[j][0:64, :n], in0=psO[0:64, :n], in1=rden[j][0:64, :n], op=ALU.mult)),
                        reads=[bO, brden[j]], writes=[botmp[j]])
                    self.shift(otmp[j][0:64, :n], botmp[j], self.YT[64:128, 6 + c, c0:c0 + n], self.bY[(6 + c, gi)])

    def shift(self, src, bsrc, dst, bdst):
        ch = self.ch_shift[self.shift_rr % len(self.ch_shift)]
        self.shift_rr += 1
        op = self.S.emit("sp", ("dma_start", dict(out=dst, in_=src)), reads=[bsrc], writes=[bdst], dma=ch)
        self.arena_dmas.append(op)

    def w_o_units(self, l, groups):
        S = self.S
        for m in range(KC):
            def ld(m=m):
                i = self.wslot()
                src = self.w_o[l, :, m * 128:(m + 1) * 128].rearrange("(k p) f -> p k f", p=128)
                return i, self.wload(i, 0, 1, [128, KC, 128], src)

            def cp(slot, m=m):
                i, w = slot
                for gi in groups:
                    c0, n = gcols(gi)
                    ps, bps = self.ps_next()
                    for c in range(KC):
                        S.emit("pe", ("matmul", dict(out=ps[:, :n], lhsT=w[:, c, :],
                                                                    rhs=self.YT[:, c, c0:c0 + n], start=(c == 0),
                                                                    stop=(c == KC - 1))),
                               reads=[self.bWR[i][0], self.bY[(c, gi)]], writes=[bps])
                    S.emit("dve", ("tensor_tensor", dict(out=self.XT[:, m, c0:c0 + n],
                                                                   in0=ps[:, :n], in1=self.XT[:, m, c0:c0 + n],
                                                                   op=ALU.add)),
                           reads=[bps, self.bX[(m, gi)]], writes=[self.bX[(m, gi)]])
            self.add(ld, cp)

    def ffn(self, l, with_halo):
        S = self.S
        GTW = 2 * GN + HALO
        GT, _ = self.carve(0, [128, FC, GTW], BF16)
        for sb in range(2):
            groups = [2 * sb, 2 * sb + 1]
            if with_halo and sb == 0:
                groups = [-1] + groups

            def lcol(gi, sb=sb):
                return 2 * GN if gi < 0 else (gi - 2 * sb) * GN

            def slot_of(gi, sb=sb):
                return 2 if gi < 0 else gi - 2 * sb

            self.add(None, lambda _, groups=groups: self.norm_tokens(groups, 32 + 8 * l))
            for c in range(FC):
                def ld(c=c):
                    i = self.wslot()
                    sg = self.w_gate[l, :, c * 128:(c + 1) * 128].rearrange("(k p) f -> p k f", p=128)
                    su = self.w_up[l, :, c * 128:(c + 1) * 128].rearrange("(k p) f -> p k f", p=128)
                    return i, self.wload(i, 0, 1, [128, KC, 128], sg), self.wload(i, 1, 1, [128, KC, 128], su)

                def cp(slot, c=c, groups=groups, lcol=lcol, slot_of=slot_of):
                    i, wg, wu = slot
                    for gi in groups:
                        c0, n = gcols(gi)
                        psg, bg = self.proj(wg, self.bWR[i][0], gi)
                        psu, bu = self.proj(wu, self.bWR[i][1], gi)
                        j = self.sg_rr % 2
                        self.sg_rr += 1
                        sgt, bsg = self.sg[j], self.bsg[j]
                        S.emit("act", ("activation", dict(out=sgt[:, :n], in_=psg[:, :n],
                                                                               func=AF.Silu)),
                               reads=[bg], writes=[bsg])
                        lc = lcol(gi)
                        S.emit("dve", ("tensor_tensor", dict(
                            out=GT[:, c, lc:lc + n], in0=psu[:, :n], in1=sgt[:, :n], op=ALU.mult)),
                            reads=[bu, bsg], writes=[self.bG[(c, slot_of(gi))]])
                self.add(ld, cp)
            for m in range(KC):
                def ld(m=m):
                    i = self.wslot()
                    src = self.w_down[l, :, m * 128:(m + 1) * 128].rearrange("(c p) f -> p c f", p=128)
                    return i, self.wload(i, 0, 3, [128, FC, 128], src)

                def cp(slot, m=m, groups=groups, lcol=lcol, slot_of=slot_of):
                    i, w = slot
                    for gi in groups:
                        c0, n = gcols(gi)
                        lc = lcol(gi)
                        ps, bps = self.ps_next()
                        for c in range(FC):
                            S.emit("pe", ("matmul", dict(out=ps[:, :n], lhsT=w[:, c, :],
                                                                               rhs=GT[:, c, lc:lc + n],
                                                                               start=(c == 0), stop=(c == FC - 1))),
                                   reads=[self.bWR[i][0], self.bG[(c, slot_of(gi))]], writes=[bps])
                        S.emit("dve", ("tensor_tensor", dict(out=self.XT[:, m, c0:c0 + n],
                                                                       in0=ps[:, :n], in1=self.XT[:, m, c0:c0 + n],
                                                                       op=ALU.add)),
                               reads=[bps, self.bX[(m, gi)]], writes=[self.bX[(m, gi)]])
                self.add(ld, cp)

    def mixer_a(self, l):
        S = self.S
        halo_full = (l == 0)
        allg = [-1, 0, 1, 2, 3]
        outg = allg if halo_full else [0, 1, 2, 3]
        off = 0
        vbuf, off = self.carve(off, [128, TT + 4], F32)
        qmT, off = self.carve(off, [128, 2, TT], BF16)
        cgs, c1, c2 = [], [], []
        for lst in (cgs, c1, c2):
            for _ in range(2):
                a, off = self.carve(off, [128, GN], F32)
                lst.append(a)
        PT, rden, otmp = [], [], []
        for _ in range(2):
            a, off = self.carve(off, [128, GN], BF16)
            PT.append(a)
        for _ in range(2):
            a, off = self.carve(off, [128, GN], F32)
            rden.append(a)
        for _ in range(2):
            a, off = self.carve(off, [128, GN], BF16)
            otmp.append(a)
        bv = {g: Buf() for g in allg}
        bcgs = [Buf(), Buf()]
        bc1 = [Buf(), Buf()]
        bc2 = [Buf(), Buf()]
        bPT = [Buf(), Buf()]
        brden = [Buf(), Buf()]
        botmp = [Buf(), Buf()]
        b_qm = {(c, g): Buf() for c in range(2) for g in allg}
        flag = self.small[:, 124:125]
        rr = [0]

        def pre(_):
            self.norm_tokens(allg, 8 * l)
            S.emit("dve", ("memset", dict(ap=vbuf[:, 0:2], constant=0.0)), writes=[bv[-1]])
        self.add(None, pre)
        self.mem_kv(l)

        for j in range(6):
            def ld(j=j):
                i = self.wslot()
                views = []
                for s in range(3):
                    src = self.a_in[l, :, s * MAINW + j * 128: s * MAINW + (j + 1) * 128].rearrange(
                        "(k p) f -> p k f", p=128)
                    views.append(self.wload(i, s, 1, [128, KC, 128], src))
                return i, views

            def cp(slot, j=j):
                i, (wb, wc, wu) = slot
                for gi in allg:
                    c0, n = gcols(gi)
                    need_y = gi in outg
                    psc, bpc = self.proj(wc, self.bWR[i][1], gi)
                    psu, bpu = self.proj(wu, self.bWR[i][2], gi)
                    if need_y:
                        psb, bpb = self.proj(wb, self.bWR[i][0], gi)
                    q = rr[0] % 2
                    rr[0] += 1
                    S.emit("act", ("copy", dict(out=cgs[q][:, :n], in_=psc[:, :n])),
                           reads=[bpc], writes=[bcgs[q]])
                    S.emit("dve", ("tensor_tensor", dict(out=vbuf[:, 2 + c0:2 + c0 + n],
                                                                          in0=psu[:, :n], in1=cgs[q][:, :n],
                                                                          op=ALU.mult)),
                           reads=[bpu, bcgs[q]], writes=[bv[gi]])
                    if gi < 0:
                        S.emit("dve", ("tensor_scalar", dict(out=vbuf[:, 2:2 + HALO], in0=vbuf[:, 2:2 + HALO],
                                                                scalar1=flag, scalar2=None, op0=ALU.mult)),
                               reads=[self.b_small], writes=[bv[gi]])
                    if not need_y:
                        continue
                    wcol = 88 + (l * 6 + j) * 3
                    w0 = self.small[:, wcol:wcol + 1]
                    w1 = self.small[:, wcol + 1:wcol + 2]
                    w2 = self.small[:, wcol + 2:wcol + 3]
                    prev = [bv[gi - 1]] if gi >= 0 else []
                    S.emit("act", ("mul", dict(out=c1[q][:, :n], in_=vbuf[:, 2 + c0:2 + c0 + n], mul=w2)),
                           reads=[bv[gi], self.b_small], writes=[bc1[q]])
                    S.emit("dve", ("scalar_tensor_tensor", dict(
                        out=c2[q][:, :n], in0=vbuf[:, 1 + c0:1 + c0 + n], scalar=w1, in1=c1[q][:, :n],
                        op0=ALU.mult, op1=ALU.add)), reads=[bv[gi], bc1[q], self.b_small] + prev, writes=[bc2[q]])
                    S.emit("dve", ("scalar_tensor_tensor", dict(
                        out=c1[q][:, :n], in0=vbuf[:, c0:c0 + n], scalar=w0, in1=c2[q][:, :n],
                        op0=ALU.mult, op1=ALU.add)), reads=[bv[gi], bc2[q], self.b_small] + prev, writes=[bc1[q]])
                    S.emit("dve", ("tensor_tensor", dict(out=self.YT[:, j, c0:c0 + n],
                                                                          in0=psb[:, :n], in1=c1[q][:, :n],
                                                                          op=ALU.mult)),
                           reads=[bpb, bc1[q]], writes=[self.bY[(j, gi)]])
            self.add(ld, cp)

        def ldq():
            i = self.wslot()
            src = self.a_in[l, :, 3 * MAINW:3 * MAINW + 256].rearrange("(k p) f -> p k f", p=128)
            return i, self.wload(i, 0, 2, [128, KC, 256], src)

        def cpq(slot):
            i, w = slot
            for gi in outg:
                c0, n = gcols(gi)
                for c in range(2):
                    ps, bps = self.ps_next()
                    for k in range(KC):
                        S.emit("pe", ("matmul", dict(out=ps[:, :n], lhsT=w[:, k, c * 128:(c + 1) * 128],
                                                                         rhs=self.HT[:, k, c0:c0 + n], start=(k == 0),
                                                                         stop=(k == KC - 1))),
                               reads=[self.bWR[i][0], self.bWR[i][1], self.bH[(k, gi)]], writes=[bps])
                    S.emit("act", ("copy", dict(out=qmT[:, c, c0:c0 + n], in_=ps[:, :n])),
                           reads=[bps], writes=[b_qm[(c, gi)]])
            self.mem_attn(outg, qmT, b_qm, PT, bPT, rden, brden, otmp, botmp)
        self.add(ldq, cpq)
        self.w_o_units(l, outg)
        self.add(None, lambda _: self.barrier())

    def kv_proj(self):
        S = self.S
        off = 0
        kst, vst = [], []
        for _ in range(2):
            a, off = self.carve(off, [128, GN], BF16)
            kst.append(a)
        for _ in range(2):
            a, off = self.carve(off, [128, 256], BF16)
            vst.append(a)
        bk = [Buf(), Buf()]
        bvs = [Buf(), Buf()]
        ch_st = self.chan("st")
        b_gate = Buf()
        self.b_kvready = Buf()
        rr = [0]
        self.add(None, lambda _: self.norm_tokens([0, 1, 2, 3], 64))
        for c in range(6):
            def ld(c=c):
                i = self.wslot()
                src = self.w_kv[:, c * 128:(c + 1) * 128].rearrange("(k p) f -> p k f", p=128)
                return i, self.wload(i, 0, 1, [128, KC, 128], src)

            def cp(slot, c=c):
                i, w = slot
                for g in range(NG):
                    ps, bps = self.proj(w, self.bWR[i][0], g)
                    q = rr[0] % 2
                    rr[0] += 1
                    S.emit("act", ("copy", dict(out=kst[q][:, :], in_=ps[:, :])),
                           reads=[bps], writes=[bk[q]])
                    dst = self.k_src[c // 2][(c % 2) * 128:(c % 2 + 1) * 128, g * GN:(g + 1) * GN]
                    op = S.emit("sp", ("dma_start", dict(out=dst, in_=kst[q][:, :])),
                                reads=[bk[q], b_gate], dma=ch_st)
                    self.arena_dmas.append(op)
            self.add(ld, cp)
        for hv in range(3):
            def ld(hv=hv):
                i = self.wslot()
                src = self.w_kv[:, MAINW + hv * 256: MAINW + (hv + 1) * 256].rearrange("(k p) f -> p k f", p=128)
                return i, self.wload(i, 0, 2, [128, KC, 256], src)

            def cp(slot, hv=hv):
                i, w = slot
                for tt in range(16):
                    g = tt // 4
                    c0 = HALO + tt * 128
                    ps, bps = self.ps_next()
                    for k in range(KC):
                        S.emit("pe", ("matmul", dict(out=ps[:, :256], lhsT=self.HT[:, k, c0:c0 + 128],
                                                                           rhs=w[:, k, :], start=(k == 0),
                                                                           stop=(k == KC - 1))),
                               reads=[self.bWR[i][0], self.bWR[i][1], self.bH[(k, g)]], writes=[bps])
                    q = rr[0] % 2
                    rr[0] += 1
                    S.emit("act", ("copy", dict(out=vst[q][:, :], in_=ps[:, :256])),
                           reads=[bps], writes=[bvs[q]])
                    dst = self.v_src[hv][tt * 128:(tt + 1) * 128, :]
                    op = S.emit("sp", ("dma_start", dict(out=dst, in_=vst[q][:, :])),
                                reads=[bvs[q], b_gate], dma=ch_st)
                    self.arena_dmas.append(op)
            self.add(ld, cp)

        def coll(_):
            rg = [[0, 1], [2, 3], [4, 5], [6, 7]]
            srcs = self.k_src + self.v_src
            dsts = self.k_g + self.v_g
            for ci in range(6):
                S.emit("pool", ("collective_compute", dict(_pos=("AllGather", ALU.bypass), replica_groups=rg,
                                                           ins=[srcs[ci].ap().opt()], outs=[dsts[ci].ap().opt()])),
                       reads=([b_gate] if ci else []),
                       writes=([b_gate] if ci == 0 else []) + ([self.b_kvready] if ci == 5 else []), force_sig=True)
            self.barrier()
        self.add(None, coll)

    def mixer_b(self, l):
        S = self.S
        jb = l - 2
        G4 = [0, 1, 2, 3]
        off = 0
        Qc = []
        for _ in range(2):
            a, off = self.carve(off, [128, T], BF16)
            Qc.append(a)
        qmT, off = self.carve(off, [128, 2, TT], BF16)
        KT, off = self.carve(off, [128, SEQ], BF16)
        Vh, off = self.carve(off, [128, 32, 64], BF16)
        ebuf, off = self.carve(off, [128, GN], F32)
        spb, apb, scl, obf = [], [], [], []
        for _ in range(2):
            a, off = self.carve(off, [128, GN], BF16)
            spb.append(a)
        for _ in range(2):
            a, off = self.carve(off, [128, GN], BF16)
            apb.append(a)
        for _ in range(2):
            a, off = self.carve(off, [128, GN], F32)
            scl.append(a)
        sacc, off = self.carve(off, [128, GN], BF16)
        for _ in range(2):
            a, off = self.carve(off, [128, GN], BF16)
            obf.append(a)
        assert off <= 11776, off
        b_Q = [Buf(), Buf()]
        b_qm = {(c, g): Buf() for c in range(2) for g in G4}
        b_KT = [Buf(), Buf()]
        b_V = [Buf(), Buf()]
        b_e = Buf()
        b_sp = [Buf(), Buf()]
        b_ap = [Buf(), Buf()]
        b_scl = [Buf(), Buf()]
        b_sacc = Buf()
        b_obf = [Buf(), Buf()]
        flag = self.small[:, 124:125]
        mask = self.cst[:, 0:128]
        trin8 = self.cst[:, 128:256]
        ones64 = self.cst[:, 256:320]
        neg8 = self.cst[:, 384:512]
        zeros = self.zeros_bf
        BA, BC = [0, 1, 2], [4, 5]
        rr = {"q": 0, "s": 0, "o": 0}

        self.add(None, lambda _: self.norm_tokens(G4, 8 * l))
        self.mem_kv(l)

        def ldq():
            i = self.wslot()
            src = self.b_in[jb, :, MAINW:MAINW + 256].rearrange("(k p) f -> p k f", p=128)
            return i, self.wload(i, 0, 2, [128, KC, 256], src)

        def cpq(slot):
            i, w = slot
            for gi in G4:
                c0, n = gcols(gi)
                for c in range(2):
                    ps, bps = self.ps_next([7])
                    for k in range(KC):
                        S.emit("pe", ("matmul", dict(out=ps[:, :n], lhsT=w[:, k, c * 128:(c + 1) * 128],
                                                                         rhs=self.HT[:, k, c0:c0 + n], start=(k == 0),
                                                                         stop=(k == KC - 1))),
                               reads=[self.bWR[i][0], self.bWR[i][1], self.bH[(k, gi)]], writes=[bps])
                    S.emit("act", ("copy", dict(out=qmT[:, c, c0:c0 + n], in_=ps[:, :n])),
                           reads=[bps], writes=[b_qm[(c, gi)]])
            self.mem_attn(G4, qmT, b_qm, apb, b_ap, scl, b_scl, obf, b_obf)
        self.add(ldq, cpq)

        def attn_chunk(c, qi):
            steps = []
            for eh in range(2):
                for G in range(NG):
                    blocks = [(16 + ob, (ob - 4 * G) * 128 if ob >= 4 * G else 0, ob >= 4 * G)
                              for ob in range(4 * G + 3, -1, -1)]
                    blocks += [(pb, 0, False) for pb in range(15, -1, -1)]
                    for bi, (kb, c0, diag) in enumerate(blocks):
                        steps.append(dict(eh=eh, G=G, kb=kb, c0=c0, diag=diag, first=(bi == 0),
                                          last=(bi == len(blocks) - 1)))
            ns = len(steps)

            def views(st):
                po = 64 * st["eh"]
                c0 = st["c0"]
                n = GN - c0
                q0 = st["G"] * GN + c0
                ktb = KT[po:po + 64, st["kb"] * 128:(st["kb"] + 1) * 128]
                qv = Qc[qi][po:po + 64, q0:q0 + n]
                bkt = b_KT[0] if st["kb"] < 16 else b_KT[1]
                bvv = b_V[0] if st["kb"] < 16 else b_V[1]
                return c0, ktb, qv, bkt, bvv

            def stageA(k):
                st = steps[k]
                c0, ktb, qv, bkt, bvv = views(st)
                psA, bA = self.PS[BA[k % 3]], self.bPS[BA[k % 3]]
                S.emit("pe", ("matmul", dict(out=psA[:, c0:], lhsT=ktb, rhs=qv, start=True, stop=False)),
                       reads=[bkt, b_Q[qi]], writes=[bA])

            def stage1(k):
                st = steps[k]
                c0 = st["c0"]
                s = k % 2
                psA, bA = self.PS[BA[k % 3]], self.bPS[BA[k % 3]]
                S.emit("act", ("activation", dict(out=ebuf[:, c0:], in_=psA[:, c0:], func=AF.Exp, scale=0.125)),
                       reads=[bA], writes=[b_e])
                S.emit("act", ("activation", dict(out=spb[s][:, c0:], in_=ebuf[:, c0:], func=AF.Ln, bias=1.0,
                                                  scale=1.0)), reads=[b_e], writes=[b_sp[s]])
                if st["diag"]:
                    S.emit("dve", ("tensor_tensor", dict(out=spb[s][:, c0:c0 + 128], in0=spb[s][:, c0:c0 + 128],
                                                         in1=mask, op=ALU.mult)),
                           reads=[self.b_cst], writes=[b_sp[s]])

            def load_v(eh):
                h = 2 * c + eh
                srcp = self.v_g[h // 4][0:T, (h % 4) * 64:(h % 4 + 1) * 64].rearrange("(b p) d -> p b d", p=128)
                srco = self.v_src[h // 4][:, (h % 4) * 64:(h % 4 + 1) * 64].rearrange("(b p) d -> p b d", p=128)
                op = S.emit("sp", ("dma_start", dict(out=Vh[:, 0:16, :], in_=srcp)), reads=[self.b_kvready],
                            writes=[b_V[0]], dma=self.ch_v[0])
                self.arena_dmas.append(op)
                op = S.emit("sp", ("dma_start", dict(out=Vh[:, 16:32, :], in_=srco)), reads=[self.b_kvready],
                            writes=[b_V[1]], dma=self.ch_v[1])
                self.arena_dmas.append(op)
                S.emit("dve", ("tensor_scalar", dict(out=Vh[:, 0:16, :], in0=Vh[:, 0:16, :], scalar1=flag,
                                                     scalar2=None, op0=ALU.mult)),
                       reads=[self.b_small], writes=[b_V[0]])

            stageA(0)
            if ns > 1:
                stageA(1)
            stage1(0)
            seg = -1
            for k in range(ns):
                st = steps[k]
                c0, ktb, qv, bkt, bvv = views(st)
                s = k % 2
                psB, bB = self.PS[BA[k % 3]], self.bPS[BA[k % 3]]
                if st["first"]:
                    seg += 1
                psC, bC = self.PS[BC[seg % 2]], self.bPS[BC[seg % 2]]
                if st["first"] and st["G"] == 0:
                    load_v(st["eh"])
                S.emit("pe", ("matmul", dict(out=psB[:, c0:], lhsT=trin8, rhs=spb[s][:, c0:], start=False,
                                             stop=st["first"])),
                       reads=[self.b_cst, b_sp[s]], writes=[bB])
                if not st["first"]:
                    S.emit("pe", ("matmul", dict(out=psB[:, c0:], lhsT=neg8, rhs=sacc[:, c0:], start=False, stop=True)),
                           reads=[self.b_cst, b_sacc], writes=[bB])
                if k + 2 < ns:
                    stageA(k + 2)
                if k + 1 < ns:
                    stage1(k + 1)
                if st["first"]:
                    S.emit("dve", ("memset", dict(ap=sacc[:, :], constant=0.0)), writes=[b_sacc])
                    S.emit("pe", ("matmul", dict(out=psC[0:64, :], lhsT=ones64, rhs=zeros[:, :], start=True,
                                                 stop=False)),
                           reads=[self.b_cst, self.b_zeros], writes=[bC])
                if not st["last"]:
                    S.emit("dve", ("tensor_tensor", dict(out=sacc[:, c0:], in0=sacc[:, c0:], in1=spb[s][:, c0:],
                                                         op=ALU.add)),
                           reads=[b_sp[s]], writes=[b_sacc])
                S.emit("act", ("activation", dict(out=apb[s][:, c0:], in_=psB[:, c0:], func=AF.Exp, scale=0.125)),
                       reads=[bB], writes=[b_ap[s]])
                if st["diag"]:
                    S.emit("dve", ("tensor_tensor", dict(out=apb[s][:, c0:c0 + 128], in0=apb[s][:, c0:c0 + 128],
                                                         in1=mask, op=ALU.mult)),
                           reads=[self.b_cst], writes=[b_ap[s]])
                S.emit("pe", ("matmul", dict(out=psC[0:64, c0:], lhsT=Vh[:, st["kb"], :], rhs=apb[s][:, c0:],
                                             start=False, stop=st["last"])),
                       reads=[bvv, b_ap[s]], writes=[bC])
                if st["last"]:
                    G = st["G"]
                    cg0, _ = gcols(G)
                    if st["eh"] == 0:
                        S.emit("act", ("copy", dict(out=self.YT[0:64, c, cg0:cg0 + GN], in_=psC[0:64, :])),
                               reads=[bC], writes=[self.bY[(c, G)]])
                    else:
                        o = rr["o"] % 2
                        rr["o"] += 1
                        S.emit("act", ("copy", dict(out=obf[o][0:64, :], in_=psC[0:64, :])),
                               reads=[bC], writes=[b_obf[o]])
                        self.shift(obf[o][0:64, :], b_obf[o], self.YT[64:128, c, cg0:cg0 + GN], self.bY[(c, G)])

        for c in range(6):
            def ld(c=c):
                i = self.wslot()
                src = self.b_in[jb, :, c * 128:(c + 1) * 128].rearrange("(k p) f -> p k f", p=128)
                return i, self.wload(i, 0, 1, [128, KC, 128], src)

            def cp(slot, c=c):
                i, w = slot
                qi = rr["q"] % 2
                rr["q"] += 1
                for g in G4:
                    ps, bps = self.proj(w, self.bWR[i][0], g, banks=[7])
                    S.emit("act", ("copy", dict(out=Qc[qi][:, g * GN:(g + 1) * GN], in_=ps[:, :])),
                           reads=[bps], writes=[b_Q[qi]])
                op = S.emit("sp", ("dma_start", dict(out=KT[:, 0:T], in_=self.k_g[c // 2][(c % 2) * 128:(c % 2 + 1) * 128, :])),
                            reads=[self.b_kvready], writes=[b_KT[0]], dma=self.ch_kt[0])
                self.arena_dmas.append(op)
                op = S.emit("sp", ("dma_start", dict(out=KT[:, T:SEQ], in_=self.k_src[c // 2][(c % 2) * 128:(c % 2 + 1) * 128, :])),
                            reads=[self.b_kvready], writes=[b_KT[1]], dma=self.ch_kt[1])
                self.arena_dmas.append(op)
                attn_chunk(c, qi)
            self.add(ld, cp)
        self.w_o_units(l, G4)
        self.add(None, lambda _: self.barrier())

    def final(self):
        S = self.S
        ost = []
        off = 0
        for _ in range(2):
            a, off = self.carve(off, [128, KC, GN], F32)
            ost.append(a)
        bo = [[Buf() for _ in range(KC)] for _ in range(2)]
        outv = self.outT.rearrange("(k p) t -> p k t", p=128)

        def fn(_):
            for g in range(NG):
                c0, n = gcols(g)
                q = g % 2
                if self.debug_out == "x":
                    src = self.XT[:, :, c0:c0 + n]
                    rd = [self.bX[(k, g)] for k in range(KC)]
                else:
                    self.norm_group([self.XT[:, k, c0:c0 + n] for k in range(KC)],
                                    [self.bX[(k, g)] for k in range(KC)], 80,
                                    [ost[q][:, k, :] for k in range(KC)], bo[q], n)
                    src = ost[q][:, :, :]
                    rd = bo[q]
                op = S.emit("sp", ("dma_start", dict(out=outv[:, :, g * GN:(g + 1) * GN], in_=src)),
                            reads=rd, dma=self.ch_out)
                self.out_ops.append(op)
            S.wait_all("sp", list(self.out_ops))
        self.add(None, fn)

    def program(self):
        nc = self.build()
        S = self.S
        self.units = []
        allX = [self.bX[(k, g)] for k in range(KC) for g in range(-1, NG)]
        memx, _ = self.carve(9400, [128, KC, NMEM], F32)
        b_memx = [Buf() for _ in range(KC)]

        def start(_):
            S.emit("sp", ("dma_start", dict(out=self.small[:, :], in_=self.small_in)), writes=[self.b_small],
                   dma=self.chan("l0"))
            S.emit("pool", ("dma_start", dict(out=self.cst[:, :], in_=self.consts_in)), writes=[self.b_cst],
                   dma=self.chan("l1"))
            S.emit("sp", ("dma_start", dict(out=memx, in_=self.memT_in.rearrange("(k p) t -> p k t", p=128))),
                   writes=b_memx, dma=self.chan("l2"))
            S.emit("sp", ("dma_start", dict(out=self.XT[:, :, :],
                                               in_=self.xT_in.rearrange("(k p) t -> p k t", p=128))),
                   writes=allX, dma=self.chan("l3"))
            S.emit("dve", ("memset", dict(ap=self.zeros_bf[:, :], constant=0.0)), writes=[self.b_zeros])
            self.norm_group([memx[:, k, :] for k in range(KC)], b_memx, 72,
                            [self.memnT[:, k, :] for k in range(KC)], [self.b_memn] * KC, NMEM)
            self.barrier()
        self.add(None, start)
        if self.kv_only:
            self.kv_proj()
        for l in range(self.n_layers):
            if l < 2:
                self.mixer_a(l)
                self.ffn(l, with_halo=(l == 0))
                if l == 1:
                    self.add(None, lambda _: self.barrier())
                    self.kv_proj()
            else:
                self.mixer_b(l)
                self.ffn(l, with_halo=False)
            if l != 1:
                self.add(None, lambda _: self.barrier())
        self.final()
        self.run_units(prefetch=1)
        S.finalize(self.sem)
        with nc.Block() as block:
            @block.tensor
            def _(e):
                S.replay("pe", e)

            @block.scalar
            def _(e):
                S.replay("act", e)

            @block.vector
            def _(e):
                S.replay("dve", e)

            @block.gpsimd
            def _(e):
                S.replay("pool", e)

            @block.sync
            def _(e):
                S.replay("sp", e)
        return nc


def _host_inputs(inp):
    x = np.asarray(inp["x"], dtype=np.float32)
    mem = np.asarray(inp["mem"], dtype=np.float32)
    small = np.zeros((128, 128), np.float32)

    def put(col, vec):
        small[:, col:col + 8] = np.asarray(vec, np.float32).reshape(8, 128).T
    for i in range(4):
        put(8 * i, inp["mix_norm"][i])
        put(32 + 8 * i, inp["ffn_norm"][i])
    put(64, inp["kv_norm"])
    put(72, inp["mem_norm"])
    put(80, inp["final_norm"])
    cw = np.asarray(inp["conv_w"], np.float32)
    for l in range(2):
        for j in range(6):
            for k in range(3):
                small[:, 88 + (l * 6 + j) * 3 + k] = cw[l, k, j * 128:(j + 1) * 128]
    consts = np.zeros((128, 512), np.float32)
    s = np.arange(128)
    consts[:, 0:128] = (s[:, None] < s[None, :]).astype(np.float32)
    consts[:, 128:256] = -8.0 * (s[:, None] >= s[None, :]).astype(np.float32)
    consts[:, 256:384] = 1.0
    consts[:, 384:512] = -8.0
    shared = {k: np.ascontiguousarray(np.asarray(inp[k], np.float32)) for k in
              ("a_in", "b_in", "w_kv_shared", "w_mem_kv", "w_o", "w_gate", "w_up", "w_down")}
    maps = []
    for c in range(N_CORES):
        b, half = c // 2, c % 2
        xT = np.zeros((D, TT), np.float32)
        xT[:, HALO:] = x[b, half * T:(half + 1) * T, :].T
        if half == 1:
            xT[:, :HALO] = x[b, T - HALO:T, :].T
        sm = small.copy()
        sm[:, 124] = float(half)
        m = {"xT": xT, "memT": np.ascontiguousarray(mem[b].T), "small": sm, "consts": consts}
        m.update(shared)
        maps.append(m)
    return maps


_NC_CACHE = {}


def kernel(**inputs):
    key = "full"
    if key not in _NC_CACHE:
        _NC_CACHE[key] = Builder().program()
    nc = _NC_CACHE[key]
    maps = _host_inputs(inputs)
    res = run_bass_kernel_spmd(nc, maps, core_ids=list(range(N_CORES)))
    out = np.zeros((4, SEQ, D), np.float32)
    for c in range(N_CORES):
        b, half = c // 2, c % 2
        out[b, half * T:(half + 1) * T, :] = np.asarray(res.results[c]["outT"]).T
    return out
```

```python
import numpy as np
import concourse.bass as bass
import concourse.mybir as mybir
from concourse.bass_utils import run_bass_kernel_spmd

F32 = mybir.dt.float32
BF16 = mybir.dt.bfloat16
AF = mybir.ActivationFunctionType
ALU = mybir.AluOpType

D = 1024
KC = 8
T = 2048
HALO = 32
TT = T + HALO
NG = 4
GN = 512
SEQ = 4096
NMEM = 256
DFF = 2816
FC = 22
MAINW = 768
EPS = 1e-6
N_CORES = 8
SEM_EPOCH = 24000
N_FILL = 2


class Buf:
    __slots__ = ("w", "r", "name")

    def __init__(self, name=""):
        self.w = None
        self.r = {}
        self.name = name


class Op:
    __slots__ = ("eng", "fn", "deps", "sig", "sem", "val", "dma", "idx")

    def __init__(self, eng, fn, dma):
        self.eng = eng
        self.fn = fn
        self.deps = set()
        self.sig = False
        self.sem = None
        self.val = 0
        self.dma = dma
        self.idx = 0


class Sched:
    ENGS = ("pe", "act", "dve", "pool", "sp")

    def __init__(self, nc):
        self.nc = nc
        self.ops = {e: [] for e in self.ENGS}
        self.n = 0

    def emit(self, eng, fn, reads=(), writes=(), dma=None, force_sig=False):
        op = Op(eng, fn, dma)
        op.idx = self.n
        self.n += 1
        deps = op.deps
        for b in reads:
            if b.w is not None:
                deps.add(b.w)
        for b in writes:
            if b.w is not None:
                deps.add(b.w)
            for r in b.r.values():
                if isinstance(r, list):
                    deps.update(r)
                else:
                    deps.add(r)
        deps.discard(op)
        if eng == "pe" and dma is None:
            for d in [d for d in deps if d.eng == "pe" and d.dma is None]:
                deps.discard(d)
        for d in deps:
            d.sig = True
        for b in reads:
            if dma is not None:
                b.r.setdefault(("dma", eng), [])
                b.r[("dma", eng)].append(op)
            else:
                b.r[eng] = op
        for b in writes:
            b.w = op
            b.r = {}
        if dma is not None or force_sig:
            op.sig = True
        self.ops[eng].append(op)
        return op

    def wait_all(self, eng, ops):
        op = Op(eng, None, None)
        op.deps = set(ops)
        for d in ops:
            d.sig = True
        self.ops[eng].append(op)
        return op

    def finalize(self, semalloc):
        for e in self.ENGS:
            cnt = 0
            sem = None
            for op in self.ops[e]:
                if op.fn is None or not op.sig:
                    continue
                if op.dma is not None:
                    op.dma.count += 1
                    op.sem = op.dma.sem
                    op.val = 16 * op.dma.count
                else:
                    if sem is None or cnt >= SEM_EPOCH:
                        sem = semalloc("e_" + e)
                        cnt = 0
                    cnt += 1
                    op.sem = sem
                    op.val = cnt

    def replay(self, eng, handle):
        waited = {}
        for op in self.ops[eng]:
            need = {}
            for d in op.deps:
                k = id(d.sem)
                if k not in need or need[k][1] < d.val:
                    need[k] = (d.sem, d.val)
            for k, (sem, val) in need.items():
                if waited.get(k, 0) < val:
                    handle.wait_ge(sem, val)
                    waited[k] = val
            if op.fn is None:
                continue
            name, kw = op.fn
            pos = kw.pop("_pos", ())
            ins = getattr(handle, name)(*pos, **kw)
            if op.sig:
                ins.then_inc(op.sem, 16 if op.dma is not None else 1)


class DmaChan:
    def __init__(self, sem):
        self.sem = sem
        self.count = 0


def gcols(gi):
    if gi < 0:
        return 0, HALO
    return HALO + GN * gi, GN


class Builder:
    def __init__(self, n_layers=4, debug_out=None, kv_only=False, only_inputs=None):
        self.n_layers = n_layers
        self.debug_out = debug_out
        self.kv_only = kv_only
        self.only_inputs = only_inputs

    def sem(self, name):
        self._semn += 1
        s = self.nc.alloc_semaphore(f"{name}_{self._semn}")
        return s

    def chan(self, name):
        return DmaChan(self.sem("d_" + name))

    def build(self):
        nc = bass.Bass("TRN2", target_bir_lowering=False)
        self.nc = nc
        self._semn = 0
        S = Sched(nc)
        self.S = S
        dt0 = nc.dram_tensor

        class _Dummy:
            def ap(self):
                return None

        def dt(name, shape, dtype, kind="Internal"):
            if kind == "ExternalInput" and self.only_inputs is not None and name not in self.only_inputs:
                return _Dummy()
            return dt0(name, shape, dtype, kind=kind)
        self.xT_in = dt("xT", [D, TT], F32, kind="ExternalInput").ap()
        self.memT_in = dt("memT", [D, NMEM], F32, kind="ExternalInput").ap()
        self.small_in = dt("small", [128, 128], F32, kind="ExternalInput").ap()
        self.consts_in = dt("consts", [128, 512], F32, kind="ExternalInput").ap()
        self.a_in = dt("a_in", [2, D, 2560], F32, kind="ExternalInput").ap()
        self.b_in = dt("b_in", [2, D, 1024], F32, kind="ExternalInput").ap()
        self.w_kv = dt("w_kv_shared", [D, 1536], F32, kind="ExternalInput").ap()
        self.w_mem_kv = dt("w_mem_kv", [4, D, 512], F32, kind="ExternalInput").ap()
        self.w_o = dt("w_o", [4, D, D], F32, kind="ExternalInput").ap()
        self.w_gate = dt("w_gate", [4, D, DFF], F32, kind="ExternalInput").ap()
        self.w_up = dt("w_up", [4, D, DFF], F32, kind="ExternalInput").ap()
        self.w_down = dt("w_down", [4, DFF, D], F32, kind="ExternalInput").ap()
        self.outT = dt("outT", [D, T], F32, kind="ExternalOutput").ap()
        self.k_src = [dt(f"k_src{i}", [256, T], BF16) for i in range(3)]
        self.v_src = [dt(f"v_src{i}", [T, 256], BF16) for i in range(3)]
        self.k_g = [dt(f"k_g{i}", [512, T], BF16) for i in range(3)]
        self.v_g = [dt(f"v_g{i}", [2 * T, 256], BF16) for i in range(3)]

        st = nc.alloc_sbuf_tensor
        self.XT = st("XT", [128, KC, TT], F32)
        self.HT = st("HT", [128, KC, TT], BF16)
        self.YT = st("YT", [128, KC, TT], BF16)
        self.ARENA = st("ARENA", [128, 11776], F32)
        self.WR = [st(f"WR{i}", [128, 3072], BF16) for i in range(2)]
        self.small = st("small_sb", [128, 128], F32)
        self.cst = st("cst", [128, 512], BF16)
        self.memnT = st("memnT", [128, KC, NMEM], BF16)
        self.mKT = st("mKT", [128, 2, NMEM], BF16)
        self.mV = st("mV", [128, 2, 256], BF16)
        self.sq = [st(f"sq{i}", [128, GN], BF16) for i in range(2)]
        self.rstd = [st(f"rstd{i}", [128, GN], F32) for i in range(2)]
        self.PS = [nc.alloc_psum_tensor(f"ps{i}", [128, GN], F32) for i in range(8)]

        self.bPS = [Buf(f"ps{i}") for i in range(8)]
        self.ps_rr = 0
        self.bWR = [[Buf(f"wr{i}_{j}") for j in range(3)] for i in range(2)]
        self.wr_rr = 0
        self.wr_chan = [[self.chan(f"wr{i}_{j}") for j in range(3)] for i in range(2)]
        self.sg = [st(f"sg{i}", [128, GN], F32) for i in range(2)]
        self.bsg = [Buf() for _ in range(2)]
        self.sg_rr = 0
        self.zeros_bf = st("zeros_bf", [128, GN], BF16)
        self.b_zeros = Buf()
        self.bG = {(c, s): Buf() for c in range(FC) for s in range(3)}
        self.ch_v = [self.chan("v0"), self.chan("v1")]
        self.ch_kt = [self.chan("kt0"), self.chan("kt1")]
        self.arena_dmas = []
        self.ps_rrs = {}
        self.bX = {(k, g): Buf() for k in range(KC) for g in range(-1, NG)}
        self.bH = {(k, g): Buf() for k in range(KC) for g in range(-1, NG)}
        self.bY = {(k, g): Buf() for k in range(KC) for g in range(-1, NG)}
        self.bsq = [Buf() for _ in range(2)]
        self.brstd = [Buf() for _ in range(2)]
        self.sq_rr = 0
        self.rstd_rr = 0
        self.b_small = Buf()
        self.b_cst = Buf()
        self.b_memn = Buf()
        self.b_mK = Buf()
        self.b_mV = Buf()
        self.ch_misc = self.chan("misc")
        self.ch_shift = [self.chan(f"sh{i}") for i in range(4)]
        self.shift_rr = 0
        self.ch_out = self.chan("out")
        self.out_ops = []
        return nc

    def ps_next(self, banks=None):
        if banks is None:
            banks = list(range(8))
        key = tuple(banks)
        r = self.ps_rrs.get(key, 0)
        self.ps_rrs[key] = r + 1
        i = banks[r % len(banks)]
        return self.PS[i], self.bPS[i]

    def barrier(self):
        S = self.S
        last = []
        for e in ("pe", "act", "dve"):
            for op in reversed(S.ops[e]):
                if op.fn is not None:
                    last.append(op)
                    break
        dm = list(self.arena_dmas)
        self.arena_dmas = []
        for e in ("pe", "act", "dve", "sp"):
            S.wait_all(e, [o for o in last if o.eng != e] + dm)

    def carve(self, off, shape, dtype):
        n = 1
        for s in shape[1:]:
            n *= s
        words = n if dtype == F32 else (n + 1) // 2
        ap = self.ARENA[:, off:off + words]
        if dtype == BF16:
            ap = ap.bitcast(BF16)
        if len(shape) == 3:
            ap = ap.rearrange("p (a b) -> p a b", a=shape[1])
        return ap, off + words

    def add(self, load, compute):
        self.units.append((load, compute))

    def run_units(self, prefetch=1):
        pend = []
        n = len(self.units)
        for i in range(n + prefetch):
            if i < n:
                ld = self.units[i][0]
                pend.append(ld() if ld is not None else None)
            if i >= prefetch:
                self.units[i - prefetch][1](pend[i - prefetch])

    def wslot(self):
        i = self.wr_rr % 2
        self.wr_rr += 1
        return i

    def wload(self, i, seg0, nseg, shape, src):
        n = shape[1] * shape[2]
        dst = self.WR[i][:, seg0 * 1024: seg0 * 1024 + n].rearrange("p (a b) -> p a b", a=shape[1])
        bufs = [self.bWR[i][s] for s in range(seg0, seg0 + nseg)]
        self.S.emit("pool", ("dma_start", dict(out=dst, in_=src)),
                    writes=bufs, dma=self.wr_chan[i][seg0])
        return dst

    def norm_group(self, xs, bxs, gcol, outs, bouts, n):
        S = self.S
        ps, bps = self.ps_next()
        ones = self.cst[:, 256:384]
        for k in range(KC):
            i = self.sq_rr % 2
            self.sq_rr += 1
            sq, bsq = self.sq[i], self.bsq[i]
            S.emit("dve", ("tensor_tensor", dict(out=sq[:, :n], in0=xs[k], in1=xs[k], op=ALU.mult)),
                   reads=[bxs[k]], writes=[bsq])
            S.emit("pe", ("matmul", dict(out=ps[:, :n], lhsT=ones, rhs=sq[:, :n],
                                                         start=(k == 0), stop=(k == KC - 1))),
                   reads=[bsq, self.b_cst], writes=[bps])
        i = self.rstd_rr % 2
        self.rstd_rr += 1
        rstd, brstd = self.rstd[i], self.brstd[i]
        S.emit("act", ("activation", dict(out=rstd[:, :n], in_=ps[:, :n], func=AF.Ln, bias=EPS, scale=1.0 / D)),
               reads=[bps], writes=[brstd])
        S.emit("act", ("activation", dict(out=rstd[:, :n], in_=rstd[:, :n], func=AF.Exp, scale=-0.5)),
               reads=[brstd], writes=[brstd])
        for k in range(KC):
            g = self.small[:, gcol + k: gcol + k + 1]
            S.emit("dve", ("scalar_tensor_tensor", dict(out=outs[k], in0=xs[k], scalar=g,
                                                                     in1=rstd[:, :n], op0=ALU.mult, op1=ALU.mult)),
                   reads=[bxs[k], brstd, self.b_small], writes=[bouts[k]])

    def norm_tokens(self, groups, gcol):
        for gi in groups:
            c0, n = gcols(gi)
            self.norm_group([self.XT[:, k, c0:c0 + n] for k in range(KC)], [self.bX[(k, gi)] for k in range(KC)],
                            gcol, [self.HT[:, k, c0:c0 + n] for k in range(KC)],
                            [self.bH[(k, gi)] for k in range(KC)], n)

    def proj(self, wview, bw, gi, banks=None):
        c0, n = gcols(gi)
        ps, bps = self.ps_next(banks)
        for k in range(KC):
            self.S.emit("pe", ("matmul", dict(out=ps[:, :n], lhsT=wview[:, k, :], rhs=self.HT[:, k, c0:c0 + n],
                                                       start=(k == 0), stop=(k == KC - 1))),
                        reads=[bw, self.bH[(k, gi)]], writes=[bps])
        return ps, bps

    def mem_kv(self, l):
        S = self.S

        def ld(part):
            def f():
                i = self.wslot()
                src = self.w_mem_kv[l, :, part * 256:(part + 1) * 256].rearrange("(k p) f -> p k f", p=128)
                return i, self.wload(i, 0, 2, [128, KC, 256], src)
            return f

        def ck(slot):
            i, w = slot
            for c in range(2):
                ps, bps = self.ps_next()
                for k in range(KC):
                    S.emit("pe", ("matmul", dict(out=ps[:, :NMEM], lhsT=w[:, k, c * 128:(c + 1) * 128],
                                                                     rhs=self.memnT[:, k, :], start=(k == 0),
                                                                     stop=(k == KC - 1))),
                           reads=[self.bWR[i][0], self.bWR[i][1], self.b_memn], writes=[bps])
                S.emit("act", ("copy", dict(out=self.mKT[:, c, :], in_=ps[:, :NMEM])),
                       reads=[bps], writes=[self.b_mK])

        def cv(slot):
            i, w = slot
            for mc in range(2):
                ps, bps = self.ps_next()
                for k in range(KC):
                    S.emit("pe", ("matmul", dict(out=ps[:, :256],
                                                                       lhsT=self.memnT[:, k, mc * 128:(mc + 1) * 128],
                                                                       rhs=w[:, k, :], start=(k == 0),
                                                                       stop=(k == KC - 1))),
                           reads=[self.bWR[i][0], self.bWR[i][1], self.b_memn], writes=[bps])
                S.emit("act", ("copy", dict(out=self.mV[:, mc, :], in_=ps[:, :256])),
                       reads=[bps], writes=[self.b_mV])

        self.add(ld(0), ck)
        self.add(ld(1), cv)

    def mem_attn(self, groups, qmT, b_qm, PT, bPT, rden, brden, otmp, botmp):
        S = self.S
        ones64 = self.cst[:, 256:320]
        rr = 0
        for gi in groups:
            c0, n = gcols(gi)
            for hm in range(4):
                c, off = hm // 2, (hm % 2) * 64
                psO, bO = self.ps_next()
                psD, bD = self.ps_next()
                for mc in range(2):
                    psS, bS = self.ps_next()
                    S.emit("pe", ("matmul", dict(out=psS[:, :n], lhsT=self.mKT[off:off + 64, c, mc * 128:(mc + 1) * 128],
                        rhs=qmT[off:off + 64, c, c0:c0 + n], start=True, stop=True)),
                        reads=[self.b_mK, b_qm[(c, gi)]], writes=[bS])
                    j = rr % 2
                    rr += 1
                    S.emit("act", ("activation", dict(out=PT[j][:, :n], in_=psS[:, :n], func=AF.Exp,
                                                                       scale=0.125)),
                           reads=[bS], writes=[bPT[j]])
                    S.emit("pe", ("matmul", dict(out=psO[0:64, :n], lhsT=self.mV[:, mc, hm * 64:(hm + 1) * 64], rhs=PT[j][:, :n],
                        start=(mc == 0), stop=(mc == 1))), reads=[self.b_mV, bPT[j]], writes=[bO])
                    S.emit("pe", ("matmul", dict(out=psD[0:64, :n], lhsT=ones64, rhs=PT[j][:, :n], start=(mc == 0), stop=(mc == 1))),
                        reads=[self.b_cst, bPT[j]], writes=[bD])
                j = rr % 2
                S.emit("dve", ("reciprocal", dict(out=rden[j][0:64, :n], in_=psD[0:64, :n])),
                       reads=[bD], writes=[brden[j]])
                if off == 0:
                    S.emit("dve", ("tensor_tensor", dict(
                        out=self.YT[0:64, 6 + c, c0:c0 + n], in0=psO[0:64, :n], in1=rden[j][0:64, :n], op=ALU.mult)),
                        reads=[bO, brden[j]], writes=[self.bY[(6 + c, gi)]])
                else:
                    S.emit("dve", ("tensor_tensor", dict(
                        out=otmp[j][0:64, :n], in0=psO[0:64, :n], in1=rden[j][0:64, :n], op=ALU.mult)),
                        reads=[bO, brden[j]], writes=[botmp[j]])
                    self.shift(otmp[j][0:64, :n], botmp[j], self.YT[64:128, 6 + c, c0:c0 + n], self.bY[(6 + c, gi)])

    def shift(self, src, bsrc, dst, bdst):
        ch = self.ch_shift[self.shift_rr % len(self.ch_shift)]
        self.shift_rr += 1
        op = self.S.emit("sp", ("dma_start", dict(out=dst, in_=src)), reads=[bsrc], writes=[bdst], dma=ch)
        self.arena_dmas.append(op)

    def w_o_units(self, l, groups):
        S = self.S
        for m in range(KC):
            def ld(m=m):
                i = self.wslot()
                src = self.w_o[l, :, m * 128:(m + 1) * 128].rearrange("(k p) f -> p k f", p=128)
                return i, self.wload(i, 0, 1, [128, KC, 128], src)

            def cp(slot, m=m):
                i, w = slot
                for gi in groups:
                    c0, n = gcols(gi)
                    ps, bps = self.ps_next()
                    for c in range(KC):
                        S.emit("pe", ("matmul", dict(out=ps[:, :n], lhsT=w[:, c, :],
                                                                    rhs=self.YT[:, c, c0:c0 + n], start=(c == 0),
                                                                    stop=(c == KC - 1))),
                               reads=[self.bWR[i][0], self.bY[(c, gi)]], writes=[bps])
                    S.emit("dve", ("tensor_tensor", dict(out=self.XT[:, m, c0:c0 + n],
                                                                   in0=ps[:, :n], in1=self.XT[:, m, c0:c0 + n],
                                                                   op=ALU.add)),
                           reads=[bps, self.bX[(m, gi)]], writes=[self.bX[(m, gi)]])
            self.add(ld, cp)

    def ffn(self, l, with_halo):
        S = self.S
        GTW = 2 * GN + HALO
        GT, _ = self.carve(0, [128, FC, GTW], BF16)
        for sb in range(2):
            groups = [2 * sb, 2 * sb + 1]
            if with_halo and sb == 0:
                groups = [-1] + groups

            def lcol(gi, sb=sb):
                return 2 * GN if gi < 0 else (gi - 2 * sb) * GN

            def slot_of(gi, sb=sb):
                return 2 if gi < 0 else gi - 2 * sb

            self.add(None, lambda _, groups=groups: self.norm_tokens(groups, 32 + 8 * l))
            for c in range(FC):
                def ld(c=c):
                    i = self.wslot()
                    sg = self.w_gate[l, :, c * 128:(c + 1) * 128].rearrange("(k p) f -> p k f", p=128)
                    su = self.w_up[l, :, c * 128:(c + 1) * 128].rearrange("(k p) f -> p k f", p=128)
                    return i, self.wload(i, 0, 1, [128, KC, 128], sg), self.wload(i, 1, 1, [128, KC, 128], su)

                def cp(slot, c=c, groups=groups, lcol=lcol, slot_of=slot_of):
                    i, wg, wu = slot
                    for gi in groups:
                        c0, n = gcols(gi)
                        psg, bg = self.proj(wg, self.bWR[i][0], gi)
                        psu, bu = self.proj(wu, self.bWR[i][1], gi)
                        j = self.sg_rr % 2
                        self.sg_rr += 1
                        sgt, bsg = self.sg[j], self.bsg[j]
                        S.emit("act", ("activation", dict(out=sgt[:, :n], in_=psg[:, :n],
                                                                               func=AF.Silu)),
                               reads=[bg], writes=[bsg])
                        lc = lcol(gi)
                        S.emit("dve", ("tensor_tensor", dict(
                            out=GT[:, c, lc:lc + n], in0=psu[:, :n], in1=sgt[:, :n], op=ALU.mult)),
                            reads=[bu, bsg], writes=[self.bG[(c, slot_of(gi))]])
                self.add(ld, cp)
            for m in range(KC):
                def ld(m=m):
                    i = self.wslot()
                    src = self.w_down[l, :, m * 128:(m + 1) * 128].rearrange("(c p) f -> p c f", p=128)
                    return i, self.wload(i, 0, 3, [128, FC, 128], src)

                def cp(slot, m=m, groups=groups, lcol=lcol, slot_of=slot_of):
                    i, w = slot
                    for gi in groups:
                        c0, n = gcols(gi)
                        lc = lcol(gi)
                        ps, bps = self.ps_next()
                        for c in range(FC):
                            S.emit("pe", ("matmul", dict(out=ps[:, :n], lhsT=w[:, c, :],
                                                                               rhs=GT[:, c, lc:lc + n],
                                                                               start=(c == 0), stop=(c == FC - 1))),
                                   reads=[self.bWR[i][0], self.bG[(c, slot_of(gi))]], writes=[bps])
                        S.emit("dve", ("tensor_tensor", dict(out=self.XT[:, m, c0:c0 + n],
                                                                       in0=ps[:, :n], in1=self.XT[:, m, c0:c0 + n],
                                                                       op=ALU.add)),
                               reads=[bps, self.bX[(m, gi)]], writes=[self.bX[(m, gi)]])
                self.add(ld, cp)

    def mixer_a(self, l):
        S = self.S
        halo_full = (l == 0)
        allg = [-1, 0, 1, 2, 3]
        outg = allg if halo_full else [0, 1, 2, 3]
        off = 0
        vbuf, off = self.carve(off, [128, TT + 4], F32)
        qmT, off = self.carve(off, [128, 2, TT], BF16)
        cgs, c1, c2 = [], [], []
        for lst in (cgs, c1, c2):
            for _ in range(2):
                a, off = self.carve(off, [128, GN], F32)
                lst.append(a)
        PT, rden, otmp = [], [], []
        for _ in range(2):
            a, off = self.carve(off, [128, GN], BF16)
            PT.append(a)
        for _ in range(2):
            a, off = self.carve(off, [128, GN], F32)
            rden.append(a)
        for _ in range(2):
            a, off = self.carve(off, [128, GN], BF16)
            otmp.append(a)
        bv = {g: Buf() for g in allg}
        bcgs = [Buf(), Buf()]
        bc1 = [Buf(), Buf()]
        bc2 = [Buf(), Buf()]
        bPT = [Buf(), Buf()]
        brden = [Buf(), Buf()]
        botmp = [Buf(), Buf()]
        b_qm = {(c, g): Buf() for c in range(2) for g in allg}
        flag = self.small[:, 124:125]
        rr = [0]

        def pre(_):
            self.norm_tokens(allg, 8 * l)
            S.emit("dve", ("memset", dict(ap=vbuf[:, 0:2], constant=0.0)), writes=[bv[-1]])
        self.add(None, pre)
        self.mem_kv(l)

        for j in range(6):
            def ld(j=j):
                i = self.wslot()
                views = []
                for s in range(3):
                    src = self.a_in[l, :, s * MAINW + j * 128: s * MAINW + (j + 1) * 128].rearrange(
                        "(k p) f -> p k f", p=128)
                    views.append(self.wload(i, s, 1, [128, KC, 128], src))
                return i, views

            def cp(slot, j=j):
                i, (wb, wc, wu) = slot
                for gi in allg:
                    c0, n = gcols(gi)
                    need_y = gi in outg
                    psc, bpc = self.proj(wc, self.bWR[i][1], gi)
                    psu, bpu = self.proj(wu, self.bWR[i][2], gi)
                    if need_y:
                        psb, bpb = self.proj(wb, self.bWR[i][0], gi)
                    q = rr[0] % 2
                    rr[0] += 1
                    S.emit("act", ("copy", dict(out=cgs[q][:, :n], in_=psc[:, :n])),
                           reads=[bpc], writes=[bcgs[q]])
                    S.emit("dve", ("tensor_tensor", dict(out=vbuf[:, 2 + c0:2 + c0 + n],
                                                                          in0=psu[:, :n], in1=cgs[q][:, :n],
                                                                          op=ALU.mult)),
                           reads=[bpu, bcgs[q]], writes=[bv[gi]])
                    if gi < 0:
                        S.emit("dve", ("tensor_scalar", dict(out=vbuf[:, 2:2 + HALO], in0=vbuf[:, 2:2 + HALO],
                                                                scalar1=flag, scalar2=None, op0=ALU.mult)),
                               reads=[self.b_small], writes=[bv[gi]])
                    if not need_y:
                        continue
                    wcol = 88 + (l * 6 + j) * 3
                    w0 = self.small[:, wcol:wcol + 1]
                    w1 = self.small[:, wcol + 1:wcol + 2]
                    w2 = self.small[:, wcol + 2:wcol + 3]
                    prev = [bv[gi - 1]] if gi >= 0 else []
                    S.emit("act", ("mul", dict(out=c1[q][:, :n], in_=vbuf[:, 2 + c0:2 + c0 + n], mul=w2)),
                           reads=[bv[gi], self.b_small], writes=[bc1[q]])
                    S.emit("dve", ("scalar_tensor_tensor", dict(
                        out=c2[q][:, :n], in0=vbuf[:, 1 + c0:1 + c0 + n], scalar=w1, in1=c1[q][:, :n],
                        op0=ALU.mult, op1=ALU.add)), reads=[bv[gi], bc1[q], self.b_small] + prev, writes=[bc2[q]])
                    S.emit("dve", ("scalar_tensor_tensor", dict(
                        out=c1[q][:, :n], in0=vbuf[:, c0:c0 + n], scalar=w0, in1=c2[q][:, :n],
                        op0=ALU.mult, op1=ALU.add)), reads=[bv[gi], bc2[q], self.b_small] + prev, writes=[bc1[q]])
                    S.emit("dve", ("tensor_tensor", dict(out=self.YT[:, j, c0:c0 + n],
                                                                          in0=psb[:, :n], in1=c1[q][:, :n],
                                                                          op=ALU.mult)),
                           reads=[bpb, bc1[q]], writes=[self.bY[(j, gi)]])
            self.add(ld, cp)

        def ldq():
            i = self.wslot()
            src = self.a_in[l, :, 3 * MAINW:3 * MAINW + 256].rearrange("(k p) f -> p k f", p=128)
            return i, self.wload(i, 0, 2, [128, KC, 256], src)

        def cpq(slot):
            i, w = slot
            for gi in outg:
                c0, n = gcols(gi)
                for c in range(2):
                    ps, bps = self.ps_next()
                    for k in range(KC):
                        S.emit("pe", ("matmul", dict(out=ps[:, :n], lhsT=w[:, k, c * 128:(c + 1) * 128],
                                                                         rhs=self.HT[:, k, c0:c0 + n], start=(k == 0),
                                                                         stop=(k == KC - 1))),
                               reads=[self.bWR[i][0], self.bWR[i][1], self.bH[(k, gi)]], writes=[bps])
                    S.emit("act", ("copy", dict(out=qmT[:, c, c0:c0 + n], in_=ps[:, :n])),
                           reads=[bps], writes=[b_qm[(c, gi)]])
            self.mem_attn(outg, qmT, b_qm, PT, bPT, rden, brden, otmp, botmp)
        self.add(ldq, cpq)
        self.w_o_units(l, outg)
        self.add(None, lambda _: self.barrier())

    def kv_proj(self):
        S = self.S
        off = 0
        kst, vst = [], []
        for _ in range(2):
            a, off = self.carve(off, [128, GN], BF16)
            kst.append(a)
        for _ in range(2):
            a, off = self.carve(off, [128, 256], BF16)
            vst.append(a)
        bk = [Buf(), Buf()]
        bvs = [Buf(), Buf()]
        ch_st = self.chan("st")
        b_gate = Buf()
        self.b_kvready = Buf()
        rr = [0]
        self.add(None, lambda _: self.norm_tokens([0, 1, 2, 3], 64))
        for c in range(6):
            def ld(c=c):
                i = self.wslot()
                src = self.w_kv[:, c * 128:(c + 1) * 128].rearrange("(k p) f -> p k f", p=128)
                return i, self.wload(i, 0, 1, [128, KC, 128], src)

            def cp(slot, c=c):
                i, w = slot
                for g in range(NG):
                    ps, bps = self.proj(w, self.bWR[i][0], g)
                    q = rr[0] % 2
                    rr[0] += 1
                    S.emit("act", ("copy", dict(out=kst[q][:, :], in_=ps[:, :])),
                           reads=[bps], writes=[bk[q]])
                    dst = self.k_src[c // 2][(c % 2) * 128:(c % 2 + 1) * 128, g * GN:(g + 1) * GN]
                    op = S.emit("sp", ("dma_start", dict(out=dst, in_=kst[q][:, :])),
                                reads=[bk[q], b_gate], dma=ch_st)
                    self.arena_dmas.append(op)
            self.add(ld, cp)
        for hv in range(3):
            def ld(hv=hv):
                i = self.wslot()
                src = self.w_kv[:, MAINW + hv * 256: MAINW + (hv + 1) * 256].rearrange("(k p) f -> p k f", p=128)
                return i, self.wload(i, 0, 2, [128, KC, 256], src)

            def cp(slot, hv=hv):
                i, w = slot
                for tt in range(16):
                    g = tt // 4
                    c0 = HALO + tt * 128
                    ps, bps = self.ps_next()
                    for k in range(KC):
                        S.emit("pe", ("matmul", dict(out=ps[:, :256], lhsT=self.HT[:, k, c0:c0 + 128],
                                                                           rhs=w[:, k, :], start=(k == 0),
                                                                           stop=(k == KC - 1))),
                               reads=[self.bWR[i][0], self.bWR[i][1], self.bH[(k, g)]], writes=[bps])
                    q = rr[0] % 2
                    rr[0] += 1
                    S.emit("act", ("copy", dict(out=vst[q][:, :], in_=ps[:, :256])),
                           reads=[bps], writes=[bvs[q]])
                    dst = self.v_src[hv][tt * 128:(tt + 1) * 128, :]
                    op = S.emit("sp", ("dma_start", dict(out=dst, in_=vst[q][:, :])),
                                reads=[bvs[q], b_gate], dma=ch_st)
                    self.arena_dmas.append(op)
            self.add(ld, cp)

        def coll(_):
            rg = [[0, 1], [2, 3], [4, 5], [6, 7]]
            srcs = self.k_src + self.v_src
            dsts = self.k_g + self.v_g
            for ci in range(6):
                S.emit("pool", ("collective_compute", dict(_pos=("AllGather", ALU.bypass), replica_groups=rg,
                                                           ins=[srcs[ci].ap().opt()], outs=[dsts[ci].ap().opt()])),
                       reads=([b_gate] if ci else []),
                       writes=([b_gate] if ci == 0 else []) + ([self.b_kvready] if ci == 5 else []), force_sig=True)
            self.barrier()
        self.add(None, coll)

    def mixer_b(self, l):
        S = self.S
        jb = l - 2
        G4 = [0, 1, 2, 3]
        off = 0
        Qc = []
        for _ in range(2):
            a, off = self.carve(off, [128, T], BF16)
            Qc.append(a)
        qmT, off = self.carve(off, [128, 2, TT], BF16)
        KT, off = self.carve(off, [128, SEQ], BF16)
        Vh, off = self.carve(off, [128, 32, 64], BF16)
        ebuf, off = self.carve(off, [128, GN], F32)
        spb, apb, scl, obf = [], [], [], []
        for _ in range(2):
            a, off = self.carve(off, [128, GN], BF16)
            spb.append(a)
        for _ in range(2):
            a, off = self.carve(off, [128, GN], BF16)
            apb.append(a)
        for _ in range(2):
            a, off = self.carve(off, [128, GN], F32)
            scl.append(a)
        sacc, off = self.carve(off, [128, GN], BF16)
        for _ in range(2):
            a, off = self.carve(off, [128, GN], BF16)
            obf.append(a)
        assert off <= 11776, off
        b_Q = [Buf(), Buf()]
        b_qm = {(c, g): Buf() for c in range(2) for g in G4}
        b_KT = [Buf(), Buf()]
        b_V = [Buf(), Buf()]
        b_e = Buf()
        b_sp = [Buf(), Buf()]
        b_ap = [Buf(), Buf()]
        b_scl = [Buf(), Buf()]
        b_sacc = Buf()
        b_obf = [Buf(), Buf()]
        flag = self.small[:, 124:125]
        mask = self.cst[:, 0:128]
        trin8 = self.cst[:, 128:256]
        ones64 = self.cst[:, 256:320]
        neg8 = self.cst[:, 384:512]
        ones128 = self.cst[:, 256:384]
        zeros = self.zeros_bf
        BA, BC = [0, 1, 2], [4, 5]
        rr = {"q": 0, "s": 0, "o": 0}

        self.add(None, lambda _: self.norm_tokens(G4, 8 * l))
        self.mem_kv(l)

        def ldq():
            i = self.wslot()
            src = self.b_in[jb, :, MAINW:MAINW + 256].rearrange("(k p) f -> p k f", p=128)
            return i, self.wload(i, 0, 2, [128, KC, 256], src)

        def cpq(slot):
            i, w = slot
            for gi in G4:
                c0, n = gcols(gi)
                for c in range(2):
                    ps, bps = self.ps_next([7])
                    for k in range(KC):
                        S.emit("pe", ("matmul", dict(out=ps[:, :n], lhsT=w[:, k, c * 128:(c + 1) * 128],
                                                                         rhs=self.HT[:, k, c0:c0 + n], start=(k == 0),
                                                                         stop=(k == KC - 1))),
                               reads=[self.bWR[i][0], self.bWR[i][1], self.bH[(k, gi)]], writes=[bps])
                    S.emit("act", ("copy", dict(out=qmT[:, c, c0:c0 + n], in_=ps[:, :n])),
                           reads=[bps], writes=[b_qm[(c, gi)]])
            self.mem_attn(G4, qmT, b_qm, apb, b_ap, scl, b_scl, obf, b_obf)
        self.add(ldq, cpq)

        def attn_chunk(c, qi):
            steps = []
            for eh in range(2):
                for G in range(NG):
                    blocks = [(16 + ob, (ob - 4 * G) * 128 if ob >= 4 * G else 0, ob >= 4 * G)
                              for ob in range(4 * G + 3, -1, -1)]
                    blocks += [(pb, 0, False) for pb in range(15, -1, -1)]
                    for bi, (kb, c0, diag) in enumerate(blocks):
                        steps.append(dict(eh=eh, G=G, kb=kb, c0=c0, diag=diag, first=(bi == 0),
                                          last=(bi == len(blocks) - 1)))
            ns = len(steps)

            def views(st):
                po = 64 * st["eh"]
                c0 = st["c0"]
                n = GN - c0
                q0 = st["G"] * GN + c0
                ktb = KT[po:po + 64, st["kb"] * 128:(st["kb"] + 1) * 128]
                qv = Qc[qi][po:po + 64, q0:q0 + n]
                bkt = b_KT[0] if st["kb"] < 16 else b_KT[1]
                bvv = b_V[0] if st["kb"] < 16 else b_V[1]
                return c0, ktb, qv, bkt, bvv

            def stageA(k):
                st = steps[k]
                c0, ktb, qv, bkt, bvv = views(st)
                psA, bA = self.PS[BA[k % 3]], self.bPS[BA[k % 3]]
                S.emit("pe", ("matmul", dict(out=psA[:, c0:], lhsT=ktb, rhs=qv, start=True, stop=False)),
                       reads=[bkt, b_Q[qi]], writes=[bA])

            def stage1(k):
                st = steps[k]
                c0 = st["c0"]
                s = k % 2
                psA, bA = self.PS[BA[k % 3]], self.bPS[BA[k % 3]]
                S.emit("act", ("activation", dict(out=ebuf[:, c0:], in_=psA[:, c0:], func=AF.Exp, scale=0.125)),
                       reads=[bA], writes=[b_e])
                S.emit("act", ("activation", dict(out=spb[s][:, c0:], in_=ebuf[:, c0:], func=AF.Ln, bias=1.0,
                                                  scale=1.0)), reads=[b_e], writes=[b_sp[s]])
                if st["diag"]:
                    S.emit("dve", ("tensor_tensor", dict(out=spb[s][:, c0:c0 + 128], in0=spb[s][:, c0:c0 + 128],
                                                         in1=mask, op=ALU.mult)),
                           reads=[self.b_cst], writes=[b_sp[s]])

            def load_v(eh):
                h = 2 * c + eh
                srcp = self.v_g[h // 4][0:T, (h % 4) * 64:(h % 4 + 1) * 64].rearrange("(b p) d -> p b d", p=128)
                srco = self.v_src[h // 4][:, (h % 4) * 64:(h % 4 + 1) * 64].rearrange("(b p) d -> p b d", p=128)
                op = S.emit("sp", ("dma_start", dict(out=Vh[:, 0:16, :], in_=srcp)), reads=[self.b_kvready],
                            writes=[b_V[0]], dma=self.ch_v[0])
                self.arena_dmas.append(op)
                op = S.emit("sp", ("dma_start", dict(out=Vh[:, 16:32, :], in_=srco)), reads=[self.b_kvready],
                            writes=[b_V[1]], dma=self.ch_v[1])
                self.arena_dmas.append(op)
                S.emit("dve", ("tensor_scalar", dict(out=Vh[:, 0:16, :], in0=Vh[:, 0:16, :], scalar1=flag,
                                                     scalar2=None, op0=ALU.mult)),
                       reads=[self.b_small], writes=[b_V[0]])

            stageA(0)
            if ns > 1:
                stageA(1)
            stage1(0)
            seg = -1
            for k in range(ns):
                st = steps[k]
                c0, ktb, qv, bkt, bvv = views(st)
                s = k % 2
                psB, bB = self.PS[BA[k % 3]], self.bPS[BA[k % 3]]
                if st["first"]:
                    seg += 1
                psC, bC = self.PS[BC[seg % 2]], self.bPS[BC[seg % 2]]
                if st["first"] and st["G"] == 0:
                    load_v(st["eh"])
                S.emit("pe", ("matmul", dict(out=psB[:, c0:], lhsT=trin8, rhs=spb[s][:, c0:], start=False,
                                             stop=st["first"])),
                       reads=[self.b_cst, b_sp[s]], writes=[bB])
                if not st["first"]:
                    S.emit("pe", ("matmul", dict(out=psB[:, c0:], lhsT=neg8, rhs=sacc[:, c0:], start=False, stop=True)),
                           reads=[self.b_cst, b_sacc], writes=[bB])
                if k + 2 < ns:
                    stageA(k + 2)
                for _f in range(N_FILL):
                    S.emit("pe", ("matmul", dict(out=self.PS[6][:, :], lhsT=ones128, rhs=self.HT[:, _f, 0:GN],
                                                 start=True, stop=True)), writes=[self.bPS[6]])
                if k + 1 < ns:
                    stage1(k + 1)
                if st["first"]:
                    S.emit("dve", ("memset", dict(ap=sacc[:, :], constant=0.0)), writes=[b_sacc])
                    S.emit("pe", ("matmul", dict(out=psC[0:64, :], lhsT=ones64, rhs=zeros[:, :], start=True,
                                                 stop=False)),
                           reads=[self.b_cst, self.b_zeros], writes=[bC])
                if not st["last"]:
                    S.emit("dve", ("tensor_tensor", dict(out=sacc[:, c0:], in0=sacc[:, c0:], in1=spb[s][:, c0:],
                                                         op=ALU.add)),
                           reads=[b_sp[s]], writes=[b_sacc])
                S.emit("act", ("activation", dict(out=apb[s][:, c0:], in_=psB[:, c0:], func=AF.Exp, scale=0.125)),
                       reads=[bB], writes=[b_ap[s]])
                if st["diag"]:
                    S.emit("dve", ("tensor_tensor", dict(out=apb[s][:, c0:c0 + 128], in0=apb[s][:, c0:c0 + 128],
                                                         in1=mask, op=ALU.mult)),
                           reads=[self.b_cst], writes=[b_ap[s]])
                S.emit("pe", ("matmul", dict(out=psC[0:64, c0:], lhsT=Vh[:, st["kb"], :], rhs=apb[s][:, c0:],
                                             start=False, stop=st["last"])),
                       reads=[bvv, b_ap[s]], writes=[bC])
                if st["last"]:
                    G = st["G"]
                    cg0, _ = gcols(G)
                    if st["eh"] == 0:
                        S.emit("act", ("copy", dict(out=self.YT[0:64, c, cg0:cg0 + GN], in_=psC[0:64, :])),
                               reads=[bC], writes=[self.bY[(c, G)]])
                    else:
                        o = rr["o"] % 2
                        rr["o"] += 1
                        S.emit("act", ("copy", dict(out=obf[o][0:64, :], in_=psC[0:64, :])),
                               reads=[bC], writes=[b_obf[o]])
                        self.shift(obf[o][0:64, :], b_obf[o], self.YT[64:128, c, cg0:cg0 + GN], self.bY[(c, G)])

        for c in range(6):
            def ld(c=c):
                i = self.wslot()
                src = self.b_in[jb, :, c * 128:(c + 1) * 128].rearrange("(k p) f -> p k f", p=128)
                return i, self.wload(i, 0, 1, [128, KC, 128], src)

            def cp(slot, c=c):
                i, w = slot
                qi = rr["q"] % 2
                rr["q"] += 1
                for g in G4:
                    ps, bps = self.proj(w, self.bWR[i][0], g, banks=[7])
                    S.emit("act", ("copy", dict(out=Qc[qi][:, g * GN:(g + 1) * GN], in_=ps[:, :])),
                           reads=[bps], writes=[b_Q[qi]])
                op = S.emit("sp", ("dma_start", dict(out=KT[:, 0:T], in_=self.k_g[c // 2][(c % 2) * 128:(c % 2 + 1) * 128, :])),
                            reads=[self.b_kvready], writes=[b_KT[0]], dma=self.ch_kt[0])
                self.arena_dmas.append(op)
                op = S.emit("sp", ("dma_start", dict(out=KT[:, T:SEQ], in_=self.k_src[c // 2][(c % 2) * 128:(c % 2 + 1) * 128, :])),
                            reads=[self.b_kvready], writes=[b_KT[1]], dma=self.ch_kt[1])
                self.arena_dmas.append(op)
                attn_chunk(c, qi)
            self.add(ld, cp)
        self.w_o_units(l, G4)
        self.add(None, lambda _: self.barrier())

    def final(self):
        S = self.S
        ost = []
        off = 0
        for _ in range(2):
            a, off = self.carve(off, [128, KC, GN], F32)
            ost.append(a)
        bo = [[Buf() for _ in range(KC)] for _ in range(2)]
        outv = self.outT.rearrange("(k p) t -> p k t", p=128)

        def fn(_):
            for g in range(NG):
                c0, n = gcols(g)
                q = g % 2
                if self.debug_out == "x":
                    src = self.XT[:, :, c0:c0 + n]
                    rd = [self.bX[(k, g)] for k in range(KC)]
                else:
                    self.norm_group([self.XT[:, k, c0:c0 + n] for k in range(KC)],
                                    [self.bX[(k, g)] for k in range(KC)], 80,
                                    [ost[q][:, k, :] for k in range(KC)], bo[q], n)
                    src = ost[q][:, :, :]
                    rd = bo[q]
                op = S.emit("sp", ("dma_start", dict(out=outv[:, :, g * GN:(g + 1) * GN], in_=src)),
                            reads=rd, dma=self.ch_out)
                self.out_ops.append(op)
            S.wait_all("sp", list(self.out_ops))
        self.add(None, fn)

    def program(self):
        nc = self.build()
        S = self.S
        self.units = []
        allX = [self.bX[(k, g)] for k in range(KC) for g in range(-1, NG)]
        memx, _ = self.carve(9400, [128, KC, NMEM], F32)
        b_memx = [Buf() for _ in range(KC)]

        def start(_):
            S.emit("sp", ("dma_start", dict(out=self.small[:, :], in_=self.small_in)), writes=[self.b_small],
                   dma=self.chan("l0"))
            S.emit("pool", ("dma_start", dict(out=self.cst[:, :], in_=self.consts_in)), writes=[self.b_cst],
                   dma=self.chan("l1"))
            S.emit("sp", ("dma_start", dict(out=memx, in_=self.memT_in.rearrange("(k p) t -> p k t", p=128))),
                   writes=b_memx, dma=self.chan("l2"))
            S.emit("sp", ("dma_start", dict(out=self.XT[:, :, :],
                                               in_=self.xT_in.rearrange("(k p) t -> p k t", p=128))),
                   writes=allX, dma=self.chan("l3"))
            S.emit("dve", ("memset", dict(ap=self.zeros_bf[:, :], constant=0.0)), writes=[self.b_zeros])
            self.norm_group([memx[:, k, :] for k in range(KC)], b_memx, 72,
                            [self.memnT[:, k, :] for k in range(KC)], [self.b_memn] * KC, NMEM)
            self.barrier()
        self.add(None, start)
        if self.kv_only:
            self.kv_proj()
        for l in range(self.n_layers):
            if l < 2:
                self.mixer_a(l)
                self.ffn(l, with_halo=(l == 0))
                if l == 1:
                    self.add(None, lambda _: self.barrier())
                    self.kv_proj()
            else:
                self.mixer_b(l)
                self.ffn(l, with_halo=False)
            if l != 1:
                self.add(None, lambda _: self.barrier())
        self.final()
        self.run_units(prefetch=1)
        S.finalize(self.sem)
        with nc.Block() as block:
            @block.tensor
            def _(e):
                S.replay("pe", e)

            @block.scalar
            def _(e):
                S.replay("act", e)

            @block.vector
            def _(e):
                S.replay("dve", e)

            @block.gpsimd
            def _(e):
                S.replay("pool", e)

            @block.sync
            def _(e):
                S.replay("sp", e)
        return nc


def _host_inputs(inp):
    x = np.asarray(inp["x"], dtype=np.float32)
    mem = np.asarray(inp["mem"], dtype=np.float32)
    small = np.zeros((128, 128), np.float32)

    def put(col, vec):
        small[:, col:col + 8] = np.asarray(vec, np.float32).reshape(8, 128).T
    for i in range(4):
        put(8 * i, inp["mix_norm"][i])
        put(32 + 8 * i, inp["ffn_norm"][i])
    put(64, inp["kv_norm"])
    put(72, inp["mem_norm"])
    put(80, inp["final_norm"])
    cw = np.asarray(inp["conv_w"], np.float32)
    for l in range(2):
        for j in range(6):
            for k in range(3):
                small[:, 88 + (l * 6 + j) * 3 + k] = cw[l, k, j * 128:(j + 1) * 128]
    consts = np.zeros((128, 512), np.float32)
    s = np.arange(128)
    consts[:, 0:128] = (s[:, None] < s[None, :]).astype(np.float32)
    consts[:, 128:256] = -8.0 * (s[:, None] >= s[None, :]).astype(np.float32)
    consts[:, 256:384] = 1.0
    consts[:, 384:512] = -8.0
    shared = {k: np.ascontiguousarray(np.asarray(inp[k], np.float32)) for k in
              ("a_in", "b_in", "w_kv_shared", "w_mem_kv", "w_o", "w_gate", "w_up", "w_down")}
    maps = []
    for c in range(N_CORES):
        b, half = c // 2, c % 2
        xT = np.zeros((D, TT), np.float32)
        xT[:, HALO:] = x[b, half * T:(half + 1) * T, :].T
        if half == 1:
            xT[:, :HALO] = x[b, T - HALO:T, :].T
        sm = small.copy()
        sm[:, 124] = float(half)
        m = {"xT": xT, "memT": np.ascontiguousarray(mem[b].T), "small": sm, "consts": consts}
        m.update(shared)
        maps.append(m)
    return maps


_NC_CACHE = {}


def kernel(**inputs):
    key = "full"
    if key not in _NC_CACHE:
        _NC_CACHE[key] = Builder().program()
    nc = _NC_CACHE[key]
    maps = _host_inputs(inputs)
    res = run_bass_kernel_spmd(nc, maps, core_ids=list(range(N_CORES)))
    out = np.zeros((4, SEQ, D), np.float32)
    for c in range(N_CORES):
        b, half = c // 2, c % 2
        out[b, half * T:(half + 1) * T, :] = np.asarray(res.results[c]["outT"]).T
    return out
```

```python
import numpy as np
import concourse.bass as bass
import concourse.mybir as mybir
from concourse.bass_utils import run_bass_kernel_spmd

F32 = mybir.dt.float32
BF16 = mybir.dt.bfloat16
AF = mybir.ActivationFunctionType
ALU = mybir.AluOpType

D = 1024
KC = 8
T = 2048
HALO = 32
TT = T + HALO
NG = 4
GN = 512
SEQ = 4096
NMEM = 256
DFF = 2816
FC = 22
MAINW = 768
EPS = 1e-6
N_CORES = 8
SEM_EPOCH = 24000
N_FILL = 2


class Buf:
    __slots__ = ("w", "r", "name")

    def __init__(self, name=""):
        self.w = None
        self.r = {}
        self.name = name


class Op:
    __slots__ = ("eng", "fn", "deps", "sig", "sem", "val", "dma", "idx")

    def __init__(self, eng, fn, dma):
        self.eng = eng
        self.fn = fn
        self.deps = set()
        self.sig = False
        self.sem = None
        self.val = 0
        self.dma = dma
        self.idx = 0


class Sched:
    ENGS = ("pe", "act", "dve", "pool", "sp")

    def __init__(self, nc):
        self.nc = nc
        self.ops = {e: [] for e in self.ENGS}
        self.n = 0

    def emit(self, eng, fn, reads=(), writes=(), dma=None, force_sig=False):
        op = Op(eng, fn, dma)
        op.idx = self.n
        self.n += 1
        deps = op.deps
        for b in reads:
            if b.w is not None:
                deps.add(b.w)
        for b in writes:
            if b.w is not None:
                deps.add(b.w)
            for r in b.r.values():
                if isinstance(r, list):
                    deps.update(r)
                else:
                    deps.add(r)
        deps.discard(op)
        if eng == "pe" and dma is None:
            for d in [d for d in deps if d.eng == "pe" and d.dma is None]:
                deps.discard(d)
        for d in deps:
            d.sig = True
        for b in reads:
            if dma is not None:
                b.r.setdefault(("dma", eng), [])
                b.r[("dma", eng)].append(op)
            else:
                b.r[eng] = op
        for b in writes:
            b.w = op
            b.r = {}
        if dma is not None or force_sig:
            op.sig = True
        self.ops[eng].append(op)
        return op

    def wait_all(self, eng, ops):
        op = Op(eng, None, None)
        op.deps = set(ops)
        for d in ops:
            d.sig = True
        self.ops[eng].append(op)
        return op

    def finalize(self, semalloc):
        for e in self.ENGS:
            cnt = 0
            sem = None
            for op in self.ops[e]:
                if op.fn is None or not op.sig:
                    continue
                if op.dma is not None:
                    op.dma.count += 1
                    op.sem = op.dma.sem
                    op.val = 16 * op.dma.count
                else:
                    if sem is None or cnt >= SEM_EPOCH:
                        sem = semalloc("e_" + e)
                        cnt = 0
                    cnt += 1
                    op.sem = sem
                    op.val = cnt

    def replay(self, eng, handle):
        waited = {}
        for op in self.ops[eng]:
            need = {}
            for d in op.deps:
                k = id(d.sem)
                if k not in need or need[k][1] < d.val:
                    need[k] = (d.sem, d.val)
            for k, (sem, val) in need.items():
                if waited.get(k, 0) < val:
                    handle.wait_ge(sem, val)
                    waited[k] = val
            if op.fn is None:
                continue
            name, kw = op.fn
            pos = kw.pop("_pos", ())
            ins = getattr(handle, name)(*pos, **kw)
            if op.sig:
                ins.then_inc(op.sem, 16 if op.dma is not None else 1)


class DmaChan:
    def __init__(self, sem):
        self.sem = sem
        self.count = 0


def gcols(gi):
    if gi < 0:
        return 0, HALO
    return HALO + GN * gi, GN


class Builder:
    def __init__(self, n_layers=4, debug_out=None, kv_only=False, only_inputs=None):
        self.n_layers = n_layers
        self.debug_out = debug_out
        self.kv_only = kv_only
        self.only_inputs = only_inputs

    def sem(self, name):
        self._semn += 1
        s = self.nc.alloc_semaphore(f"{name}_{self._semn}")
        return s

    def chan(self, name):
        return DmaChan(self.sem("d_" + name))

    def build(self):
        nc = bass.Bass("TRN2", target_bir_lowering=False)
        self.nc = nc
        self._semn = 0
        S = Sched(nc)
        self.S = S
        dt0 = nc.dram_tensor

        class _Dummy:
            def ap(self):
                return None

        def dt(name, shape, dtype, kind="Internal"):
            if kind == "ExternalInput" and self.only_inputs is not None and name not in self.only_inputs:
                return _Dummy()
            return dt0(name, shape, dtype, kind=kind)
        self.xT_in = dt("xT", [D, TT], F32, kind="ExternalInput").ap()
        self.memT_in = dt("memT", [D, NMEM], F32, kind="ExternalInput").ap()
        self.small_in = dt("small", [128, 128], F32, kind="ExternalInput").ap()
        self.consts_in = dt("consts", [128, 512], F32, kind="ExternalInput").ap()
        self.a_in = dt("a_in", [2, D, 2560], F32, kind="ExternalInput").ap()
        self.b_in = dt("b_in", [2, D, 1024], F32, kind="ExternalInput").ap()
        self.w_kv = dt("w_kv_shared", [D, 1536], F32, kind="ExternalInput").ap()
        self.w_mem_kv = dt("w_mem_kv", [4, D, 512], F32, kind="ExternalInput").ap()
        self.w_o = dt("w_o", [4, D, D], F32, kind="ExternalInput").ap()
        self.w_gate = dt("w_gate", [4, D, DFF], F32, kind="ExternalInput").ap()
        self.w_up = dt("w_up", [4, D, DFF], F32, kind="ExternalInput").ap()
        self.w_down = dt("w_down", [4, DFF, D], F32, kind="ExternalInput").ap()
        self.outT = dt("outT", [D, T], F32, kind="ExternalOutput").ap()
        self.k_src = [dt(f"k_src{i}", [256, T], BF16) for i in range(3)]
        self.v_src = [dt(f"v_src{i}", [T, 256], BF16) for i in range(3)]
        self.k_g = [dt(f"k_g{i}", [512, T], BF16) for i in range(3)]
        self.v_g = [dt(f"v_g{i}", [2 * T, 256], BF16) for i in range(3)]

        st = nc.alloc_sbuf_tensor
        self.XT = st("XT", [128, KC, TT], F32)
        self.HT = st("HT", [128, KC, TT], BF16)
        self.YT = st("YT", [128, KC, TT], BF16)
        self.ARENA = st("ARENA", [128, 11776], F32)
        self.WR = [st(f"WR{i}", [128, 3072], BF16) for i in range(2)]
        self.small = st("small_sb", [128, 128], F32)
        self.cst = st("cst", [128, 512], BF16)
        self.memnT = st("memnT", [128, KC, NMEM], BF16)
        self.mKT = st("mKT", [128, 2, NMEM], BF16)
        self.mV = st("mV", [128, 2, 256], BF16)
        self.sq = [st(f"sq{i}", [128, GN], BF16) for i in range(2)]
        self.rstd = [st(f"rstd{i}", [128, GN], F32) for i in range(2)]
        self.PSW = [nc.alloc_psum_tensor(f"psw{i}", [128, 2 * GN], F32) for i in range(4)]
        self.PS = [self.PSW[i // 2][:, (i % 2) * GN:(i % 2 + 1) * GN] for i in range(8)]

        self.bPS = [Buf(f"ps{i}") for i in range(8)]
        self.ps_rr = 0
        self.bWR = [[Buf(f"wr{i}_{j}") for j in range(3)] for i in range(2)]
        self.wr_rr = 0
        self.wr_chan = [[self.chan(f"wr{i}_{j}") for j in range(3)] for i in range(2)]
        self.sg = [st(f"sg{i}", [128, GN], F32) for i in range(2)]
        self.bsg = [Buf() for _ in range(2)]
        self.sg_rr = 0
        self.zeros_bf = st("zeros_bf", [128, GN], BF16)
        self.b_zeros = Buf()
        self.bG = {(c, s): Buf() for c in range(FC) for s in range(3)}
        self.ch_v = [self.chan("v0"), self.chan("v1")]
        self.ch_kt = [self.chan("kt0"), self.chan("kt1")]
        self.arena_dmas = []
        self.ps_rrs = {}
        self.bX = {(k, g): Buf() for k in range(KC) for g in range(-1, NG)}
        self.bH = {(k, g): Buf() for k in range(KC) for g in range(-1, NG)}
        self.bY = {(k, g): Buf() for k in range(KC) for g in range(-1, NG)}
        self.bsq = [Buf() for _ in range(2)]
        self.brstd = [Buf() for _ in range(2)]
        self.sq_rr = 0
        self.rstd_rr = 0
        self.b_small = Buf()
        self.b_cst = Buf()
        self.b_memn = Buf()
        self.b_mK = Buf()
        self.b_mV = Buf()
        self.ch_misc = self.chan("misc")
        self.ch_shift = [self.chan(f"sh{i}") for i in range(4)]
        self.shift_rr = 0
        self.ch_out = self.chan("out")
        self.out_ops = []
        return nc

    def ps_next(self, banks=None):
        if banks is None:
            banks = list(range(8))
        key = tuple(banks)
        r = self.ps_rrs.get(key, 0)
        self.ps_rrs[key] = r + 1
        i = banks[r % len(banks)]
        return self.PS[i], self.bPS[i]

    def barrier(self):
        S = self.S
        last = []
        for e in ("pe", "act", "dve"):
            for op in reversed(S.ops[e]):
                if op.fn is not None:
                    last.append(op)
                    break
        dm = list(self.arena_dmas)
        self.arena_dmas = []
        for e in ("pe", "act", "dve", "sp"):
            S.wait_all(e, [o for o in last if o.eng != e] + dm)

    def carve(self, off, shape, dtype):
        n = 1
        for s in shape[1:]:
            n *= s
        words = n if dtype == F32 else (n + 1) // 2
        ap = self.ARENA[:, off:off + words]
        if dtype == BF16:
            ap = ap.bitcast(BF16)
        if len(shape) == 3:
            ap = ap.rearrange("p (a b) -> p a b", a=shape[1])
        return ap, off + words

    def add(self, load, compute):
        self.units.append((load, compute))

    def run_units(self, prefetch=1):
        pend = []
        n = len(self.units)
        for i in range(n + prefetch):
            if i < n:
                ld = self.units[i][0]
                pend.append(ld() if ld is not None else None)
            if i >= prefetch:
                self.units[i - prefetch][1](pend[i - prefetch])

    def wslot(self):
        i = self.wr_rr % 2
        self.wr_rr += 1
        return i

    def wload(self, i, seg0, nseg, shape, src):
        n = shape[1] * shape[2]
        dst = self.WR[i][:, seg0 * 1024: seg0 * 1024 + n].rearrange("p (a b) -> p a b", a=shape[1])
        bufs = [self.bWR[i][s] for s in range(seg0, seg0 + nseg)]
        self.S.emit("pool", ("dma_start", dict(out=dst, in_=src)),
                    writes=bufs, dma=self.wr_chan[i][seg0])
        return dst

    def norm_group(self, xs, bxs, gcol, outs, bouts, n):
        S = self.S
        ps, bps = self.ps_next()
        ones = self.cst[:, 256:384]
        for k in range(KC):
            i = self.sq_rr % 2
            self.sq_rr += 1
            sq, bsq = self.sq[i], self.bsq[i]
            S.emit("dve", ("tensor_tensor", dict(out=sq[:, :n], in0=xs[k], in1=xs[k], op=ALU.mult)),
                   reads=[bxs[k]], writes=[bsq])
            S.emit("pe", ("matmul", dict(out=ps[:, :n], lhsT=ones, rhs=sq[:, :n],
                                                         start=(k == 0), stop=(k == KC - 1))),
                   reads=[bsq, self.b_cst], writes=[bps])
        i = self.rstd_rr % 2
        self.rstd_rr += 1
        rstd, brstd = self.rstd[i], self.brstd[i]
        S.emit("act", ("activation", dict(out=rstd[:, :n], in_=ps[:, :n], func=AF.Ln, bias=EPS, scale=1.0 / D)),
               reads=[bps], writes=[brstd])
        S.emit("act", ("activation", dict(out=rstd[:, :n], in_=rstd[:, :n], func=AF.Exp, scale=-0.5)),
               reads=[brstd], writes=[brstd])
        for k in range(KC):
            g = self.small[:, gcol + k: gcol + k + 1]
            S.emit("dve", ("scalar_tensor_tensor", dict(out=outs[k], in0=xs[k], scalar=g,
                                                                     in1=rstd[:, :n], op0=ALU.mult, op1=ALU.mult)),
                   reads=[bxs[k], brstd, self.b_small], writes=[bouts[k]])

    def norm_tokens(self, groups, gcol):
        for gi in groups:
            c0, n = gcols(gi)
            self.norm_group([self.XT[:, k, c0:c0 + n] for k in range(KC)], [self.bX[(k, gi)] for k in range(KC)],
                            gcol, [self.HT[:, k, c0:c0 + n] for k in range(KC)],
                            [self.bH[(k, gi)] for k in range(KC)], n)

    def proj(self, wview, bw, gi, banks=None):
        c0, n = gcols(gi)
        ps, bps = self.ps_next(banks)
        for k in range(KC):
            self.S.emit("pe", ("matmul", dict(out=ps[:, :n], lhsT=wview[:, k, :], rhs=self.HT[:, k, c0:c0 + n],
                                                       start=(k == 0), stop=(k == KC - 1))),
                        reads=[bw, self.bH[(k, gi)]], writes=[bps])
        return ps, bps

    def mem_kv(self, l):
        S = self.S

        def ld(part):
            def f():
                i = self.wslot()
                src = self.w_mem_kv[l, :, part * 256:(part + 1) * 256].rearrange("(k p) f -> p k f", p=128)
                return i, self.wload(i, 0, 2, [128, KC, 256], src)
            return f

        def ck(slot):
            i, w = slot
            for c in range(2):
                ps, bps = self.ps_next()
                for k in range(KC):
                    S.emit("pe", ("matmul", dict(out=ps[:, :NMEM], lhsT=w[:, k, c * 128:(c + 1) * 128],
                                                                     rhs=self.memnT[:, k, :], start=(k == 0),
                                                                     stop=(k == KC - 1))),
                           reads=[self.bWR[i][0], self.bWR[i][1], self.b_memn], writes=[bps])
                S.emit("act", ("copy", dict(out=self.mKT[:, c, :], in_=ps[:, :NMEM])),
                       reads=[bps], writes=[self.b_mK])

        def cv(slot):
            i, w = slot
            for mc in range(2):
                ps, bps = self.ps_next()
                for k in range(KC):
                    S.emit("pe", ("matmul", dict(out=ps[:, :256],
                                                                       lhsT=self.memnT[:, k, mc * 128:(mc + 1) * 128],
                                                                       rhs=w[:, k, :], start=(k == 0),
                                                                       stop=(k == KC - 1))),
                           reads=[self.bWR[i][0], self.bWR[i][1], self.b_memn], writes=[bps])
                S.emit("act", ("copy", dict(out=self.mV[:, mc, :], in_=ps[:, :256])),
                       reads=[bps], writes=[self.b_mV])

        self.add(ld(0), ck)
        self.add(ld(1), cv)

    def mem_attn(self, groups, qmT, b_qm, PT, bPT, rden, brden, otmp, botmp):
        S = self.S
        ones64 = self.cst[:, 256:320]
        rr = 0
        for gi in groups:
            c0, n = gcols(gi)
            for hm in range(4):
                c, off = hm // 2, (hm % 2) * 64
                psO, bO = self.ps_next()
                psD, bD = self.ps_next()
                for mc in range(2):
                    psS, bS = self.ps_next()
                    S.emit("pe", ("matmul", dict(out=psS[:, :n], lhsT=self.mKT[off:off + 64, c, mc * 128:(mc + 1) * 128],
                        rhs=qmT[off:off + 64, c, c0:c0 + n], start=True, stop=True)),
                        reads=[self.b_mK, b_qm[(c, gi)]], writes=[bS])
                    j = rr % 2
                    rr += 1
                    S.emit("act", ("activation", dict(out=PT[j][:, :n], in_=psS[:, :n], func=AF.Exp,
                                                                       scale=0.125)),
                           reads=[bS], writes=[bPT[j]])
                    S.emit("pe", ("matmul", dict(out=psO[0:64, :n], lhsT=self.mV[:, mc, hm * 64:(hm + 1) * 64], rhs=PT[j][:, :n],
                        start=(mc == 0), stop=(mc == 1))), reads=[self.b_mV, bPT[j]], writes=[bO])
                    S.emit("pe", ("matmul", dict(out=psD[0:64, :n], lhsT=ones64, rhs=PT[j][:, :n], start=(mc == 0), stop=(mc == 1))),
                        reads=[self.b_cst, bPT[j]], writes=[bD])
                j = rr % 2
                S.emit("dve", ("reciprocal", dict(out=rden[j][0:64, :n], in_=psD[0:64, :n])),
                       reads=[bD], writes=[brden[j]])
                if off == 0:
                    S.emit("dve", ("tensor_tensor", dict(
                        out=self.YT[0:64, 6 + c, c0:c0 + n], in0=psO[0:64, :n], in1=rden[j][0:64, :n], op=ALU.mult)),
                        reads=[bO, brden[j]], writes=[self.bY[(6 + c, gi)]])
                else:
                    S.emit("dve", ("tensor_tensor", dict(
                        out=otmp[j][0:64, :n], in0=psO[0:64, :n], in1=rden[j][0:64, :n], op=ALU.mult)),
                        reads=[bO, brden[j]], writes=[botmp[j]])
                    self.shift(otmp[j][0:64, :n], botmp[j], self.YT[64:128, 6 + c, c0:c0 + n], self.bY[(6 + c, gi)])

    def shift(self, src, bsrc, dst, bdst):
        ch = self.ch_shift[self.shift_rr % len(self.ch_shift)]
        self.shift_rr += 1
        op = self.S.emit("sp", ("dma_start", dict(out=dst, in_=src)), reads=[bsrc], writes=[bdst], dma=ch)
        self.arena_dmas.append(op)

    def w_o_units(self, l, groups):
        S = self.S
        for m in range(KC):
            def ld(m=m):
                i = self.wslot()
                src = self.w_o[l, :, m * 128:(m + 1) * 128].rearrange("(k p) f -> p k f", p=128)
                return i, self.wload(i, 0, 1, [128, KC, 128], src)

            def cp(slot, m=m):
                i, w = slot
                for gi in groups:
                    c0, n = gcols(gi)
                    ps, bps = self.ps_next()
                    for c in range(KC):
                        S.emit("pe", ("matmul", dict(out=ps[:, :n], lhsT=w[:, c, :],
                                                                    rhs=self.YT[:, c, c0:c0 + n], start=(c == 0),
                                                                    stop=(c == KC - 1))),
                               reads=[self.bWR[i][0], self.bY[(c, gi)]], writes=[bps])
                    S.emit("dve", ("tensor_tensor", dict(out=self.XT[:, m, c0:c0 + n],
                                                                   in0=ps[:, :n], in1=self.XT[:, m, c0:c0 + n],
                                                                   op=ALU.add)),
                           reads=[bps, self.bX[(m, gi)]], writes=[self.bX[(m, gi)]])
            self.add(ld, cp)

    def ffn(self, l, with_halo):
        S = self.S
        GTW = 2 * GN + HALO
        GT, _ = self.carve(0, [128, FC, GTW], BF16)
        for sb in range(2):
            groups = [2 * sb, 2 * sb + 1]
            if with_halo and sb == 0:
                groups = [-1] + groups

            def lcol(gi, sb=sb):
                return 2 * GN if gi < 0 else (gi - 2 * sb) * GN

            def slot_of(gi, sb=sb):
                return 2 if gi < 0 else gi - 2 * sb

            self.add(None, lambda _, groups=groups: self.norm_tokens(groups, 32 + 8 * l))
            for c in range(FC):
                def ld(c=c):
                    i = self.wslot()
                    sg = self.w_gate[l, :, c * 128:(c + 1) * 128].rearrange("(k p) f -> p k f", p=128)
                    su = self.w_up[l, :, c * 128:(c + 1) * 128].rearrange("(k p) f -> p k f", p=128)
                    return i, self.wload(i, 0, 1, [128, KC, 128], sg), self.wload(i, 1, 1, [128, KC, 128], su)

                def cp(slot, c=c, groups=groups, lcol=lcol, slot_of=slot_of):
                    i, wg, wu = slot
                    for gi in groups:
                        c0, n = gcols(gi)
                        psg, bg = self.proj(wg, self.bWR[i][0], gi)
                        psu, bu = self.proj(wu, self.bWR[i][1], gi)
                        j = self.sg_rr % 2
                        self.sg_rr += 1
                        sgt, bsg = self.sg[j], self.bsg[j]
                        S.emit("act", ("activation", dict(out=sgt[:, :n], in_=psg[:, :n],
                                                                               func=AF.Silu)),
                               reads=[bg], writes=[bsg])
                        lc = lcol(gi)
                        S.emit("dve", ("tensor_tensor", dict(
                            out=GT[:, c, lc:lc + n], in0=psu[:, :n], in1=sgt[:, :n], op=ALU.mult)),
                            reads=[bu, bsg], writes=[self.bG[(c, slot_of(gi))]])
                self.add(ld, cp)
            for m in range(KC):
                def ld(m=m):
                    i = self.wslot()
                    src = self.w_down[l, :, m * 128:(m + 1) * 128].rearrange("(c p) f -> p c f", p=128)
                    return i, self.wload(i, 0, 3, [128, FC, 128], src)

                def cp(slot, m=m, groups=groups, lcol=lcol, slot_of=slot_of):
                    i, w = slot
                    for gi in groups:
                        c0, n = gcols(gi)
                        lc = lcol(gi)
                        ps, bps = self.ps_next()
                        for c in range(FC):
                            S.emit("pe", ("matmul", dict(out=ps[:, :n], lhsT=w[:, c, :],
                                                                               rhs=GT[:, c, lc:lc + n],
                                                                               start=(c == 0), stop=(c == FC - 1))),
                                   reads=[self.bWR[i][0], self.bG[(c, slot_of(gi))]], writes=[bps])
                        S.emit("dve", ("tensor_tensor", dict(out=self.XT[:, m, c0:c0 + n],
                                                                       in0=ps[:, :n], in1=self.XT[:, m, c0:c0 + n],
                                                                       op=ALU.add)),
                               reads=[bps, self.bX[(m, gi)]], writes=[self.bX[(m, gi)]])
                self.add(ld, cp)

    def mixer_a(self, l):
        S = self.S
        halo_full = (l == 0)
        allg = [-1, 0, 1, 2, 3]
        outg = allg if halo_full else [0, 1, 2, 3]
        off = 0
        vbuf, off = self.carve(off, [128, TT + 4], F32)
        qmT, off = self.carve(off, [128, 2, TT], BF16)
        cgs, c1, c2 = [], [], []
        for lst in (cgs, c1, c2):
            for _ in range(2):
                a, off = self.carve(off, [128, GN], F32)
                lst.append(a)
        PT, rden, otmp = [], [], []
        for _ in range(2):
            a, off = self.carve(off, [128, GN], BF16)
            PT.append(a)
        for _ in range(2):
            a, off = self.carve(off, [128, GN], F32)
            rden.append(a)
        for _ in range(2):
            a, off = self.carve(off, [128, GN], BF16)
            otmp.append(a)
        bv = {g: Buf() for g in allg}
        bcgs = [Buf(), Buf()]
        bc1 = [Buf(), Buf()]
        bc2 = [Buf(), Buf()]
        bPT = [Buf(), Buf()]
        brden = [Buf(), Buf()]
        botmp = [Buf(), Buf()]
        b_qm = {(c, g): Buf() for c in range(2) for g in allg}
        flag = self.small[:, 124:125]
        rr = [0]

        def pre(_):
            self.norm_tokens(allg, 8 * l)
            S.emit("dve", ("memset", dict(ap=vbuf[:, 0:2], constant=0.0)), writes=[bv[-1]])
        self.add(None, pre)
        self.mem_kv(l)

        for j in range(6):
            def ld(j=j):
                i = self.wslot()
                views = []
                for s in range(3):
                    src = self.a_in[l, :, s * MAINW + j * 128: s * MAINW + (j + 1) * 128].rearrange(
                        "(k p) f -> p k f", p=128)
                    views.append(self.wload(i, s, 1, [128, KC, 128], src))
                return i, views

            def cp(slot, j=j):
                i, (wb, wc, wu) = slot
                for gi in allg:
                    c0, n = gcols(gi)
                    need_y = gi in outg
                    psc, bpc = self.proj(wc, self.bWR[i][1], gi)
                    psu, bpu = self.proj(wu, self.bWR[i][2], gi)
                    if need_y:
                        psb, bpb = self.proj(wb, self.bWR[i][0], gi)
                    q = rr[0] % 2
                    rr[0] += 1
                    S.emit("act", ("copy", dict(out=cgs[q][:, :n], in_=psc[:, :n])),
                           reads=[bpc], writes=[bcgs[q]])
                    S.emit("dve", ("tensor_tensor", dict(out=vbuf[:, 2 + c0:2 + c0 + n],
                                                                          in0=psu[:, :n], in1=cgs[q][:, :n],
                                                                          op=ALU.mult)),
                           reads=[bpu, bcgs[q]], writes=[bv[gi]])
                    if gi < 0:
                        S.emit("dve", ("tensor_scalar", dict(out=vbuf[:, 2:2 + HALO], in0=vbuf[:, 2:2 + HALO],
                                                                scalar1=flag, scalar2=None, op0=ALU.mult)),
                               reads=[self.b_small], writes=[bv[gi]])
                    if not need_y:
                        continue
                    wcol = 88 + (l * 6 + j) * 3
                    w0 = self.small[:, wcol:wcol + 1]
                    w1 = self.small[:, wcol + 1:wcol + 2]
                    w2 = self.small[:, wcol + 2:wcol + 3]
                    prev = [bv[gi - 1]] if gi >= 0 else []
                    S.emit("act", ("mul", dict(out=c1[q][:, :n], in_=vbuf[:, 2 + c0:2 + c0 + n], mul=w2)),
                           reads=[bv[gi], self.b_small], writes=[bc1[q]])
                    S.emit("dve", ("scalar_tensor_tensor", dict(
                        out=c2[q][:, :n], in0=vbuf[:, 1 + c0:1 + c0 + n], scalar=w1, in1=c1[q][:, :n],
                        op0=ALU.mult, op1=ALU.add)), reads=[bv[gi], bc1[q], self.b_small] + prev, writes=[bc2[q]])
                    S.emit("dve", ("scalar_tensor_tensor", dict(
                        out=c1[q][:, :n], in0=vbuf[:, c0:c0 + n], scalar=w0, in1=c2[q][:, :n],
                        op0=ALU.mult, op1=ALU.add)), reads=[bv[gi], bc2[q], self.b_small] + prev, writes=[bc1[q]])
                    S.emit("dve", ("tensor_tensor", dict(out=self.YT[:, j, c0:c0 + n],
                                                                          in0=psb[:, :n], in1=c1[q][:, :n],
                                                                          op=ALU.mult)),
                           reads=[bpb, bc1[q]], writes=[self.bY[(j, gi)]])
            self.add(ld, cp)

        def ldq():
            i = self.wslot()
            src = self.a_in[l, :, 3 * MAINW:3 * MAINW + 256].rearrange("(k p) f -> p k f", p=128)
            return i, self.wload(i, 0, 2, [128, KC, 256], src)

        def cpq(slot):
            i, w = slot
            for gi in outg:
                c0, n = gcols(gi)
                for c in range(2):
                    ps, bps = self.ps_next()
                    for k in range(KC):
                        S.emit("pe", ("matmul", dict(out=ps[:, :n], lhsT=w[:, k, c * 128:(c + 1) * 128],
                                                                         rhs=self.HT[:, k, c0:c0 + n], start=(k == 0),
                                                                         stop=(k == KC - 1))),
                               reads=[self.bWR[i][0], self.bWR[i][1], self.bH[(k, gi)]], writes=[bps])
                    S.emit("act", ("copy", dict(out=qmT[:, c, c0:c0 + n], in_=ps[:, :n])),
                           reads=[bps], writes=[b_qm[(c, gi)]])
            self.mem_attn(outg, qmT, b_qm, PT, bPT, rden, brden, otmp, botmp)
        self.add(ldq, cpq)
        self.w_o_units(l, outg)
        self.add(None, lambda _: self.barrier())

    def kv_proj(self):
        S = self.S
        off = 0
        kst, vst = [], []
        for _ in range(2):
            a, off = self.carve(off, [128, GN], BF16)
            kst.append(a)
        for _ in range(2):
            a, off = self.carve(off, [128, 256], BF16)
            vst.append(a)
        bk = [Buf(), Buf()]
        bvs = [Buf(), Buf()]
        ch_st = self.chan("st")
        b_gate = Buf()
        self.b_kvready = Buf()
        rr = [0]
        self.add(None, lambda _: self.norm_tokens([0, 1, 2, 3], 64))
        for c in range(6):
            def ld(c=c):
                i = self.wslot()
                src = self.w_kv[:, c * 128:(c + 1) * 128].rearrange("(k p) f -> p k f", p=128)
                return i, self.wload(i, 0, 1, [128, KC, 128], src)

            def cp(slot, c=c):
                i, w = slot
                for g in range(NG):
                    ps, bps = self.proj(w, self.bWR[i][0], g)
                    q = rr[0] % 2
                    rr[0] += 1
                    S.emit("act", ("copy", dict(out=kst[q][:, :], in_=ps[:, :])),
                           reads=[bps], writes=[bk[q]])
                    dst = self.k_src[c // 2][(c % 2) * 128:(c % 2 + 1) * 128, g * GN:(g + 1) * GN]
                    op = S.emit("sp", ("dma_start", dict(out=dst, in_=kst[q][:, :])),
                                reads=[bk[q], b_gate], dma=ch_st)
                    self.arena_dmas.append(op)
            self.add(ld, cp)
        for hv in range(3):
            def ld(hv=hv):
                i = self.wslot()
                src = self.w_kv[:, MAINW + hv * 256: MAINW + (hv + 1) * 256].rearrange("(k p) f -> p k f", p=128)
                return i, self.wload(i, 0, 2, [128, KC, 256], src)

            def cp(slot, hv=hv):
                i, w = slot
                for tt in range(16):
                    g = tt // 4
                    c0 = HALO + tt * 128
                    ps, bps = self.ps_next()
                    for k in range(KC):
                        S.emit("pe", ("matmul", dict(out=ps[:, :256], lhsT=self.HT[:, k, c0:c0 + 128],
                                                                           rhs=w[:, k, :], start=(k == 0),
                                                                           stop=(k == KC - 1))),
                               reads=[self.bWR[i][0], self.bWR[i][1], self.bH[(k, g)]], writes=[bps])
                    q = rr[0] % 2
                    rr[0] += 1
                    S.emit("act", ("copy", dict(out=vst[q][:, :], in_=ps[:, :256])),
                           reads=[bps], writes=[bvs[q]])
                    dst = self.v_src[hv][tt * 128:(tt + 1) * 128, :]
                    op = S.emit("sp", ("dma_start", dict(out=dst, in_=vst[q][:, :])),
                                reads=[bvs[q], b_gate], dma=ch_st)
                    self.arena_dmas.append(op)
            self.add(ld, cp)

        def coll(_):
            rg = [[0, 1], [2, 3], [4, 5], [6, 7]]
            srcs = self.k_src + self.v_src
            dsts = self.k_g + self.v_g
            for ci in range(6):
                S.emit("pool", ("collective_compute", dict(_pos=("AllGather", ALU.bypass), replica_groups=rg,
                                                           ins=[srcs[ci].ap().opt()], outs=[dsts[ci].ap().opt()])),
                       reads=([b_gate] if ci else []),
                       writes=([b_gate] if ci == 0 else []) + ([self.b_kvready] if ci == 5 else []), force_sig=True)
            self.barrier()
        self.add(None, coll)

    def mixer_b(self, l):
        S = self.S
        jb = l - 2
        G4 = [0, 1, 2, 3]
        off = 0
        Qc = []
        for _ in range(2):
            a, off = self.carve(off, [128, T], BF16)
            Qc.append(a)
        qmT, off = self.carve(off, [128, 2, TT], BF16)
        KT, off = self.carve(off, [128, SEQ], BF16)
        Vh, off = self.carve(off, [128, 32, 64], BF16)
        ebuf, off = self.carve(off, [128, 2 * GN], BF16)
        spb, apb, scl, obf = [], [], [], []
        for _ in range(2):
            a, off = self.carve(off, [128, 2 * GN], BF16)
            spb.append(a)
        for _ in range(2):
            a, off = self.carve(off, [128, 2 * GN], BF16)
            apb.append(a)
        for _ in range(2):
            a, off = self.carve(off, [128, GN], F32)
            scl.append(a)
        sacc, off = self.carve(off, [128, GN], BF16)
        for _ in range(2):
            a, off = self.carve(off, [128, GN], BF16)
            obf.append(a)
        assert off <= 11776, off
        b_Q = [Buf(), Buf()]
        b_qm = {(c, g): Buf() for c in range(2) for g in G4}
        b_KT = [Buf(), Buf()]
        b_V = [Buf(), Buf()]
        b_e = Buf()
        b_sp = [Buf(), Buf()]
        b_ap = [Buf(), Buf()]
        b_scl = [Buf(), Buf()]
        b_sacc = Buf()
        b_obf = [Buf(), Buf()]
        flag = self.small[:, 124:125]
        mask = self.cst[:, 0:128]
        trin8 = self.cst[:, 128:256]
        ones64 = self.cst[:, 256:320]
        neg8 = self.cst[:, 384:512]
        ones128 = self.cst[:, 256:384]
        zeros = self.zeros_bf
        BA, BC = [0, 1, 2], [4, 5]
        rr = {"q": 0, "s": 0, "o": 0}

        self.add(None, lambda _: self.norm_tokens(G4, 8 * l))
        self.mem_kv(l)

        def ldq():
            i = self.wslot()
            src = self.b_in[jb, :, MAINW:MAINW + 256].rearrange("(k p) f -> p k f", p=128)
            return i, self.wload(i, 0, 2, [128, KC, 256], src)

        def cpq(slot):
            i, w = slot
            for gi in G4:
                c0, n = gcols(gi)
                for c in range(2):
                    ps, bps = self.ps_next([7])
                    for k in range(KC):
                        S.emit("pe", ("matmul", dict(out=ps[:, :n], lhsT=w[:, k, c * 128:(c + 1) * 128],
                                                                         rhs=self.HT[:, k, c0:c0 + n], start=(k == 0),
                                                                         stop=(k == KC - 1))),
                               reads=[self.bWR[i][0], self.bWR[i][1], self.bH[(k, gi)]], writes=[bps])
                    S.emit("act", ("copy", dict(out=qmT[:, c, c0:c0 + n], in_=ps[:, :n])),
                           reads=[bps], writes=[b_qm[(c, gi)]])
            self.mem_attn(G4, qmT, b_qm, apb, b_ap, scl, b_scl, obf, b_obf)
        self.add(ldq, cpq)

        def attn_chunk(c, qi):
            units = []
            for eh in range(2):
                for G in range(NG):
                    seq = [dict(blocks=[16 + ob], c0=(ob - 4 * G) * 128, diag=True) for ob in range(4 * G + 3, 4 * G - 1, -1)]
                    full = [16 + ob for ob in range(4 * G - 1, -1, -1)] + list(range(15, -1, -1))
                    for i in range(0, len(full), 2):
                        assert (full[i] < 16) == (full[i + 1] < 16)
                        seq.append(dict(blocks=[full[i], full[i + 1]], c0=0, diag=False))
                    for ui, u in enumerate(seq):
                        u.update(eh=eh, G=G, first=(ui == 0), last=(ui == len(seq) - 1))
                        units.append(u)
            nu = len(units)
            psC, bC = self.PS[6], self.bPS[6]

            def reg(u):
                return self.PSW[u % 3], [self.bPS[2 * (u % 3)], self.bPS[2 * (u % 3) + 1]]

            def cols(un):
                return un["c0"], GN * len(un["blocks"])

            def blk(un, i):
                po = 64 * un["eh"]
                kb = un["blocks"][i]
                c0 = un["c0"]
                q0 = un["G"] * GN + c0
                ktb = KT[po:po + 64, kb * 128:(kb + 1) * 128]
                qv = Qc[qi][po:po + 64, q0:q0 + GN - c0]
                bkt = b_KT[0] if kb < 16 else b_KT[1]
                bvv = b_V[0] if kb < 16 else b_V[1]
                return kb, ktb, qv, bkt, bvv

            def stageA(u):
                un = units[u]
                R, bR = reg(u)
                for i in range(len(un["blocks"])):
                    kb, ktb, qv, bkt, bvv = blk(un, i)
                    S.emit("pe", ("matmul", dict(out=R[:, i * GN + un["c0"]:(i + 1) * GN], lhsT=ktb, rhs=qv,
                                                 start=True, stop=False)),
                           reads=[bkt, b_Q[qi]], writes=[bR[i]])

            def stage1(u):
                un = units[u]
                R, bR = reg(u)
                lo, hi = cols(un)
                s = u % 2
                rb = bR[:len(un["blocks"])]
                S.emit("act", ("activation", dict(out=ebuf[:, lo:hi], in_=R[:, lo:hi], func=AF.Exp, scale=0.125)),
                       reads=rb, writes=[b_e])
                S.emit("act", ("activation", dict(out=spb[s][:, lo:hi], in_=ebuf[:, lo:hi], func=AF.Ln, bias=1.0,
                                                  scale=1.0)), reads=[b_e], writes=[b_sp[s]])
                if un["diag"]:
                    S.emit("dve", ("tensor_tensor", dict(out=spb[s][:, lo:lo + 128], in0=spb[s][:, lo:lo + 128],
                                                         in1=mask, op=ALU.mult)),
                           reads=[self.b_cst], writes=[b_sp[s]])

            def load_v(eh):
                h = 2 * c + eh
                srcp = self.v_g[h // 4][0:T, (h % 4) * 64:(h % 4 + 1) * 64].rearrange("(b p) d -> p b d", p=128)
                srco = self.v_src[h // 4][:, (h % 4) * 64:(h % 4 + 1) * 64].rearrange("(b p) d -> p b d", p=128)
                op = S.emit("sp", ("dma_start", dict(out=Vh[:, 0:16, :], in_=srcp)), reads=[self.b_kvready],
                            writes=[b_V[0]], dma=self.ch_v[0])
                self.arena_dmas.append(op)
                op = S.emit("sp", ("dma_start", dict(out=Vh[:, 16:32, :], in_=srco)), reads=[self.b_kvready],
                            writes=[b_V[1]], dma=self.ch_v[1])
                self.arena_dmas.append(op)
                S.emit("dve", ("tensor_scalar", dict(out=Vh[:, 0:16, :], in0=Vh[:, 0:16, :], scalar1=flag,
                                                     scalar2=None, op0=ALU.mult)),
                       reads=[self.b_small], writes=[b_V[0]])

            stageA(0)
            if nu > 1:
                stageA(1)
            stage1(0)
            for u in range(nu):
                un = units[u]
                R, bR = reg(u)
                lo, hi = cols(un)
                s = u % 2
                nb = len(un["blocks"])
                c0 = un["c0"]
                if un["first"] and un["G"] == 0:
                    load_v(un["eh"])
                for i in range(nb):
                    seg = slice(i * GN + c0, (i + 1) * GN)
                    sps = spb[s][:, i * GN + c0:(i + 1) * GN]
                    nmore = (0 if un["first"] else 1) + i
                    S.emit("pe", ("matmul", dict(out=R[:, seg], lhsT=trin8, rhs=sps, start=False, stop=(nmore == 0))),
                           reads=[self.b_cst, b_sp[s]], writes=[bR[i]])
                    if not un["first"]:
                        S.emit("pe", ("matmul", dict(out=R[:, seg], lhsT=neg8, rhs=sacc[:, c0:], start=False,
                                                     stop=(i == 0))),
                               reads=[self.b_cst, b_sacc], writes=[bR[i]])
                    if i == 1:
                        S.emit("pe", ("matmul", dict(out=R[:, seg], lhsT=neg8, rhs=spb[s][:, 0:GN], start=False,
                                                     stop=True)),
                               reads=[self.b_cst, b_sp[s]], writes=[bR[i]])
                if u + 2 < nu:
                    stageA(u + 2)
                for _f in range(N_FILL):
                    S.emit("pe", ("matmul", dict(out=self.PS[7][:, :], lhsT=ones128, rhs=self.HT[:, _f, 0:GN],
                                                 start=True, stop=True)), writes=[self.bPS[7]])
                if u + 1 < nu:
                    stage1(u + 1)
                if un["first"]:
                    S.emit("dve", ("memset", dict(ap=sacc[:, :], constant=0.0)), writes=[b_sacc])
                    S.emit("pe", ("matmul", dict(out=psC[0:64, :], lhsT=ones64, rhs=zeros[:, :], start=True,
                                                 stop=False)),
                           reads=[self.b_cst, self.b_zeros], writes=[bC])
                if not un["last"]:
                    for i in range(nb):
                        S.emit("dve", ("tensor_tensor", dict(out=sacc[:, c0:], in0=sacc[:, c0:],
                                                             in1=spb[s][:, i * GN + c0:(i + 1) * GN], op=ALU.add)),
                               reads=[b_sp[s]], writes=[b_sacc])
                S.emit("act", ("activation", dict(out=apb[s][:, lo:hi], in_=R[:, lo:hi], func=AF.Exp, scale=0.125)),
                       reads=bR[:nb], writes=[b_ap[s]])
                if un["diag"]:
                    S.emit("dve", ("tensor_tensor", dict(out=apb[s][:, lo:lo + 128], in0=apb[s][:, lo:lo + 128],
                                                         in1=mask, op=ALU.mult)),
                           reads=[self.b_cst], writes=[b_ap[s]])
                for i in range(nb):
                    kb, ktb, qv, bkt, bvv = blk(un, i)
                    S.emit("pe", ("matmul", dict(out=psC[0:64, c0:], lhsT=Vh[:, kb, :],
                                                 rhs=apb[s][:, i * GN + c0:(i + 1) * GN], start=False,
                                                 stop=(un["last"] and i == nb - 1))),
                           reads=[bvv, b_ap[s]], writes=[bC])
                if un["last"]:
                    G = un["G"]
                    cg0, _ = gcols(G)
                    if un["eh"] == 0:
                        S.emit("act", ("copy", dict(out=self.YT[0:64, c, cg0:cg0 + GN], in_=psC[0:64, :])),
                               reads=[bC], writes=[self.bY[(c, G)]])
                    else:
                        o = rr["o"] % 2
                        rr["o"] += 1
                        S.emit("act", ("copy", dict(out=obf[o][0:64, :], in_=psC[0:64, :])),
                               reads=[bC], writes=[b_obf[o]])
                        self.shift(obf[o][0:64, :], b_obf[o], self.YT[64:128, c, cg0:cg0 + GN], self.bY[(c, G)])

        for c in range(6):
            def ld(c=c):
                i = self.wslot()
                src = self.b_in[jb, :, c * 128:(c + 1) * 128].rearrange("(k p) f -> p k f", p=128)
                return i, self.wload(i, 0, 1, [128, KC, 128], src)

            def cp(slot, c=c):
                i, w = slot
                qi = rr["q"] % 2
                rr["q"] += 1
                for g in G4:
                    ps, bps = self.proj(w, self.bWR[i][0], g, banks=[7])
                    S.emit("act", ("copy", dict(out=Qc[qi][:, g * GN:(g + 1) * GN], in_=ps[:, :])),
                           reads=[bps], writes=[b_Q[qi]])
                op = S.emit("sp", ("dma_start", dict(out=KT[:, 0:T], in_=self.k_g[c // 2][(c % 2) * 128:(c % 2 + 1) * 128, :])),
                            reads=[self.b_kvready], writes=[b_KT[0]], dma=self.ch_kt[0])
                self.arena_dmas.append(op)
                op = S.emit("sp", ("dma_start", dict(out=KT[:, T:SEQ], in_=self.k_src[c // 2][(c % 2) * 128:(c % 2 + 1) * 128, :])),
                            reads=[self.b_kvready], writes=[b_KT[1]], dma=self.ch_kt[1])
                self.arena_dmas.append(op)
                attn_chunk(c, qi)
            self.add(ld, cp)
        self.w_o_units(l, G4)
        self.add(None, lambda _: self.barrier())

    def final(self):
        S = self.S
        ost = []
        off = 0
        for _ in range(2):
            a, off = self.carve(off, [128, KC, GN], F32)
            ost.append(a)
        bo = [[Buf() for _ in range(KC)] for _ in range(2)]
        outv = self.outT.rearrange("(k p) t -> p k t", p=128)

        def fn(_):
            for g in range(NG):
                c0, n = gcols(g)
                q = g % 2
                if self.debug_out == "x":
                    src = self.XT[:, :, c0:c0 + n]
                    rd = [self.bX[(k, g)] for k in range(KC)]
                else:
                    self.norm_group([self.XT[:, k, c0:c0 + n] for k in range(KC)],
                                    [self.bX[(k, g)] for k in range(KC)], 80,
                                    [ost[q][:, k, :] for k in range(KC)], bo[q], n)
                    src = ost[q][:, :, :]
                    rd = bo[q]
                op = S.emit("sp", ("dma_start", dict(out=outv[:, :, g * GN:(g + 1) * GN], in_=src)),
                            reads=rd, dma=self.ch_out)
                self.out_ops.append(op)
            S.wait_all("sp", list(self.out_ops))
        self.add(None, fn)

    def program(self):
        nc = self.build()
        S = self.S
        self.units = []
        allX = [self.bX[(k, g)] for k in range(KC) for g in range(-1, NG)]
        memx, _ = self.carve(9400, [128, KC, NMEM], F32)
        b_memx = [Buf() for _ in range(KC)]

        def start(_):
            S.emit("sp", ("dma_start", dict(out=self.small[:, :], in_=self.small_in)), writes=[self.b_small],
                   dma=self.chan("l0"))
            S.emit("pool", ("dma_start", dict(out=self.cst[:, :], in_=self.consts_in)), writes=[self.b_cst],
                   dma=self.chan("l1"))
            S.emit("sp", ("dma_start", dict(out=memx, in_=self.memT_in.rearrange("(k p) t -> p k t", p=128))),
                   writes=b_memx, dma=self.chan("l2"))
            S.emit("sp", ("dma_start", dict(out=self.XT[:, :, :],
                                               in_=self.xT_in.rearrange("(k p) t -> p k t", p=128))),
                   writes=allX, dma=self.chan("l3"))
            S.emit("dve", ("memset", dict(ap=self.zeros_bf[:, :], constant=0.0)), writes=[self.b_zeros])
            self.norm_group([memx[:, k, :] for k in range(KC)], b_memx, 72,
                            [self.memnT[:, k, :] for k in range(KC)], [self.b_memn] * KC, NMEM)
            self.barrier()
        self.add(None, start)
        if self.kv_only:
            self.kv_proj()
        for l in range(self.n_layers):
            if l < 2:
                self.mixer_a(l)
                self.ffn(l, with_halo=(l == 0))
                if l == 1:
                    self.add(None, lambda _: self.barrier())
                    self.kv_proj()
            else:
                self.mixer_b(l)
                self.ffn(l, with_halo=False)
            if l != 1:
                self.add(None, lambda _: self.barrier())
        self.final()
        self.run_units(prefetch=1)
        S.finalize(self.sem)
        with nc.Block() as block:
            @block.tensor
            def _(e):
                S.replay("pe", e)

            @block.scalar
            def _(e):
                S.replay("act", e)

            @block.vector
            def _(e):
                S.replay("dve", e)

            @block.gpsimd
            def _(e):
                S.replay("pool", e)

            @block.sync
            def _(e):
                S.replay("sp", e)
        return nc


def _host_inputs(inp):
    x = np.asarray(inp["x"], dtype=np.float32)
    mem = np.asarray(inp["mem"], dtype=np.float32)
    small = np.zeros((128, 128), np.float32)

    def put(col, vec):
        small[:, col:col + 8] = np.asarray(vec, np.float32).reshape(8, 128).T
    for i in range(4):
        put(8 * i, inp["mix_norm"][i])
        put(32 + 8 * i, inp["ffn_norm"][i])
    put(64, inp["kv_norm"])
    put(72, inp["mem_norm"])
    put(80, inp["final_norm"])
    cw = np.asarray(inp["conv_w"], np.float32)
    for l in range(2):
        for j in range(6):
            for k in range(3):
                small[:, 88 + (l * 6 + j) * 3 + k] = cw[l, k, j * 128:(j + 1) * 128]
    consts = np.zeros((128, 512), np.float32)
    s = np.arange(128)
    consts[:, 0:128] = (s[:, None] < s[None, :]).astype(np.float32)
    consts[:, 128:256] = -8.0 * (s[:, None] >= s[None, :]).astype(np.float32)
    consts[:, 256:384] = 1.0
    consts[:, 384:512] = -8.0
    shared = {k: np.ascontiguousarray(np.asarray(inp[k], np.float32)) for k in
              ("a_in", "b_in", "w_kv_shared", "w_mem_kv", "w_o", "w_gate", "w_up", "w_down")}
    maps = []
    for c in range(N_CORES):
        b, half = c // 2, c % 2
        xT = np.zeros((D, TT), np.float32)
        xT[:, HALO:] = x[b, half * T:(half + 1) * T, :].T
        if half == 1:
            xT[:, :HALO] = x[b, T - HALO:T, :].T
        sm = small.copy()
        sm[:, 124] = float(half)
        m = {"xT": xT, "memT": np.ascontiguousarray(mem[b].T), "small": sm, "consts": consts}
        m.update(shared)
        maps.append(m)
    return maps


_NC_CACHE = {}


def kernel(**inputs):
    key = "full"
    if key not in _NC_CACHE:
        _NC_CACHE[key] = Builder().program()
    nc = _NC_CACHE[key]
    maps = _host_inputs(inputs)
    res = run_bass_kernel_spmd(nc, maps, core_ids=list(range(N_CORES)))
    out = np.zeros((4, SEQ, D), np.float32)
    for c in range(N_CORES):
        b, half = c // 2, c % 2
        out[b, half * T:(half + 1) * T, :] = np.asarray(res.results[c]["outT"]).T
    return out
```

```python
import numpy as np
import concourse.bass as bass
import concourse.mybir as mybir
from concourse.bass_utils import run_bass_kernel_spmd

F32 = mybir.dt.float32
BF16 = mybir.dt.bfloat16
AF = mybir.ActivationFunctionType
ALU = mybir.AluOpType

D = 1024
KC = 8
T = 2048
HALO = 32
TT = T + HALO
NG = 4
GN = 512
SEQ = 4096
NMEM = 256
DFF = 2816
FC = 22
MAINW = 768
EPS = 1e-6
N_CORES = 8
SEM_EPOCH = 24000
N_FILL = 2


class Buf:
    __slots__ = ("w", "r", "name")

    def __init__(self, name=""):
        self.w = None
        self.r = {}
        self.name = name


class Op:
    __slots__ = ("eng", "fn", "deps", "sig", "sem", "val", "dma", "idx")

    def __init__(self, eng, fn, dma):
        self.eng = eng
        self.fn = fn
        self.deps = set()
        self.sig = False
        self.sem = None
        self.val = 0
        self.dma = dma
        self.idx = 0


class Sched:
    ENGS = ("pe", "act", "dve", "pool", "sp")

    def __init__(self, nc):
        self.nc = nc
        self.ops = {e: [] for e in self.ENGS}
        self.n = 0

    def emit(self, eng, fn, reads=(), writes=(), dma=None, force_sig=False):
        op = Op(eng, fn, dma)
        op.idx = self.n
        self.n += 1
        deps = op.deps
        for b in reads:
            if b.w is not None:
                deps.add(b.w)
        for b in writes:
            if b.w is not None:
                deps.add(b.w)
            for r in b.r.values():
                if isinstance(r, list):
                    deps.update(r)
                else:
                    deps.add(r)
        deps.discard(op)
        if eng == "pe" and dma is None:
            for d in [d for d in deps if d.eng == "pe" and d.dma is None]:
                deps.discard(d)
        for d in deps:
            d.sig = True
        for b in reads:
            if dma is not None:
                b.r.setdefault(("dma", eng), [])
                b.r[("dma", eng)].append(op)
            else:
                b.r[eng] = op
        for b in writes:
            b.w = op
            b.r = {}
        if dma is not None or force_sig:
            op.sig = True
        self.ops[eng].append(op)
        return op

    def wait_all(self, eng, ops):
        op = Op(eng, None, None)
        op.deps = set(ops)
        for d in ops:
            d.sig = True
        self.ops[eng].append(op)
        return op

    def finalize(self, semalloc):
        for e in self.ENGS:
            cnt = 0
            sem = None
            for op in self.ops[e]:
                if op.fn is None or not op.sig:
                    continue
                if op.dma is not None:
                    op.dma.count += 1
                    op.sem = op.dma.sem
                    op.val = 16 * op.dma.count
                else:
                    if sem is None or cnt >= SEM_EPOCH:
                        sem = semalloc("e_" + e)
                        cnt = 0
                    cnt += 1
                    op.sem = sem
                    op.val = cnt

    def replay(self, eng, handle):
        waited = {}
        for op in self.ops[eng]:
            need = {}
            for d in op.deps:
                k = id(d.sem)
                if k not in need or need[k][1] < d.val:
                    need[k] = (d.sem, d.val)
            for k, (sem, val) in need.items():
                if waited.get(k, 0) < val:
                    handle.wait_ge(sem, val)
                    waited[k] = val
            if op.fn is None:
                continue
            name, kw = op.fn
            pos = kw.pop("_pos", ())
            ins = getattr(handle, name)(*pos, **kw)
            if op.sig:
                ins.then_inc(op.sem, 16 if op.dma is not None else 1)


class DmaChan:
    def __init__(self, sem):
        self.sem = sem
        self.count = 0


def gcols(gi):
    if gi < 0:
        return 0, HALO
    return HALO + GN * gi, GN


class Builder:
    def __init__(self, n_layers=4, debug_out=None, kv_only=False, only_inputs=None):
        self.n_layers = n_layers
        self.debug_out = debug_out
        self.kv_only = kv_only
        self.only_inputs = only_inputs

    def sem(self, name):
        self._semn += 1
        s = self.nc.alloc_semaphore(f"{name}_{self._semn}")
        return s

    def chan(self, name):
        return DmaChan(self.sem("d_" + name))

    def build(self):
        nc = bass.Bass("TRN2", target_bir_lowering=False)
        self.nc = nc
        self._semn = 0
        S = Sched(nc)
        self.S = S
        dt0 = nc.dram_tensor

        class _Dummy:
            def ap(self):
                return None

        def dt(name, shape, dtype, kind="Internal"):
            if kind == "ExternalInput" and self.only_inputs is not None and name not in self.only_inputs:
                return _Dummy()
            return dt0(name, shape, dtype, kind=kind)
        self.xT_in = dt("xT", [D, TT], F32, kind="ExternalInput").ap()
        self.memT_in = dt("memT", [D, NMEM], F32, kind="ExternalInput").ap()
        self.small_in = dt("small", [128, 128], F32, kind="ExternalInput").ap()
        self.consts_in = dt("consts", [128, 512], F32, kind="ExternalInput").ap()
        self.a_in = dt("a_in", [2, D, 2560], F32, kind="ExternalInput").ap()
        self.b_in = dt("b_in", [2, D, 1024], F32, kind="ExternalInput").ap()
        self.w_kv = dt("w_kv_shared", [D, 1536], F32, kind="ExternalInput").ap()
        self.w_mem_kv = dt("w_mem_kv", [4, D, 512], F32, kind="ExternalInput").ap()
        self.w_o = dt("w_o", [4, D, D], F32, kind="ExternalInput").ap()
        self.w_gate = dt("w_gate", [4, D, DFF], F32, kind="ExternalInput").ap()
        self.w_up = dt("w_up", [4, D, DFF], F32, kind="ExternalInput").ap()
        self.w_down = dt("w_down", [4, DFF, D], F32, kind="ExternalInput").ap()
        self.outT = dt("outT", [D, T], F32, kind="ExternalOutput").ap()
        self.k_src = [dt(f"k_src{i}", [256, T], BF16) for i in range(3)]
        self.v_src = [dt(f"v_src{i}", [T, 256], BF16) for i in range(3)]
        self.k_g = [dt(f"k_g{i}", [512, T], BF16) for i in range(3)]
        self.v_g = [dt(f"v_g{i}", [2 * T, 256], BF16) for i in range(3)]

        st = nc.alloc_sbuf_tensor
        self.XT = st("XT", [128, KC, TT], F32)
        self.HT = st("HT", [128, KC, TT], BF16)
        self.YT = st("YT", [128, KC, TT], BF16)
        self.ARENA = st("ARENA", [128, 11776], F32)
        self.WR = [st(f"WR{i}", [128, 3072], BF16) for i in range(2)]
        self.small = st("small_sb", [128, 128], F32)
        self.cst = st("cst", [128, 512], BF16)
        self.memnT = st("memnT", [128, KC, NMEM], BF16)
        self.mKT = st("mKT", [128, 2, NMEM], BF16)
        self.mV = st("mV", [128, 2, 256], BF16)
        self.sq = [st(f"sq{i}", [128, GN], BF16) for i in range(2)]
        self.rstd = [st(f"rstd{i}", [128, GN], F32) for i in range(2)]
        self.PSW = [nc.alloc_psum_tensor(f"psw{i}", [128, 2 * GN], F32) for i in range(4)]
        self.PS = [self.PSW[i // 2][:, (i % 2) * GN:(i % 2 + 1) * GN] for i in range(8)]

        self.bPS = [Buf(f"ps{i}") for i in range(8)]
        self.ps_rr = 0
        self.bWR = [[Buf(f"wr{i}_{j}") for j in range(3)] for i in range(2)]
        self.wr_rr = 0
        self.wr_chan = [[self.chan(f"wr{i}_{j}") for j in range(3)] for i in range(2)]
        self.sg = [st(f"sg{i}", [128, GN], F32) for i in range(2)]
        self.bsg = [Buf() for _ in range(2)]
        self.sg_rr = 0
        self.zeros_bf = st("zeros_bf", [128, GN], BF16)
        self.b_zeros = Buf()
        self.bG = {(c, s): Buf() for c in range(FC) for s in range(3)}
        self.ch_v = [self.chan("v0"), self.chan("v1")]
        self.ch_kt = [self.chan("kt0"), self.chan("kt1")]
        self.arena_dmas = []
        self.ps_rrs = {}
        self.bX = {(k, g): Buf() for k in range(KC) for g in range(-1, NG)}
        self.bH = {(k, g): Buf() for k in range(KC) for g in range(-1, NG)}
        self.bY = {(k, g): Buf() for k in range(KC) for g in range(-1, NG)}
        self.bsq = [Buf() for _ in range(2)]
        self.brstd = [Buf() for _ in range(2)]
        self.sq_rr = 0
        self.rstd_rr = 0
        self.b_small = Buf()
        self.b_cst = Buf()
        self.b_memn = Buf()
        self.b_mK = Buf()
        self.b_mV = Buf()
        self.ch_misc = self.chan("misc")
        self.ch_shift = [self.chan(f"sh{i}") for i in range(4)]
        self.shift_rr = 0
        self.ch_out = self.chan("out")
        self.out_ops = []
        return nc

    def ps_next(self, banks=None):
        if banks is None:
            banks = list(range(8))
        key = tuple(banks)
        r = self.ps_rrs.get(key, 0)
        self.ps_rrs[key] = r + 1
        i = banks[r % len(banks)]
        return self.PS[i], self.bPS[i]

    def barrier(self):
        S = self.S
        last = []
        for e in ("pe", "act", "dve"):
            for op in reversed(S.ops[e]):
                if op.fn is not None:
                    last.append(op)
                    break
        dm = list(self.arena_dmas)
        self.arena_dmas = []
        for e in ("pe", "act", "dve", "sp"):
            S.wait_all(e, [o for o in last if o.eng != e] + dm)

    def carve(self, off, shape, dtype):
        n = 1
        for s in shape[1:]:
            n *= s
        words = n if dtype == F32 else (n + 1) // 2
        ap = self.ARENA[:, off:off + words]
        if dtype == BF16:
            ap = ap.bitcast(BF16)
        if len(shape) == 3:
            ap = ap.rearrange("p (a b) -> p a b", a=shape[1])
        return ap, off + words

    def add(self, load, compute):
        self.units.append((load, compute))

    def run_units(self, prefetch=1):
        pend = []
        n = len(self.units)
        for i in range(n + prefetch):
            if i < n:
                ld = self.units[i][0]
                pend.append(ld() if ld is not None else None)
            if i >= prefetch:
                self.units[i - prefetch][1](pend[i - prefetch])

    def wslot(self):
        i = self.wr_rr % 2
        self.wr_rr += 1
        return i

    def wload(self, i, seg0, nseg, shape, src):
        n = shape[1] * shape[2]
        dst = self.WR[i][:, seg0 * 1024: seg0 * 1024 + n].rearrange("p (a b) -> p a b", a=shape[1])
        bufs = [self.bWR[i][s] for s in range(seg0, seg0 + nseg)]
        self.S.emit("pool", ("dma_start", dict(out=dst, in_=src)),
                    writes=bufs, dma=self.wr_chan[i][seg0])
        return dst

    def norm_group(self, xs, bxs, gcol, outs, bouts, n):
        S = self.S
        ps, bps = self.ps_next()
        ones = self.cst[:, 256:384]
        for k in range(KC):
            i = self.sq_rr % 2
            self.sq_rr += 1
            sq, bsq = self.sq[i], self.bsq[i]
            S.emit("dve", ("tensor_tensor", dict(out=sq[:, :n], in0=xs[k], in1=xs[k], op=ALU.mult)),
                   reads=[bxs[k]], writes=[bsq])
            S.emit("pe", ("matmul", dict(out=ps[:, :n], lhsT=ones, rhs=sq[:, :n],
                                                         start=(k == 0), stop=(k == KC - 1))),
                   reads=[bsq, self.b_cst], writes=[bps])
        i = self.rstd_rr % 2
        self.rstd_rr += 1
        rstd, brstd = self.rstd[i], self.brstd[i]
        S.emit("act", ("activation", dict(out=rstd[:, :n], in_=ps[:, :n], func=AF.Ln, bias=EPS, scale=1.0 / D)),
               reads=[bps], writes=[brstd])
        S.emit("act", ("activation", dict(out=rstd[:, :n], in_=rstd[:, :n], func=AF.Exp, scale=-0.5)),
               reads=[brstd], writes=[brstd])
        for k in range(KC):
            g = self.small[:, gcol + k: gcol + k + 1]
            S.emit("dve", ("scalar_tensor_tensor", dict(out=outs[k], in0=xs[k], scalar=g,
                                                                     in1=rstd[:, :n], op0=ALU.mult, op1=ALU.mult)),
                   reads=[bxs[k], brstd, self.b_small], writes=[bouts[k]])

    def norm_tokens(self, groups, gcol):
        for gi in groups:
            c0, n = gcols(gi)
            self.norm_group([self.XT[:, k, c0:c0 + n] for k in range(KC)], [self.bX[(k, gi)] for k in range(KC)],
                            gcol, [self.HT[:, k, c0:c0 + n] for k in range(KC)],
                            [self.bH[(k, gi)] for k in range(KC)], n)

    def proj(self, wview, bw, gi, banks=None):
        c0, n = gcols(gi)
        ps, bps = self.ps_next(banks)
        for k in range(KC):
            self.S.emit("pe", ("matmul", dict(out=ps[:, :n], lhsT=wview[:, k, :], rhs=self.HT[:, k, c0:c0 + n],
                                                       start=(k == 0), stop=(k == KC - 1))),
                        reads=[bw, self.bH[(k, gi)]], writes=[bps])
        return ps, bps

    def mem_kv(self, l):
        S = self.S

        def ld(part):
            def f():
                i = self.wslot()
                src = self.w_mem_kv[l, :, part * 256:(part + 1) * 256].rearrange("(k p) f -> p k f", p=128)
                return i, self.wload(i, 0, 2, [128, KC, 256], src)
            return f

        def ck(slot):
            i, w = slot
            for c in range(2):
                ps, bps = self.ps_next()
                for k in range(KC):
                    S.emit("pe", ("matmul", dict(out=ps[:, :NMEM], lhsT=w[:, k, c * 128:(c + 1) * 128],
                                                                     rhs=self.memnT[:, k, :], start=(k == 0),
                                                                     stop=(k == KC - 1))),
                           reads=[self.bWR[i][0], self.bWR[i][1], self.b_memn], writes=[bps])
                S.emit("act", ("copy", dict(out=self.mKT[:, c, :], in_=ps[:, :NMEM])),
                       reads=[bps], writes=[self.b_mK])

        def cv(slot):
            i, w = slot
            for mc in range(2):
                ps, bps = self.ps_next()
                for k in range(KC):
                    S.emit("pe", ("matmul", dict(out=ps[:, :256],
                                                                       lhsT=self.memnT[:, k, mc * 128:(mc + 1) * 128],
                                                                       rhs=w[:, k, :], start=(k == 0),
                                                                       stop=(k == KC - 1))),
                           reads=[self.bWR[i][0], self.bWR[i][1], self.b_memn], writes=[bps])
                S.emit("act", ("copy", dict(out=self.mV[:, mc, :], in_=ps[:, :256])),
                       reads=[bps], writes=[self.b_mV])

        self.add(ld(0), ck)
        self.add(ld(1), cv)

    def mem_attn(self, groups, qmT, b_qm, PT, bPT, rden, brden, otmp, botmp):
        S = self.S
        ones64 = self.cst[:, 256:320]
        rr = 0
        for gi in groups:
            c0, n = gcols(gi)
            for hm in range(4):
                c, off = hm // 2, (hm % 2) * 64
                psO, bO = self.ps_next()
                psD, bD = self.ps_next()
                for mc in range(2):
                    psS, bS = self.ps_next()
                    S.emit("pe", ("matmul", dict(out=psS[:, :n], lhsT=self.mKT[off:off + 64, c, mc * 128:(mc + 1) * 128],
                        rhs=qmT[off:off + 64, c, c0:c0 + n], start=True, stop=True)),
                        reads=[self.b_mK, b_qm[(c, gi)]], writes=[bS])
                    j = rr % 2
                    rr += 1
                    S.emit("act", ("activation", dict(out=PT[j][:, :n], in_=psS[:, :n], func=AF.Exp,
                                                                       scale=0.125)),
                           reads=[bS], writes=[bPT[j]])
                    S.emit("pe", ("matmul", dict(out=psO[0:64, :n], lhsT=self.mV[:, mc, hm * 64:(hm + 1) * 64], rhs=PT[j][:, :n],
                        start=(mc == 0), stop=(mc == 1))), reads=[self.b_mV, bPT[j]], writes=[bO])
                    S.emit("pe", ("matmul", dict(out=psD[0:64, :n], lhsT=ones64, rhs=PT[j][:, :n], start=(mc == 0), stop=(mc == 1))),
                        reads=[self.b_cst, bPT[j]], writes=[bD])
                j = rr % 2
                S.emit("dve", ("reciprocal", dict(out=rden[j][0:64, :n], in_=psD[0:64, :n])),
                       reads=[bD], writes=[brden[j]])
                if off == 0:
                    S.emit("dve", ("tensor_tensor", dict(
                        out=self.YT[0:64, 6 + c, c0:c0 + n], in0=psO[0:64, :n], in1=rden[j][0:64, :n], op=ALU.mult)),
                        reads=[bO, brden[j]], writes=[self.bY[(6 + c, gi)]])
                else:
                    S.emit("dve", ("tensor_tensor", dict(
                        out=otmp[j][0:64, :n], in0=psO[0:64, :n], in1=rden[j][0:64, :n], op=ALU.mult)),
                        reads=[bO, brden[j]], writes=[botmp[j]])
                    self.shift(otmp[j][0:64, :n], botmp[j], self.YT[64:128, 6 + c, c0:c0 + n], self.bY[(6 + c, gi)])

    def shift(self, src, bsrc, dst, bdst):
        ch = self.ch_shift[self.shift_rr % len(self.ch_shift)]
        self.shift_rr += 1
        op = self.S.emit("sp", ("dma_start", dict(out=dst, in_=src)), reads=[bsrc], writes=[bdst], dma=ch)
        self.arena_dmas.append(op)

    def w_o_units(self, l, groups):
        S = self.S
        for m in range(KC):
            def ld(m=m):
                i = self.wslot()
                src = self.w_o[l, :, m * 128:(m + 1) * 128].rearrange("(k p) f -> p k f", p=128)
                return i, self.wload(i, 0, 1, [128, KC, 128], src)

            def cp(slot, m=m):
                i, w = slot
                for gi in groups:
                    c0, n = gcols(gi)
                    ps, bps = self.ps_next()
                    for c in range(KC):
                        S.emit("pe", ("matmul", dict(out=ps[:, :n], lhsT=w[:, c, :],
                                                                    rhs=self.YT[:, c, c0:c0 + n], start=(c == 0),
                                                                    stop=(c == KC - 1))),
                               reads=[self.bWR[i][0], self.bY[(c, gi)]], writes=[bps])
                    S.emit("dve", ("tensor_tensor", dict(out=self.XT[:, m, c0:c0 + n],
                                                                   in0=ps[:, :n], in1=self.XT[:, m, c0:c0 + n],
                                                                   op=ALU.add)),
                           reads=[bps, self.bX[(m, gi)]], writes=[self.bX[(m, gi)]])
            self.add(ld, cp)

    def ffn(self, l, with_halo):
        S = self.S
        GTW = 2 * GN + HALO
        GT, _ = self.carve(0, [128, FC, GTW], BF16)
        for sb in range(2):
            groups = [2 * sb, 2 * sb + 1]
            if with_halo and sb == 0:
                groups = [-1] + groups

            def lcol(gi, sb=sb):
                return 2 * GN if gi < 0 else (gi - 2 * sb) * GN

            def slot_of(gi, sb=sb):
                return 2 if gi < 0 else gi - 2 * sb

            self.add(None, lambda _, groups=groups: self.norm_tokens(groups, 32 + 8 * l))
            for c in range(FC):
                def ld(c=c):
                    i = self.wslot()
                    sg = self.w_gate[l, :, c * 128:(c + 1) * 128].rearrange("(k p) f -> p k f", p=128)
                    su = self.w_up[l, :, c * 128:(c + 1) * 128].rearrange("(k p) f -> p k f", p=128)
                    return i, self.wload(i, 0, 1, [128, KC, 128], sg), self.wload(i, 1, 1, [128, KC, 128], su)

                def cp(slot, c=c, groups=groups, lcol=lcol, slot_of=slot_of):
                    i, wg, wu = slot
                    for gi in groups:
                        c0, n = gcols(gi)
                        psg, bg = self.proj(wg, self.bWR[i][0], gi)
                        psu, bu = self.proj(wu, self.bWR[i][1], gi)
                        j = self.sg_rr % 2
                        self.sg_rr += 1
                        sgt, bsg = self.sg[j], self.bsg[j]
                        S.emit("act", ("activation", dict(out=sgt[:, :n], in_=psg[:, :n],
                                                                               func=AF.Silu)),
                               reads=[bg], writes=[bsg])
                        lc = lcol(gi)
                        S.emit("dve", ("tensor_tensor", dict(
                            out=GT[:, c, lc:lc + n], in0=psu[:, :n], in1=sgt[:, :n], op=ALU.mult)),
                            reads=[bu, bsg], writes=[self.bG[(c, slot_of(gi))]])
                self.add(ld, cp)
            for m in range(KC):
                def ld(m=m):
                    i = self.wslot()
                    src = self.w_down[l, :, m * 128:(m + 1) * 128].rearrange("(c p) f -> p c f", p=128)
                    return i, self.wload(i, 0, 3, [128, FC, 128], src)

                def cp(slot, m=m, groups=groups, lcol=lcol, slot_of=slot_of):
                    i, w = slot
                    for gi in groups:
                        c0, n = gcols(gi)
                        lc = lcol(gi)
                        ps, bps = self.ps_next()
                        for c in range(FC):
                            S.emit("pe", ("matmul", dict(out=ps[:, :n], lhsT=w[:, c, :],
                                                                               rhs=GT[:, c, lc:lc + n],
                                                                               start=(c == 0), stop=(c == FC - 1))),
                                   reads=[self.bWR[i][0], self.bG[(c, slot_of(gi))]], writes=[bps])
                        S.emit("dve", ("tensor_tensor", dict(out=self.XT[:, m, c0:c0 + n],
                                                                       in0=ps[:, :n], in1=self.XT[:, m, c0:c0 + n],
                                                                       op=ALU.add)),
                               reads=[bps, self.bX[(m, gi)]], writes=[self.bX[(m, gi)]])
                self.add(ld, cp)

    def mixer_a(self, l):
        S = self.S
        halo_full = (l == 0)
        allg = [-1, 0, 1, 2, 3]
        outg = allg if halo_full else [0, 1, 2, 3]
        off = 0
        vbuf, off = self.carve(off, [128, TT + 4], F32)
        qmT, off = self.carve(off, [128, 2, TT], BF16)
        cgs, c1, c2 = [], [], []
        for lst in (cgs, c1, c2):
            for _ in range(2):
                a, off = self.carve(off, [128, GN], F32)
                lst.append(a)
        PT, rden, otmp = [], [], []
        for _ in range(2):
            a, off = self.carve(off, [128, GN], BF16)
            PT.append(a)
        for _ in range(2):
            a, off = self.carve(off, [128, GN], F32)
            rden.append(a)
        for _ in range(2):
            a, off = self.carve(off, [128, GN], BF16)
            otmp.append(a)
        bv = {g: Buf() for g in allg}
        bcgs = [Buf(), Buf()]
        bc1 = [Buf(), Buf()]
        bc2 = [Buf(), Buf()]
        bPT = [Buf(), Buf()]
        brden = [Buf(), Buf()]
        botmp = [Buf(), Buf()]
        b_qm = {(c, g): Buf() for c in range(2) for g in allg}
        flag = self.small[:, 124:125]
        rr = [0]

        def pre(_):
            self.norm_tokens(allg, 8 * l)
            S.emit("dve", ("memset", dict(ap=vbuf[:, 0:2], constant=0.0)), writes=[bv[-1]])
        self.add(None, pre)
        self.mem_kv(l)

        for j in range(6):
            def ld(j=j):
                i = self.wslot()
                views = []
                for s in range(3):
                    src = self.a_in[l, :, s * MAINW + j * 128: s * MAINW + (j + 1) * 128].rearrange(
                        "(k p) f -> p k f", p=128)
                    views.append(self.wload(i, s, 1, [128, KC, 128], src))
                return i, views

            def cp(slot, j=j):
                i, (wb, wc, wu) = slot
                for gi in allg:
                    c0, n = gcols(gi)
                    need_y = gi in outg
                    psc, bpc = self.proj(wc, self.bWR[i][1], gi)
                    psu, bpu = self.proj(wu, self.bWR[i][2], gi)
                    if need_y:
                        psb, bpb = self.proj(wb, self.bWR[i][0], gi)
                    q = rr[0] % 2
                    rr[0] += 1
                    S.emit("act", ("copy", dict(out=cgs[q][:, :n], in_=psc[:, :n])),
                           reads=[bpc], writes=[bcgs[q]])
                    S.emit("dve", ("tensor_tensor", dict(out=vbuf[:, 2 + c0:2 + c0 + n],
                                                                          in0=psu[:, :n], in1=cgs[q][:, :n],
                                                                          op=ALU.mult)),
                           reads=[bpu, bcgs[q]], writes=[bv[gi]])
                    if gi < 0:
                        S.emit("dve", ("tensor_scalar", dict(out=vbuf[:, 2:2 + HALO], in0=vbuf[:, 2:2 + HALO],
                                                                scalar1=flag, scalar2=None, op0=ALU.mult)),
                               reads=[self.b_small], writes=[bv[gi]])
                    if not need_y:
                        continue
                    wcol = 88 + (l * 6 + j) * 3
                    w0 = self.small[:, wcol:wcol + 1]
                    w1 = self.small[:, wcol + 1:wcol + 2]
                    w2 = self.small[:, wcol + 2:wcol + 3]
                    prev = [bv[gi - 1]] if gi >= 0 else []
                    S.emit("act", ("mul", dict(out=c1[q][:, :n], in_=vbuf[:, 2 + c0:2 + c0 + n], mul=w2)),
                           reads=[bv[gi], self.b_small], writes=[bc1[q]])
                    S.emit("dve", ("scalar_tensor_tensor", dict(
                        out=c2[q][:, :n], in0=vbuf[:, 1 + c0:1 + c0 + n], scalar=w1, in1=c1[q][:, :n],
                        op0=ALU.mult, op1=ALU.add)), reads=[bv[gi], bc1[q], self.b_small] + prev, writes=[bc2[q]])
                    S.emit("dve", ("scalar_tensor_tensor", dict(
                        out=c1[q][:, :n], in0=vbuf[:, c0:c0 + n], scalar=w0, in1=c2[q][:, :n],
                        op0=ALU.mult, op1=ALU.add)), reads=[bv[gi], bc2[q], self.b_small] + prev, writes=[bc1[q]])
                    S.emit("dve", ("tensor_tensor", dict(out=self.YT[:, j, c0:c0 + n],
                                                                          in0=psb[:, :n], in1=c1[q][:, :n],
                                                                          op=ALU.mult)),
                           reads=[bpb, bc1[q]], writes=[self.bY[(j, gi)]])
            self.add(ld, cp)

        def ldq():
            i = self.wslot()
            src = self.a_in[l, :, 3 * MAINW:3 * MAINW + 256].rearrange("(k p) f -> p k f", p=128)
            return i, self.wload(i, 0, 2, [128, KC, 256], src)

        def cpq(slot):
            i, w = slot
            for gi in outg:
                c0, n = gcols(gi)
                for c in range(2):
                    ps, bps = self.ps_next()
                    for k in range(KC):
                        S.emit("pe", ("matmul", dict(out=ps[:, :n], lhsT=w[:, k, c * 128:(c + 1) * 128],
                                                                         rhs=self.HT[:, k, c0:c0 + n], start=(k == 0),
                                                                         stop=(k == KC - 1))),
                               reads=[self.bWR[i][0], self.bWR[i][1], self.bH[(k, gi)]], writes=[bps])
                    S.emit("act", ("copy", dict(out=qmT[:, c, c0:c0 + n], in_=ps[:, :n])),
                           reads=[bps], writes=[b_qm[(c, gi)]])
            self.mem_attn(outg, qmT, b_qm, PT, bPT, rden, brden, otmp, botmp)
        self.add(ldq, cpq)
        self.w_o_units(l, outg)
        self.add(None, lambda _: self.barrier())

    def kv_proj(self):
        S = self.S
        off = 0
        kst, vst = [], []
        for _ in range(2):
            a, off = self.carve(off, [128, GN], BF16)
            kst.append(a)
        for _ in range(2):
            a, off = self.carve(off, [128, 256], BF16)
            vst.append(a)
        bk = [Buf(), Buf()]
        bvs = [Buf(), Buf()]
        ch_st = self.chan("st")
        b_gate = Buf()
        self.b_kvready = Buf()
        rr = [0]
        self.add(None, lambda _: self.norm_tokens([0, 1, 2, 3], 64))
        for c in range(6):
            def ld(c=c):
                i = self.wslot()
                src = self.w_kv[:, c * 128:(c + 1) * 128].rearrange("(k p) f -> p k f", p=128)
                return i, self.wload(i, 0, 1, [128, KC, 128], src)

            def cp(slot, c=c):
                i, w = slot
                for g in range(NG):
                    ps, bps = self.proj(w, self.bWR[i][0], g)
                    q = rr[0] % 2
                    rr[0] += 1
                    S.emit("act", ("copy", dict(out=kst[q][:, :], in_=ps[:, :])),
                           reads=[bps], writes=[bk[q]])
                    dst = self.k_src[c // 2][(c % 2) * 128:(c % 2 + 1) * 128, g * GN:(g + 1) * GN]
                    op = S.emit("sp", ("dma_start", dict(out=dst, in_=kst[q][:, :])),
                                reads=[bk[q], b_gate], dma=ch_st)
                    self.arena_dmas.append(op)
            self.add(ld, cp)
        for hv in range(3):
            def ld(hv=hv):
                i = self.wslot()
                src = self.w_kv[:, MAINW + hv * 256: MAINW + (hv + 1) * 256].rearrange("(k p) f -> p k f", p=128)
                return i, self.wload(i, 0, 2, [128, KC, 256], src)

            def cp(slot, hv=hv):
                i, w = slot
                for tt in range(16):
                    g = tt // 4
                    c0 = HALO + tt * 128
                    ps, bps = self.ps_next()
                    for k in range(KC):
                        S.emit("pe", ("matmul", dict(out=ps[:, :256], lhsT=self.HT[:, k, c0:c0 + 128],
                                                                           rhs=w[:, k, :], start=(k == 0),
                                                                           stop=(k == KC - 1))),
                               reads=[self.bWR[i][0], self.bWR[i][1], self.bH[(k, g)]], writes=[bps])
                    q = rr[0] % 2
                    rr[0] += 1
                    S.emit("act", ("copy", dict(out=vst[q][:, :], in_=ps[:, :256])),
                           reads=[bps], writes=[bvs[q]])
                    dst = self.v_src[hv][tt * 128:(tt + 1) * 128, :]
                    op = S.emit("sp", ("dma_start", dict(out=dst, in_=vst[q][:, :])),
                                reads=[bvs[q], b_gate], dma=ch_st)
                    self.arena_dmas.append(op)
            self.add(ld, cp)

        def coll(_):
            rg = [[0, 1], [2, 3], [4, 5], [6, 7]]
            srcs = self.k_src + self.v_src
            dsts = self.k_g + self.v_g
            for ci in range(6):
                S.emit("pool", ("collective_compute", dict(_pos=("AllGather", ALU.bypass), replica_groups=rg,
                                                           ins=[srcs[ci].ap().opt()], outs=[dsts[ci].ap().opt()])),
                       reads=([b_gate] if ci else []),
                       writes=([b_gate] if ci == 0 else []) + ([self.b_kvready] if ci == 5 else []), force_sig=True)
            self.barrier()
        self.add(None, coll)

    def mixer_b(self, l):
        S = self.S
        jb = l - 2
        G4 = [0, 1, 2, 3]
        off = 0
        Qc = []
        for _ in range(2):
            a, off = self.carve(off, [128, T], BF16)
            Qc.append(a)
        qmT, off = self.carve(off, [128, 2, TT], BF16)
        KT, off = self.carve(off, [128, SEQ], BF16)
        Vh, off = self.carve(off, [128, 32, 64], BF16)
        ebuf, off = self.carve(off, [128, 2 * GN], BF16)
        spb, apb, scl, obf = [], [], [], []
        for _ in range(2):
            a, off = self.carve(off, [128, 2 * GN], BF16)
            spb.append(a)
        for _ in range(2):
            a, off = self.carve(off, [128, 2 * GN], BF16)
            apb.append(a)
        for _ in range(2):
            a, off = self.carve(off, [128, GN], F32)
            scl.append(a)
        sacc, off = self.carve(off, [128, GN], BF16)
        for _ in range(2):
            a, off = self.carve(off, [128, GN], BF16)
            obf.append(a)
        assert off <= 11776, off
        b_Q = [Buf(), Buf()]
        b_qm = {(c, g): Buf() for c in range(2) for g in G4}
        b_KT = [Buf(), Buf()]
        b_V = [Buf(), Buf()]
        b_e = Buf()
        b_sp = [Buf(), Buf()]
        b_ap = [Buf(), Buf()]
        b_scl = [Buf(), Buf()]
        b_sacc = Buf()
        b_obf = [Buf(), Buf()]
        flag = self.small[:, 124:125]
        mask = self.cst[:, 0:128]
        trin8 = self.cst[:, 128:256]
        ones64 = self.cst[:, 256:320]
        neg8 = self.cst[:, 384:512]
        ones128 = self.cst[:, 256:384]
        zeros = self.zeros_bf
        BA, BC = [0, 1, 2], [4, 5]
        rr = {"q": 0, "s": 0, "o": 0}

        self.add(None, lambda _: self.norm_tokens(G4, 8 * l))
        self.mem_kv(l)

        def ldq():
            i = self.wslot()
            src = self.b_in[jb, :, MAINW:MAINW + 256].rearrange("(k p) f -> p k f", p=128)
            return i, self.wload(i, 0, 2, [128, KC, 256], src)

        def cpq(slot):
            i, w = slot
            for gi in G4:
                c0, n = gcols(gi)
                for c in range(2):
                    ps, bps = self.ps_next([7])
                    for k in range(KC):
                        S.emit("pe", ("matmul", dict(out=ps[:, :n], lhsT=w[:, k, c * 128:(c + 1) * 128],
                                                                         rhs=self.HT[:, k, c0:c0 + n], start=(k == 0),
                                                                         stop=(k == KC - 1))),
                               reads=[self.bWR[i][0], self.bWR[i][1], self.bH[(k, gi)]], writes=[bps])
                    S.emit("act", ("copy", dict(out=qmT[:, c, c0:c0 + n], in_=ps[:, :n])),
                           reads=[bps], writes=[b_qm[(c, gi)]])
            self.mem_attn(G4, qmT, b_qm, apb, b_ap, scl, b_scl, obf, b_obf)
        self.add(ldq, cpq)

        def attn_chunk(c, qi):
            units = []
            for eh in range(2):
                for G in range(NG):
                    seq = [dict(blocks=[16 + ob], c0=(ob - 4 * G) * 128, diag=True) for ob in range(4 * G + 3, 4 * G - 1, -1)]
                    full = [16 + ob for ob in range(4 * G - 1, -1, -1)] + list(range(15, -1, -1))
                    for i in range(0, len(full), 2):
                        assert (full[i] < 16) == (full[i + 1] < 16)
                        seq.append(dict(blocks=[full[i], full[i + 1]], c0=0, diag=False))
                    for ui, u in enumerate(seq):
                        u.update(eh=eh, G=G, first=(ui == 0), last=(ui == len(seq) - 1))
                        units.append(u)
            nu = len(units)
            psC, bC = self.PS[6], self.bPS[6]

            def reg(u):
                return self.PSW[u % 3], [self.bPS[2 * (u % 3)], self.bPS[2 * (u % 3) + 1]]

            def cols(un):
                return un["c0"], GN * len(un["blocks"])

            def blk(un, i):
                po = 64 * un["eh"]
                kb = un["blocks"][i]
                c0 = un["c0"]
                q0 = un["G"] * GN + c0
                ktb = KT[po:po + 64, kb * 128:(kb + 1) * 128]
                qv = Qc[qi][po:po + 64, q0:q0 + GN - c0]
                bkt = b_KT[0] if kb < 16 else b_KT[1]
                bvv = b_V[0] if kb < 16 else b_V[1]
                return kb, ktb, qv, bkt, bvv

            def stageA(u):
                un = units[u]
                R, bR = reg(u)
                for i in range(len(un["blocks"])):
                    kb, ktb, qv, bkt, bvv = blk(un, i)
                    S.emit("pe", ("matmul", dict(out=R[:, i * GN + un["c0"]:(i + 1) * GN], lhsT=ktb, rhs=qv,
                                                 start=True, stop=False)),
                           reads=[bkt, b_Q[qi]], writes=[bR[i]])

            def stage1(u):
                un = units[u]
                R, bR = reg(u)
                lo, hi = cols(un)
                s = u % 2
                rb = bR[:len(un["blocks"])]
                S.emit("act", ("activation", dict(out=ebuf[:, lo:hi], in_=R[:, lo:hi], func=AF.Exp, scale=0.125)),
                       reads=rb, writes=[b_e])
                S.emit("act", ("activation", dict(out=spb[s][:, lo:hi], in_=ebuf[:, lo:hi], func=AF.Ln, bias=1.0,
                                                  scale=1.0)), reads=[b_e], writes=[b_sp[s]])
                if un["diag"]:
                    S.emit("dve", ("tensor_tensor", dict(out=spb[s][:, lo:lo + 128], in0=spb[s][:, lo:lo + 128],
                                                         in1=mask, op=ALU.mult)),
                           reads=[self.b_cst], writes=[b_sp[s]])

            def load_v(eh):
                h = 2 * c + eh
                srcp = self.v_g[h // 4][0:T, (h % 4) * 64:(h % 4 + 1) * 64].rearrange("(b p) d -> p b d", p=128)
                srco = self.v_src[h // 4][:, (h % 4) * 64:(h % 4 + 1) * 64].rearrange("(b p) d -> p b d", p=128)
                op = S.emit("sp", ("dma_start", dict(out=Vh[:, 0:16, :], in_=srcp)), reads=[self.b_kvready],
                            writes=[b_V[0]], dma=self.ch_v[0])
                self.arena_dmas.append(op)
                op = S.emit("sp", ("dma_start", dict(out=Vh[:, 16:32, :], in_=srco)), reads=[self.b_kvready],
                            writes=[b_V[1]], dma=self.ch_v[1])
                self.arena_dmas.append(op)
                S.emit("dve", ("tensor_scalar", dict(out=Vh[:, 0:16, :], in0=Vh[:, 0:16, :], scalar1=flag,
                                                     scalar2=None, op0=ALU.mult)),
                       reads=[self.b_small], writes=[b_V[0]])

            stageA(0)
            if nu > 1:
                stageA(1)
            stage1(0)
            for u in range(nu):
                un = units[u]
                R, bR = reg(u)
                lo, hi = cols(un)
                s = u % 2
                nb = len(un["blocks"])
                c0 = un["c0"]
                if un["first"] and un["G"] == 0:
                    load_v(un["eh"])
                for i in range(nb):
                    seg = slice(i * GN + c0, (i + 1) * GN)
                    sps = spb[s][:, i * GN + c0:(i + 1) * GN]
                    nmore = (0 if un["first"] else 1) + i
                    S.emit("pe", ("matmul", dict(out=R[:, seg], lhsT=trin8, rhs=sps, start=False, stop=(nmore == 0))),
                           reads=[self.b_cst, b_sp[s]], writes=[bR[i]])
                    if not un["first"]:
                        S.emit("pe", ("matmul", dict(out=R[:, seg], lhsT=neg8, rhs=sacc[:, c0:], start=False,
                                                     stop=(i == 0))),
                               reads=[self.b_cst, b_sacc], writes=[bR[i]])
                    if i == 1:
                        S.emit("pe", ("matmul", dict(out=R[:, seg], lhsT=neg8, rhs=spb[s][:, 0:GN], start=False,
                                                     stop=True)),
                               reads=[self.b_cst, b_sp[s]], writes=[bR[i]])
                if u + 2 < nu:
                    stageA(u + 2)
                for _f in range(N_FILL):
                    S.emit("pe", ("matmul", dict(out=self.PS[7][:, :], lhsT=ones128, rhs=self.HT[:, _f, 0:GN],
                                                 start=True, stop=True)), writes=[self.bPS[7]])
                if u + 1 < nu:
                    stage1(u + 1)
                if un["first"]:
                    S.emit("dve", ("memset", dict(ap=sacc[:, :], constant=0.0)), writes=[b_sacc])
                    S.emit("pe", ("matmul", dict(out=psC[0:64, :], lhsT=ones64, rhs=zeros[:, :], start=True,
                                                 stop=False)),
                           reads=[self.b_cst, self.b_zeros], writes=[bC])
                if not un["last"]:
                    for i in range(nb):
                        S.emit("dve", ("tensor_tensor", dict(out=sacc[:, c0:], in0=sacc[:, c0:],
                                                             in1=spb[s][:, i * GN + c0:(i + 1) * GN], op=ALU.add)),
                               reads=[b_sp[s]], writes=[b_sacc])
                S.emit("act", ("activation", dict(out=apb[s][:, lo:hi], in_=R[:, lo:hi], func=AF.Exp, scale=0.125)),
                       reads=bR[:nb], writes=[b_ap[s]])
                if un["diag"]:
                    S.emit("dve", ("tensor_tensor", dict(out=apb[s][:, lo:lo + 128], in0=apb[s][:, lo:lo + 128],
                                                         in1=mask, op=ALU.mult)),
                           reads=[self.b_cst], writes=[b_ap[s]])
                for i in range(nb):
                    kb, ktb, qv, bkt, bvv = blk(un, i)
                    S.emit("pe", ("matmul", dict(out=psC[0:64, c0:], lhsT=Vh[:, kb, :],
                                                 rhs=apb[s][:, i * GN + c0:(i + 1) * GN], start=False,
                                                 stop=(un["last"] and i == nb - 1))),
                           reads=[bvv, b_ap[s]], writes=[bC])
                if un["last"]:
                    G = un["G"]
                    cg0, _ = gcols(G)
                    if un["eh"] == 0:
                        S.emit("dve", ("tensor_copy", dict(out=self.YT[0:64, c, cg0:cg0 + GN], in_=psC[0:64, :])),
                               reads=[bC], writes=[self.bY[(c, G)]])
                    else:
                        o = rr["o"] % 2
                        rr["o"] += 1
                        S.emit("dve", ("tensor_copy", dict(out=obf[o][0:64, :], in_=psC[0:64, :])),
                               reads=[bC], writes=[b_obf[o]])
                        self.shift(obf[o][0:64, :], b_obf[o], self.YT[64:128, c, cg0:cg0 + GN], self.bY[(c, G)])

        for c in range(6):
            def ld(c=c):
                i = self.wslot()
                src = self.b_in[jb, :, c * 128:(c + 1) * 128].rearrange("(k p) f -> p k f", p=128)
                return i, self.wload(i, 0, 1, [128, KC, 128], src)

            def cp(slot, c=c):
                i, w = slot
                qi = rr["q"] % 2
                rr["q"] += 1
                for g in G4:
                    ps, bps = self.proj(w, self.bWR[i][0], g, banks=[7])
                    S.emit("act", ("copy", dict(out=Qc[qi][:, g * GN:(g + 1) * GN], in_=ps[:, :])),
                           reads=[bps], writes=[b_Q[qi]])
                op = S.emit("sp", ("dma_start", dict(out=KT[:, 0:T], in_=self.k_g[c // 2][(c % 2) * 128:(c % 2 + 1) * 128, :])),
                            reads=[self.b_kvready], writes=[b_KT[0]], dma=self.ch_kt[0])
                self.arena_dmas.append(op)
                op = S.emit("sp", ("dma_start", dict(out=KT[:, T:SEQ], in_=self.k_src[c // 2][(c % 2) * 128:(c % 2 + 1) * 128, :])),
                            reads=[self.b_kvready], writes=[b_KT[1]], dma=self.ch_kt[1])
                self.arena_dmas.append(op)
                attn_chunk(c, qi)
            self.add(ld, cp)
        self.w_o_units(l, G4)
        self.add(None, lambda _: self.barrier())

    def final(self):
        S = self.S
        ost = []
        off = 0
        for _ in range(2):
            a, off = self.carve(off, [128, KC, GN], F32)
            ost.append(a)
        bo = [[Buf() for _ in range(KC)] for _ in range(2)]
        outv = self.outT.rearrange("(k p) t -> p k t", p=128)

        def fn(_):
            for g in range(NG):
                c0, n = gcols(g)
                q = g % 2
                if self.debug_out == "x":
                    src = self.XT[:, :, c0:c0 + n]
                    rd = [self.bX[(k, g)] for k in range(KC)]
                else:
                    self.norm_group([self.XT[:, k, c0:c0 + n] for k in range(KC)],
                                    [self.bX[(k, g)] for k in range(KC)], 80,
                                    [ost[q][:, k, :] for k in range(KC)], bo[q], n)
                    src = ost[q][:, :, :]
                    rd = bo[q]
                op = S.emit("sp", ("dma_start", dict(out=outv[:, :, g * GN:(g + 1) * GN], in_=src)),
                            reads=rd, dma=self.ch_out)
                self.out_ops.append(op)
            S.wait_all("sp", list(self.out_ops))
        self.add(None, fn)

    def program(self):
        nc = self.build()
        S = self.S
        self.units = []
        allX = [self.bX[(k, g)] for k in range(KC) for g in range(-1, NG)]
        memx, _ = self.carve(9400, [128, KC, NMEM], F32)
        b_memx = [Buf() for _ in range(KC)]

        def start(_):
            S.emit("sp", ("dma_start", dict(out=self.small[:, :], in_=self.small_in)), writes=[self.b_small],
                   dma=self.chan("l0"))
            S.emit("pool", ("dma_start", dict(out=self.cst[:, :], in_=self.consts_in)), writes=[self.b_cst],
                   dma=self.chan("l1"))
            S.emit("sp", ("dma_start", dict(out=memx, in_=self.memT_in.rearrange("(k p) t -> p k t", p=128))),
                   writes=b_memx, dma=self.chan("l2"))
            S.emit("sp", ("dma_start", dict(out=self.XT[:, :, :],
                                               in_=self.xT_in.rearrange("(k p) t -> p k t", p=128))),
                   writes=allX, dma=self.chan("l3"))
            S.emit("dve", ("memset", dict(ap=self.zeros_bf[:, :], constant=0.0)), writes=[self.b_zeros])
            self.norm_group([memx[:, k, :] for k in range(KC)], b_memx, 72,
                            [self.memnT[:, k, :] for k in range(KC)], [self.b_memn] * KC, NMEM)
            self.barrier()
        self.add(None, start)
        if self.kv_only:
            self.kv_proj()
        for l in range(self.n_layers):
            if l < 2:
                self.mixer_a(l)
                self.ffn(l, with_halo=(l == 0))
                if l == 1:
                    self.add(None, lambda _: self.barrier())
                    self.kv_proj()
            else:
                self.mixer_b(l)
                self.ffn(l, with_halo=False)
            if l != 1:
                self.add(None, lambda _: self.barrier())
        self.final()
        self.run_units(prefetch=1)
        S.finalize(self.sem)
        with nc.Block() as block:
            @block.tensor
            def _(e):
                S.replay("pe", e)

            @block.scalar
            def _(e):
                S.replay("act", e)

            @block.vector
            def _(e):
                S.replay("dve", e)

            @block.gpsimd
            def _(e):
                S.replay("pool", e)

            @block.sync
            def _(e):
                S.replay("sp", e)
        return nc


def _host_inputs(inp):
    x = np.asarray(inp["x"], dtype=np.float32)
    mem = np.asarray(inp["mem"], dtype=np.float32)
    small = np.zeros((128, 128), np.float32)

    def put(col, vec):
        small[:, col:col + 8] = np.asarray(vec, np.float32).reshape(8, 128).T
    for i in range(4):
        put(8 * i, inp["mix_norm"][i])
        put(32 + 8 * i, inp["ffn_norm"][i])
    put(64, inp["kv_norm"])
    put(72, inp["mem_norm"])
    put(80, inp["final_norm"])
    cw = np.asarray(inp["conv_w"], np.float32)
    for l in range(2):
        for j in range(6):
            for k in range(3):
                small[:, 88 + (l * 6 + j) * 3 + k] = cw[l, k, j * 128:(j + 1) * 128]
    consts = np.zeros((128, 512), np.float32)
    s = np.arange(128)
    consts[:, 0:128] = (s[:, None] < s[None, :]).astype(np.float32)
    consts[:, 128:256] = -8.0 * (s[:, None] >= s[None, :]).astype(np.float32)
    consts[:, 256:384] = 1.0
    consts[:, 384:512] = -8.0
    shared = {k: np.ascontiguousarray(np.asarray(inp[k], np.float32)) for k in
              ("a_in", "b_in", "w_kv_shared", "w_mem_kv", "w_o", "w_gate", "w_up", "w_down")}
    maps = []
    for c in range(N_CORES):
        b, half = c // 2, c % 2
        xT = np.zeros((D, TT), np.float32)
        xT[:, HALO:] = x[b, half * T:(half + 1) * T, :].T
        if half == 1:
            xT[:, :HALO] = x[b, T - HALO:T, :].T
        sm = small.copy()
        sm[:, 124] = float(half)
        m = {"xT": xT, "memT": np.ascontiguousarray(mem[b].T), "small": sm, "consts": consts}
        m.update(shared)
        maps.append(m)
    return maps


_NC_CACHE = {}


def kernel(**inputs):
    key = "full"
    if key not in _NC_CACHE:
        _NC_CACHE[key] = Builder().program()
    nc = _NC_CACHE[key]
    maps = _host_inputs(inputs)
    res = run_bass_kernel_spmd(nc, maps, core_ids=list(range(N_CORES)))
    out = np.zeros((4, SEQ, D), np.float32)
    for c in range(N_CORES):
        b, half = c // 2, c % 2
        out[b, half * T:(half + 1) * T, :] = np.asarray(res.results[c]["outT"]).T
    return out
```
